# Optimizing a Trainium2 kernel written in Bass

```python
import jax, jax.numpy as jnp
from jax import lax
import numpy as np

D_MODEL = 1024
BATCH = 4
SEQ = 8192
DEPTH = 4
DEC_BATCH = 32
DEC_SEQ = 32
PAST_LEN = 2048

CHUNK = 64
A_HEAD_DIM = 64
A_WIDTH = D_MODEL // 2
A_HEADS = A_WIDTH // A_HEAD_DIM
A_BACK = 8
A_PAST = A_BACK * CHUNK
A_BAND = (A_BACK + 1) * CHUNK
REL_MAX = 128
REL_SIZE = (CHUNK - 1) + REL_MAX + 1
POOL_WINDOWS = (2, 4, 8, 16)
B_GROUPS = len(POOL_WINDOWS)
B_WIDTH = D_MODEL // 4
B_GROUP_DIM = B_WIDTH // B_GROUPS
POOL_HIST = max(POOL_WINDOWS) - 1
C_HEADS = 4
C_VAL_WIDTH = D_MODEL // 4
C_KEY_WIDTH = D_MODEL // 4
C_KEY_DIM = C_KEY_WIDTH // C_HEADS
C_VAL_DIM = C_VAL_WIDTH // C_HEADS
MIX_WIDTH = A_WIDTH + B_WIDTH + C_VAL_WIDTH
IN_SPLITS = (A_WIDTH, A_WIDTH, A_WIDTH, B_WIDTH, C_KEY_WIDTH, C_KEY_WIDTH, C_VAL_WIDTH, C_VAL_WIDTH)
IN_WIDTH = sum(IN_SPLITS)
FFN_HIDDEN = 2816
PLE_DIM = 256
DEEPNORM_ALPHA = (2 * DEPTH) ** 0.25
DEEPNORM_BETA = (8 * DEPTH) ** -0.25
LN_EPS = 1e-5
RMS_EPS = 1e-6

kernel_name = 'hybrid_streaming_encoder_step'


def layer_norm(x, g, b):
    xf = x.astype(jnp.float32)
    mu = jnp.mean(xf, axis=-1, keepdims=True)
    var = jnp.mean(jnp.square(xf - mu), axis=-1, keepdims=True)
    y = (xf - mu) * lax.rsqrt(var + LN_EPS) * g.astype(jnp.float32) + b.astype(jnp.float32)
    return y.astype(x.dtype)


def swiglu(x, w_gu, w_down):
    gate, up = jnp.split(x @ w_gu, 2, axis=-1)
    return (jax.nn.silu(gate) * up) @ w_down


def rel_position_bias(table, n_q, n_k, offset):
    d = offset + jnp.arange(n_q)[:, None] - jnp.arange(n_k)[None, :]
    idx = jnp.clip(d, -(CHUNK - 1), REL_MAX) + (CHUNK - 1)
    return table[:, idx].astype(jnp.float32)


def band_attention_prompt(q, k, v, rel_bias):
    n, t, h, dh = q.shape
    nc = t // CHUNK
    qc = q.reshape(n, nc, CHUNK, h, dh)
    pad = ((0, 0), (A_PAST, 0), (0, 0), (0, 0))
    kp = jnp.pad(k, pad).reshape(n, nc + A_BACK, CHUNK, h, dh)
    vp = jnp.pad(v, pad).reshape(n, nc + A_BACK, CHUNK, h, dh)
    kb = jnp.concatenate([kp[:, j:j + nc] for j in range(A_BACK + 1)], axis=2)
    vb = jnp.concatenate([vp[:, j:j + nc] for j in range(A_BACK + 1)], axis=2)
    key_chunk = jnp.arange(nc)[:, None] - A_BACK + (jnp.arange(A_BAND) // CHUNK)[None, :]
    valid = key_chunk >= 0
    bias = rel_position_bias(rel_bias, CHUNK, A_BAND, A_PAST)
    s = jnp.einsum('bcqhd,bckhd->bchqk', qc, kb).astype(jnp.float32) * (A_HEAD_DIM ** -0.5)
    s = s + bias[None, None]
    s = jnp.where(valid[None, :, None, None, :], s, -1e30)
    p = jax.nn.softmax(s, axis=-1)
    o = jnp.einsum('bchqk,bckhd->bcqhd', p.astype(v.dtype), vb)
    return o.reshape(n, t, h * dh)


def band_attention_sample(q, k, v, cache_k, cache_v, rel_bias):
    n, t, h, dh = q.shape
    lc = cache_k.shape[1]
    kk = jnp.concatenate([cache_k.astype(k.dtype), k], axis=1)
    vv = jnp.concatenate([cache_v.astype(v.dtype), v], axis=1)
    bias = rel_position_bias(rel_bias, t, lc + t, lc)
    s = jnp.einsum('bqhd,bkhd->bhqk', q, kk).astype(jnp.float32) * (A_HEAD_DIM ** -0.5) + bias[None]
    p = jax.nn.softmax(s, axis=-1)
    o = jnp.einsum('bhqk,bkhd->bqhd', p.astype(vv.dtype), vv)
    return o.reshape(n, t, h * dh)


def pool_mixer(u, hist, pos0, w_grp, scale):
    n, t, c = u.shape
    full = jnp.concatenate([hist.astype(u.dtype), u], axis=1).astype(jnp.float32)
    cs = jnp.concatenate([jnp.zeros((n, 1, c), jnp.float32), jnp.cumsum(full, axis=1)], axis=1)
    pos = pos0 + jnp.arange(t)
    end = cs[:, POOL_HIST + 1:]
    means = []
    for gi, w in enumerate(POOL_WINDOWS):
        cols = slice(gi * B_GROUP_DIM, (gi + 1) * B_GROUP_DIM)
        start = cs[:, POOL_HIST + 1 - w:POOL_HIST + 1 - w + t, cols]
        cnt = jnp.minimum(pos + 1, w).astype(jnp.float32)
        means.append((end[:, :, cols] - start) / cnt[None, :, None])
    d = jnp.concatenate(means, axis=-1) - full[:, POOL_HIST:]
    y = jnp.einsum('ntgc,gce->ntge', d.reshape(n, t, B_GROUPS, B_GROUP_DIM), w_grp.astype(jnp.float32))
    y = y.reshape(n, t, c) * scale.astype(jnp.float32)
    return y, full[:, -POOL_HIST:].astype(u.dtype)


def gla_chunked(q, k, v, log_f, s0, block):
    n, t, h, dk = q.shape
    dv = v.shape[-1]
    nb = t // block

    def blocks(a):
        return a.reshape(n, nb, block, h, a.shape[-1]).transpose(1, 0, 3, 2, 4)

    causal = jnp.tril(jnp.ones((block, block), dtype=bool))

    def step(state, xs):
        qb, kb, vb, lfb = xs
        b = jnp.cumsum(lfb, axis=2)
        rel = jnp.where(causal[:, :, None], b[:, :, :, None, :] - b[:, :, None, :, :], -jnp.inf)
        scores = jnp.einsum('nhtsc,nhsc->nhts', qb[:, :, :, None, :] * jnp.exp(rel), kb)
        out = (jnp.einsum('nhtc,nhcv->nhtv', qb * jnp.exp(b), state)
               + jnp.einsum('nhts,nhsv->nhtv', scores, vb))
        b_last = b[:, :, -1:, :]
        state = (jnp.exp(b_last[:, :, 0, :, None]) * state
                 + jnp.einsum('nhsc,nhsv->nhcv', kb * jnp.exp(b_last - b), vb))
        return state, out

    state, out = lax.scan(step, s0, (blocks(q), blocks(k), blocks(v), blocks(log_f)))
    out = out.transpose(1, 0, 3, 2, 4).reshape(n, t, h, dv)
    return out, state


def hgrn2_mixer(q, f, i, g, lb, norm_w, s0, block):
    n, t, _ = q.shape
    shp = (n, t, C_HEADS, C_KEY_DIM)
    lbh = lb.reshape(C_HEADS, C_KEY_DIM)
    forget = lbh + (1.0 - lbh) * jax.nn.sigmoid(f.astype(jnp.float32)).reshape(shp)
    log_f = jnp.log(forget)
    key_in = 1.0 - forget
    qf = jax.nn.silu(q.astype(jnp.float32)).reshape(shp)
    vf = i.astype(jnp.float32).reshape(n, t, C_HEADS, C_VAL_DIM)
    o, s = gla_chunked(qf, key_in, vf, log_f, s0.astype(jnp.float32), block)
    o = o * lax.rsqrt(jnp.mean(o * o, axis=-1, keepdims=True) + RMS_EPS) * norm_w.astype(jnp.float32)
    o = o.reshape(n, t, C_VAL_WIDTH) * jax.nn.silu(g.astype(jnp.float32))
    return o, s


def token_mixers(x, w_in, rel_bias, pool_w, pool_scale, lb, norm_w, w_out, state):
    n, t, _ = x.shape
    z = x @ w_in
    offs = [int(o) for o in np.cumsum(IN_SPLITS)[:-1]]
    q_a, k_a, v_a, u_b, q_c, f_c, i_c, g_c = jnp.split(z, offs, axis=-1)
    q_a = q_a.reshape(n, t, A_HEADS, A_HEAD_DIM)
    k_a = k_a.reshape(n, t, A_HEADS, A_HEAD_DIM)
    v_a = v_a.reshape(n, t, A_HEADS, A_HEAD_DIM)
    if state is None:
        o_a = band_attention_prompt(q_a, k_a, v_a, rel_bias)
        rows = min(A_PAST, t)
        new_k, new_v = k_a[:, t - rows:], v_a[:, t - rows:]
        pool_hist = jnp.zeros((n, POOL_HIST, B_WIDTH), u_b.dtype)
        pos0 = 0
        s0 = jnp.zeros((n, C_HEADS, C_KEY_DIM, C_VAL_DIM), jnp.float32)
        block = CHUNK
    else:
        cache_k, cache_v, pool_hist, s0 = state
        o_a = band_attention_sample(q_a, k_a, v_a, cache_k, cache_v, rel_bias)
        new_k, new_v = k_a, v_a
        pos0 = PAST_LEN
        block = t
    o_b, new_pool = pool_mixer(u_b, pool_hist, pos0, pool_w, pool_scale)
    o_c, new_s = hgrn2_mixer(q_c, f_c, i_c, g_c, lb, norm_w, s0, block)
    mixed = jnp.concatenate([o_a, o_b.astype(x.dtype), o_c.astype(x.dtype)], axis=-1)
    return mixed @ w_out, (new_k, new_v, new_pool, new_s)


def hgrn_lower_bound_schedule(raw):
    s = jax.nn.softmax(raw.astype(jnp.float32), axis=0)
    return jnp.cumsum(s, axis=0) - s[0:1]


def run_trunk(x, p, states, lb, ffn1_w_gu, ffn1_w_down, w_in, attn_rel_bias, pool_w, pool_scale,
              hgrn_norm_w, w_out, ffn2_w_gu, ffn2_w_down, ple_w_gate, ple_w_proj, ln_g, ln_b):
    collected = ([], [], [], [])
    for i in range(DEPTH):
        x = layer_norm(DEEPNORM_ALPHA * x + 0.5 * swiglu(x, ffn1_w_gu[i], ffn1_w_down[i]), ln_g[i, 0], ln_b[i, 0])
        layer_state = None if states is None else (states[0][i], states[1][i], states[2][i], states[3][i])
        m, new = token_mixers(x, w_in[i], attn_rel_bias[i], pool_w[i], pool_scale[i], lb[i],
                              hgrn_norm_w[i], w_out[i], layer_state)
        x = layer_norm(DEEPNORM_ALPHA * x + m, ln_g[i, 1], ln_b[i, 1])
        x = layer_norm(DEEPNORM_ALPHA * x + 0.5 * swiglu(x, ffn2_w_gu[i], ffn2_w_down[i]), ln_g[i, 2], ln_b[i, 2])
        emb = jax.nn.sigmoid(x @ ple_w_gate[i]) * (p[i].astype(x.dtype) @ ple_w_proj[i])
        x = layer_norm(DEEPNORM_ALPHA * x + emb, ln_g[i, 3], ln_b[i, 3])
        for lst, arr in zip(collected, new):
            lst.append(arr)
    return x, tuple(jnp.stack(lst, axis=0) for lst in collected)


def setup_inputs(seed: int = 0) -> dict:
    key = jax.random.key(seed)
    ks = jax.random.split(key, 24)
    f32 = jnp.float32

    def nrm(k, shape, scale):
        return jax.random.normal(k, shape, f32) * scale

    a_cache = min(A_PAST, PAST_LEN)
    return {
        'x_prompt': nrm(ks[0], (BATCH, SEQ, D_MODEL), 1.0),
        'x_sample': nrm(ks[1], (DEC_BATCH, DEC_SEQ, D_MODEL), 1.0),
        'p_prompt': nrm(ks[2], (DEPTH, BATCH, SEQ, PLE_DIM), 1.0),
        'p_sample': nrm(ks[3], (DEPTH, DEC_BATCH, DEC_SEQ, PLE_DIM), 1.0),
        'cache_attn_k': nrm(ks[4], (DEPTH, DEC_BATCH, a_cache, A_HEADS, A_HEAD_DIM), 1.0),
        'cache_attn_v': nrm(ks[5], (DEPTH, DEC_BATCH, a_cache, A_HEADS, A_HEAD_DIM), 1.0),
        'state_pool': nrm(ks[6], (DEPTH, DEC_BATCH, POOL_HIST, B_WIDTH), 1.0),
        'state_hgrn': nrm(ks[7], (DEPTH, DEC_BATCH, C_HEADS, C_KEY_DIM, C_VAL_DIM), 0.5),
        'ffn1_w_gu': nrm(ks[8], (DEPTH, D_MODEL, 2 * FFN_HIDDEN), D_MODEL ** -0.5),
        'ffn1_w_down': nrm(ks[9], (DEPTH, FFN_HIDDEN, D_MODEL), FFN_HIDDEN ** -0.5 * DEEPNORM_BETA),
        'w_in': nrm(ks[10], (DEPTH, D_MODEL, IN_WIDTH), D_MODEL ** -0.5),
        'attn_rel_bias': nrm(ks[11], (DEPTH, A_HEADS, REL_SIZE), 0.3),
        'pool_w': nrm(ks[12], (DEPTH, B_GROUPS, B_GROUP_DIM, B_GROUP_DIM), B_GROUP_DIM ** -0.5),
        'pool_scale': 1.0 + nrm(ks[13], (DEPTH, B_WIDTH), 0.1),
        'hgrn_lower_bounds': nrm(ks[14], (DEPTH, C_KEY_WIDTH), 0.3),
        'hgrn_norm_w': 1.0 + nrm(ks[15], (DEPTH, C_VAL_DIM), 0.1),
        'w_out': nrm(ks[16], (DEPTH, MIX_WIDTH, D_MODEL), MIX_WIDTH ** -0.5 * DEEPNORM_BETA),
        'ffn2_w_gu': nrm(ks[17], (DEPTH, D_MODEL, 2 * FFN_HIDDEN), D_MODEL ** -0.5),
        'ffn2_w_down': nrm(ks[18], (DEPTH, FFN_HIDDEN, D_MODEL), FFN_HIDDEN ** -0.5 * DEEPNORM_BETA),
        'ple_w_gate': nrm(ks[19], (DEPTH, D_MODEL, D_MODEL), D_MODEL ** -0.5),
        'ple_w_proj': nrm(ks[20], (DEPTH, PLE_DIM, D_MODEL), PLE_DIM ** -0.5 * DEEPNORM_BETA),
        'ln_g': 1.0 + nrm(ks[21], (DEPTH, 4, D_MODEL), 0.1),
        'ln_b': nrm(ks[22], (DEPTH, 4, D_MODEL), 0.02),
    }


def reference(x_prompt, x_sample, p_prompt, p_sample, cache_attn_k, cache_attn_v, state_pool, state_hgrn,
              ffn1_w_gu, ffn1_w_down, w_in, attn_rel_bias, pool_w, pool_scale, hgrn_lower_bounds,
              hgrn_norm_w, w_out, ffn2_w_gu, ffn2_w_down, ple_w_gate, ple_w_proj, ln_g, ln_b):
    lb = hgrn_lower_bound_schedule(hgrn_lower_bounds)
    y_prompt, (pk, pv, ppool, phgrn) = run_trunk(
        x_prompt, p_prompt, None, lb, ffn1_w_gu, ffn1_w_down, w_in, attn_rel_bias, pool_w, pool_scale,
        hgrn_norm_w, w_out, ffn2_w_gu, ffn2_w_down, ple_w_gate, ple_w_proj, ln_g, ln_b)
    y_sample, (sk, sv, spool, shgrn) = run_trunk(
        x_sample, p_sample, (cache_attn_k, cache_attn_v, state_pool, state_hgrn), lb,
        ffn1_w_gu, ffn1_w_down, w_in, attn_rel_bias, pool_w, pool_scale,
        hgrn_norm_w, w_out, ffn2_w_gu, ffn2_w_down, ple_w_gate, ple_w_proj, ln_g, ln_b)
    return (y_prompt, y_sample, pk, pv, ppool, phgrn, sk, sv, spool, shgrn)
```

```python
import contextlib
import numpy as np
import concourse.bass as bass
import concourse.mybir as mybir
from concourse.bass_utils import run_bass_kernel_spmd

F32 = mybir.dt.float32
BF16 = mybir.dt.bfloat16
AF = mybir.ActivationFunctionType
ALU = mybir.AluOpType
AX = mybir.AxisListType

D = 1024
HID = 2816
INW = 2816
ALPHA = float(8 ** 0.25)
LN_EPS = 1e-5
RMS_EPS = 1e-6
NSLOT = 6
RUN_SAMPLE = True
STREAMS = ("pe", "act", "dve", "pool", "sp")


class Buf:
    __slots__ = ("name", "lw", "rd", "excl")

    def __init__(self, name, excl=False):
        self.name = name
        self.lw = None
        self.rd = []
        self.excl = excl


class Op:
    __slots__ = ("stream", "fn", "waits", "sig", "comp", "cidx", "tag")

    def __init__(self, stream, fn, comp, cidx):
        self.stream, self.fn, self.comp, self.cidx = stream, fn, comp, cidx
        self.tag = None
        self.waits = []
        self.sig = False


class Sched:
    def __init__(self):
        self.ops = {s: [] for s in STREAMS}
        self.comp_ops = {}
        self.seen = {s: {} for s in STREAMS}
        self.nops = 0
        self.tag = "init"
        self.annotate = False

    def _add(self, stream, fn, reads, writes, dkey=None):
        comp = stream if dkey is None else "d:" + dkey
        lst = self.comp_ops.setdefault(comp, [])
        op = Op(stream, fn, comp, len(lst))
        op.tag = self.tag
        if dkey is not None:
            op.sig = True
        lst.append(op)
        me = (comp, op.cidx)
        deps = {}

        def need(d):
            if d is None:
                return
            e, i = d
            if e == "pe" and comp == "pe":
                return
            if deps.get(e, -1) < i:
                deps[e] = i

        for b in reads:
            need(b.lw)
            if b.excl:
                for r in b.rd:
                    if r[0] != comp:
                        need(r)
        for b in writes:
            need(b.lw)
            for r in b.rd:
                need(r)
        if dkey is not None and op.cidx > 0:
            need((comp, op.cidx - 1))
        seen = self.seen[stream]
        for e, i in deps.items():
            if seen.get(e, -1) >= i:
                continue
            seen[e] = i
            op.waits.append((e, i))
            self.comp_ops[e][i].sig = True
        for b in reads:
            if len(b.rd) > 64:
                last = {}
                for r in b.rd:
                    if last.get(r[0], -1) < r[1]:
                        last[r[0]] = r[1]
                b.rd = list(last.items())
            b.rd.append(me)
        for b in writes:
            b.lw = me
            b.rd = []
        self.ops[stream].append(op)
        self.nops += 1
        return op

    def mm(self, out, lhsT, rhs, start, stop, reads, writes):
        return self._add("pe", ("matmul", dict(out=out, lhsT=lhsT, rhs=rhs, start=start, stop=stop)), reads, writes)

    def tr(self, out, in_, identity, reads, writes):
        return self._add("pe", ("transpose", dict(out=out, in_=in_, identity=identity)), reads, writes)

    def A(self, out, in_, func, reads, writes, **kw):
        return self._add("act", ("activation", dict(out=out, in_=in_, func=func, **kw)), reads, writes)

    def V(self, name, reads, writes, **kw):
        return self._add("dve", (name, kw), reads, writes)

    def G(self, name, reads, writes, **kw):
        return self._add("pool", (name, kw), reads, writes)

    def D(self, stream, dkey, out, in_, reads, writes, slow=False):
        kw = dict(out=out, in_=in_)
        if slow:
            kw["allow_slow_non_contiguous"] = True
        return self._add(stream, ("dma_start", kw), reads, writes, dkey=dkey)

    def emit(self, nc, final_bufs):
        self._add("sp", None, final_bufs, [])
        rank = {}
        for comp, lst in self.comp_ops.items():
            c = 0
            for op in lst:
                if op.sig:
                    c += 1
                    rank[(comp, op.cidx)] = c
        with contextlib.ExitStack() as st:
            sems = {}
            for comp in self.comp_ops:
                if any(op.sig for op in self.comp_ops[comp]):
                    sems[comp] = st.enter_context(nc.semaphore("s_" + comp.replace(":", "_")))
            block = st.enter_context(nc.Block())

            def run(stream):
                def body(eng):
                    for op in self.ops[stream]:
                        for (e, i) in op.waits:
                            eng.wait_ge(sems[e], rank[(e, i)] * (16 if e.startswith("d:") else 1))
                        if op.fn is None:
                            continue
                        ins = getattr(eng, op.fn[0])(**op.fn[1])
                        if self.annotate:
                            ins.annotate(op.tag)
                        if op.sig:
                            ins.then_inc(sems[op.comp], 16 if op.comp.startswith("d:") else 1)
                return body

            block.tensor(run("pe"))
            block.scalar(run("act"))
            block.vector(run("dve"))
            block.gpsimd(run("pool"))
            block.sync(run("sp"))
        return len(sems)


class Prog:
    def __init__(self, depth, n_ptiles, n_samp):
        self.depth, self.n_ptiles, self.n_samp = depth, n_ptiles, n_samp
        self.nc = bass.Bass("TRN2", target_bir_lowering=False)
        self.S = Sched()
        self.st = contextlib.ExitStack()
        self.rr = {}

    def dram(self, name, shape, kind):
        return self.nc.dram_tensor(name, list(shape), F32, kind=kind).ap()

    def sb(self, name, shape, dt=F32):
        return self.st.enter_context(self.nc.sbuf_tensor("sb_" + name, list(shape), dt))

    def build(self):
        nc, S, depth = self.nc, self.S, self.depth
        NP = self.n_ptiles * 512
        NS = self.n_samp * 32
        nsm = self.n_samp
        I, O = "ExternalInput", "ExternalOutput"
        dr = self.dram
        xp = dr("xp", [max(NP, 1), D], I); pp = dr("pp", [depth, max(NP, 1), 256], I)
        xs = dr("xs", [NS, D], I); ps_ = dr("ps", [depth, NS, 256], I)
        ck = dr("ck", [depth, nsm, 512, 512], I); cv = dr("cv", [depth, nsm, 512, 512], I)
        spool = dr("spool", [depth, nsm, 15, 256], I); shg = dr("shg", [depth, nsm, 4, 64, 64], I)
        Wd = {}
        for nm, shp in [("f1gu", [depth, D, 2 * HID]), ("f1d", [depth, HID, D]), ("win", [depth, D, INW]),
                        ("wout", [depth, D, D]), ("f2gu", [depth, D, 2 * HID]), ("f2d", [depth, HID, D]),
                        ("pleg", [depth, D, D]), ("plep", [depth, 256, D]), ("lng", [depth, 4, D]),
                        ("lnb", [depth, 4, D]), ("poolw", [depth, 4, 64, 64]), ("pscale", [depth, 256]),
                        ("lbraw", [depth, 256]), ("normw", [depth, 64]), ("rb34", [depth, 8, 128, 256]),
                        ("rbc", [depth, 8])]:
            Wd[nm] = dr(nm, shp, I)
        yp = dr("yp", [max(NP, 1), D], O); ys = dr("ys", [NS, D], O)
        pk2 = dr("pk", [depth * 512, 512], O); pv2 = dr("pv", [depth * 512, 512], O)
        ppool2 = dr("ppool", [depth * 15, 256], O); phg2 = dr("phg", [depth * 256, 64], O)
        sk2 = dr("sk", [depth * NS, 512], O); sv2 = dr("sv", [depth * NS, 512], O)
        spool2 = dr("spool_o", [depth * nsm * 15, 256], O); shg2 = dr("shg_o", [depth * nsm * 256, 64], O)
        pk = [pk2[l * 512:(l + 1) * 512, :] for l in range(depth)]; pv = [pv2[l * 512:(l + 1) * 512, :] for l in range(depth)]
        sk = [sk2[l * NS:(l + 1) * NS, :] for l in range(depth)]; sv = [sv2[l * NS:(l + 1) * NS, :] for l in range(depth)]
        ppool = [ppool2[l * 15:(l + 1) * 15, :] for l in range(depth)]
        phg = [phg2[l * 256:(l + 1) * 256, :].rearrange("(h c) v -> h c v", h=4) for l in range(depth)]
        spool_o = {(l, si): spool2[(l * nsm + si) * 15:(l * nsm + si + 1) * 15, :] for l in range(depth) for si in range(nsm)}
        shg_o = {(l, si): shg2[(l * nsm + si) * 256:(l * nsm + si + 1) * 256, :].rearrange("(h c) v -> h c v", h=4) for l in range(depth) for si in range(nsm)}
        outbufs = []

        def outbuf(name):
            b = Buf(name)
            outbufs.append(b)
            return b

        sb = self.sb
        x = sb("x", [128, 4, D]); Bx = [Buf(f"x{t}") for t in range(4)]
        xT = sb("xT", [128, 8, 512], BF16); BxT = [Buf(f"xT{t}") for t in range(4)]
        hT = sb("hT", [128, 12, 512], BF16); BhT = [Buf(f"hT{j}") for j in range(12)]
        ring = [sb(f"ring{i}", [128, 4096], BF16) for i in range(NSLOT)]
        Bring = [Buf(f"ring{i}") for i in range(NSLOT)]
        gb = [sb(f"gb{i}", [128, 2, D]) for i in range(2)]; Bgb = [Buf(f"gb{i}") for i in range(2)]
        ident = sb("ident", [128, 128]); identb = sb("identb", [128, 128], BF16); Bid = Buf("ident")
        tri = sb("tri", [64, 64]); Btri = Buf("tri")
        resetm = sb("resetm", [64, 2, 128]); Bresetm = Buf("resetm")
        ptmp = sb("ptmp", [128, 16]); Bptmp = Buf("ptmp")
        mhalf = sb("mhalf", [128, 1]); Bmh = Buf("mhalf")
        sgb = [sb(f"sg{i}", [128, 512]) for i in range(2)]; Bsg = [Buf(f"sg{i}") for i in range(2)]
        stt = [sb(f"stt{i}", [128, 24]) for i in range(4)]; Bstt = [Buf(f"stt{i}") for i in range(4)]
        kc_l = [sb(f"kcar{l}", [128, 4, 512], BF16) for l in range(depth)]
        vc_l = [sb(f"vcar{l}", [128, 4, 8, 65], BF16) for l in range(depth)]
        Bkc = [[Buf(f"kc{l}_{t}") for t in range(4)] for l in range(depth)]
        Bvc = [[Buf(f"vc{l}_{t}") for t in range(4)] for l in range(depth)]
        kcur = sb("kcur", [128, 4, 512], BF16); Bkcur = Buf("kcur")
        vcur = sb("vcur", [128, 4, 8, 65], BF16); Bvcur = [Buf(f"vcur{t}") for t in range(4)]
        qT = sb("qT", [128, 4, 512], BF16); BqT = Buf("qT")
        eb34 = sb("eb34", [128, 8, 256]); Beb = Buf("eb34")
        cbt = sb("cbt", [128, depth, 8]); Bcb = Buf("cbt")
        PT = [sb(f"PT{i}", [128, 640], BF16) for i in range(2)]; BPT = [Buf(f"PT{i}") for i in range(2)]
        etmp = [sb(f"etmp{i}", [128, 256]) for i in range(2)]; Bet = [Buf(f"etmp{i}") for i in range(2)]
        oa = sb("oa", [128, 512], BF16); Boa = Buf("oa")
        rden = sb("rden", [128, 8]); Brd = Buf("rden")
        kvst = sgb; Bkvst = Bsg
        uT = sb("uT", [128, 2, 144]); BuT = Buf("uT")
        sA = sb("sA", [128, 2, 144]); BsA = Buf("sA")
        sBb = sb("sBb", [128, 2, 144]); BsB = Buf("sBb")
        dT = sb("dT", [128, 2, 128], BF16); BdT = Buf("dT")
        ucar = sb("ucar", [128, depth, 2, 16]); Bucar = [Buf(f"ucar{l}") for l in range(depth)]
        wpool = sb("wpool", [128, depth, 2, 128], BF16); Bwp = Buf("wpool")
        pscale = sb("pscale", [128, depth, 2]); Bps = Buf("pscale")
        rc16 = sb("rc16", [128, 2, 16]); Brc = Buf("rc16")
        pst = sb("pst", [16, 256]); Bpst = Buf("pst")
        ptok = sb("ptok", [16, 256]); Bptok = Buf("ptok")
        lbt = sb("lbt", [64, depth, 4]); omlt = sb("omlt", [64, depth, 4]); Blb = Buf("lb")
        lbtmp = sb("lbtmp", [64, depth, 4])
        lbsum = sb("lbsum", [64, 4])
        normwb = sb("normwb", [64, depth, 64]); Bnw = Buf("normw")
        Sst = sb("Sst", [64, depth, 4, 64]); BSst = [Buf(f"Sst{l}") for l in range(depth)]
        QS = sb("QS", [64, 4, 128]); BQS = Buf("QS")
        Fb = sb("Fb", [64, 4, 128]); BF_ = Buf("F")
        KIN = sb("KIN", [64, 4, 128]); BKIN = Buf("KIN")
        Bc = sb("Bc", [64, 4, 128]); BBc = Buf("Bc")
        EQ = sb("EQ", [64, 4, 128]); BEQ = Buf("EQ")
        EK = sb("EK", [64, 4, 128]); BEK = Buf("EK")
        QTt = sb("QTt", [64, 4, 128], BF16); BQT = Buf("QTt")
        KTt = sb("KTt", [64, 4, 128], BF16); BKT = Buf("KTt")
        EBM = sb("EBM", [64, 4, 8]); BEBM = Buf("EBM")
        EBL = sb("EBL", [64, 4, 8]); BEBL = Buf("EBL")
        VTOK = [sb(f"VTOK{i}", [64, 256], BF16) for i in range(4)]; BVT = [Buf(f"VTOK{i}") for i in range(4)]
        G2 = [sb(f"G2{i}", [64, 256]) for i in range(4)]; BG2 = [Buf(f"G2{i}") for i in range(4)]
        KTOK = sb("KTOK", [64, 4, 64], BF16); BKTOK = Buf("KTOK")
        SM = sb("SM", [64, 4, 64], BF16); BSM = Buf("SM")
        Sbf = sb("Sbf", [64, 4, 64], BF16); BSbf = Buf("Sbf")
        T1 = sb("T1", [64, 4, 64]); BT1 = Buf("T1")
        osq = sb("osq", [64, 256]); Bosq = Buf("osq")
        oss = sb("oss", [64, 8]); Boss = Buf("oss")
        octok = sb("octok", [64, 256], BF16); Boc = Buf("octok")
        hst = sb("hst", [64, 256]); Bhst = Buf("hst")
        ptokp = sb("ptokp", [128, 1, 256]); Bptokp = Buf("ptokp")
        pTt = sb("pTt", [128, 2, 512], BF16); BpT = Buf("pTt")
        etile = sgb; Bet2 = Bsg
        ckst = sb("ckst", [128, 512], BF16); Bckst = Buf("ckst")
        pb = [self.st.enter_context(nc.psum_tensor(f"pb{i}", [128, 512], F32)) for i in range(8)]
        Bpb = [Buf(f"pb{i}", excl=True) for i in range(8)]
        pbb = [p.bitcast(BF16) for p in pb]

        def rr(key, n):
            v = self.rr.get(key, 0)
            self.rr[key] = v + 1
            return v % n

        plan = []
        consumed = [0]
        issued = [0]

        def layer_plan(l):
            p = []
            for f in ("f1", "f2"):
                q = []
                for g in (0, 1):
                    for b in ((0, 1, 2) if g == 0 else (3, 4, 5)):
                        nc_ = 512 if b < 5 else 256
                        q.append(("k", f + "gu", l, b * 512, nc_))
                        q.append(("k", f + "gu", l, HID + b * 512, nc_))
                    for b in ((0, 1, 2) if g == 0 else (3, 4, 5)):
                        q.append(("j", f + "d", l, b * 512, 4 if b < 5 else 2))
                if f == "f1":
                    p += q
                    for b in range(6):
                        p.append(("k", "win", l, b * 512, 512 if b < 5 else 256))
                    p.append(("k", "wout", l, 0, 512)); p.append(("k", "wout", l, 512, 512))
                else:
                    p += q
                    p.append(("k", "pleg", l, 0, 512)); p.append(("k", "pleg", l, 512, 512))
                    p.append(("j", "plep", l, 0, 2))
            return p

        ntiles_total = self.n_ptiles + (1 if self.n_samp else 0)
        for _ in range(ntiles_total):
            for l in range(depth):
                plan.extend(layer_plan(l))

        def issue_upto(n):
            while issued[0] < min(n, len(plan)):
                i = issued[0]
                kind, nm, l, a0, n_ = plan[i]
                s = i % NSLOT
                if kind == "k":
                    src = Wd[nm][l][:, a0:a0 + n_].rearrange("(k p) n -> p k n", p=128)
                    dst = ring[s][:, 0:8 * n_].rearrange("p (k n) -> p k n", k=8)
                else:
                    src = Wd[nm][l][a0:a0 + n_ * 128, :].rearrange("(j p) n -> p j n", p=128)
                    dst = ring[s][:, 0:n_ * 1024].rearrange("p (j n) -> p j n", j=n_)
                S.D("pool", f"ring{s}", dst, src, [], [Bring[s]])
                issued[0] += 1

        def wnext(kind, nm, l, a0, n_):
            i = consumed[0]
            assert plan[i] == (kind, nm, l, a0, n_), (plan[i], (kind, nm, l, a0, n_))
            issue_upto(i + NSLOT - 2)
            consumed[0] += 1
            s = i % NSLOT
            if kind == "k":
                return ring[s][:, 0:8 * n_].rearrange("p (k n) -> p k n", k=8), Bring[s]
            return ring[s][:, 0:n_ * 1024].rearrange("p (j n) -> p j n", j=n_), Bring[s]

        mm, tr, A, V, G, Dm = S.mm, S.tr, S.A, S.V, S.G, S.D
        G("memset", [], [Bid], ap=ident[:], constant=1.0)
        G("affine_select", [Bid], [Bid], out=ident[:], in_=ident[:], pattern=[[-1, 128]], compare_op=ALU.is_equal, fill=0.0, base=0, channel_multiplier=1)
        V("tensor_copy", [Bid], [Bid], out=identb[:], in_=ident[:])
        G("memset", [], [Btri], ap=tri[:], constant=1.0)
        G("affine_select", [Btri], [Btri], out=tri[:], in_=tri[:], pattern=[[1, 64]], compare_op=ALU.is_ge, fill=0.0, base=0, channel_multiplier=-1)
        G("memset", [], [Bresetm], ap=resetm[:], constant=1.0)
        for blk in range(2):
            G("memset", [Bresetm], [Bresetm], ap=resetm[:, 0, blk * 64:blk * 64 + 1], constant=0.0)
        for blk in range(4):
            G("memset", [Bresetm], [Bresetm], ap=resetm[:, 1, blk * 32:blk * 32 + 1], constant=0.0)
        for blk in range(8):
            pass
        G("memset", [], [Bmh], ap=mhalf[:], constant=-0.5)
        G("memset", [], [Bptok], ap=ptok[:], constant=0.0)
        for l in range(depth):
            G("memset", [], Bvc[l], ap=vc_l[l][:], constant=1.0)
            G("memset", [], Bkc[l], ap=kc_l[l][:], constant=0.0)
        G("memset", [], Bvcur, ap=vcur[:], constant=1.0)
        G("memset", [], [Brc], ap=rc16[:], constant=1.0)
        for c in range(2):
            for hf in range(2):
                w = (2, 4, 8, 16)[c * 2 + hf]
                for t in range(1, 16):
                    val = 1.0 / min(t + 1, w)
                    if t < w:
                        hi = 16 if t == w - 1 else t + 1
                        G("memset", [Brc], [Brc], ap=rc16[hf * 64:hf * 64 + 64, c, t:hi], constant=val)
        G("memset", [], [Bwp], ap=wpool[:], constant=0.0)
        for l in range(depth):
            for g in range(4):
                c, hf = g // 2, g % 2
                Dm("pool", "cstw", wpool[hf * 64:hf * 64 + 64, l, c, hf * 64:hf * 64 + 64], Wd["poolw"][l, g], [Bwp], [Bwp])
        with nc.allow_non_contiguous_dma(reason="tiny constant loads"):
            for l in range(depth):
                Dm("sp", "cst", pscale[:, l, :], Wd["pscale"][l].rearrange("(c p) -> p c", p=128), [], [Bps], slow=True)
                Dm("sp", "cst", lbtmp[:, l, :], Wd["lbraw"][l].rearrange("(h c) -> c h", c=64), [], [Blb], slow=True)
                Dm("sp", "cst", cbt[:, l, :], Wd["rbc"][l].partition_broadcast(128), [], [Bcb])
                Dm("sp", "cst", normwb[:, l, :], Wd["normw"][l].partition_broadcast(64), [], [Bnw])
        A(lbtmp[:], lbtmp[:], AF.Exp, [Blb], [Blb])
        V("tensor_copy", [Blb], [Blb], out=lbsum[:], in_=lbtmp[:, 0, :])
        for l in range(1, depth):
            V("tensor_tensor", [Blb], [Blb], out=lbsum[:], in0=lbsum[:], in1=lbtmp[:, l, :], op=ALU.add)
        V("reciprocal", [Blb], [Blb], out=lbsum[:], in_=lbsum[:])
        V("memset", [Blb], [Blb], ap=lbt[:, 0, :], constant=0.0)
        for l in range(1, depth):
            V("tensor_tensor", [Blb], [Blb], out=lbtmp[:, l, :], in0=lbtmp[:, l, :], in1=lbsum[:], op=ALU.mult)
            V("tensor_tensor", [Blb], [Blb], out=lbt[:, l, :], in0=lbt[:, l - 1, :], in1=lbtmp[:, l, :], op=ALU.add)
        V("tensor_scalar", [Blb], [Blb], out=omlt[:], in0=lbt[:], scalar1=-1.0, scalar2=1.0, op0=ALU.mult, op1=ALU.add)

        def load_gb(l, i):
            s = rr("gb", 2)
            Dm("sp", f"gb{s}", gb[s][:, 0, :], Wd["lng"][l, i].partition_broadcast(128), [], [Bgb[s]])
            Dm("sp", f"gb{s}", gb[s][:, 1, :], Wd["lnb"][l, i].partition_broadcast(128), [], [Bgb[s]])
            return s

        def transposes(tt, ntok):
            S.tag = "tr"
            for half in range(2):
                b = 6 + half
                for k4 in range(4):
                    kc = half * 4 + k4
                    tr(pb[b][:, k4 * 128:k4 * 128 + ntok], x[0:ntok, tt, kc * 128:(kc + 1) * 128], ident[0:ntok, 0:ntok], [Bx[tt], Bid], [Bpb[b]])
                src = pb[b][:].rearrange("p (k n) -> p k n", k=4)[:, :, 0:ntok]
                dst = xT[:, half * 4:half * 4 + 4, tt * 128:tt * 128 + ntok]
                if half == 0:
                    A(dst, src, AF.Copy, [Bpb[b]], [BxT[tt]])
                else:
                    V("tensor_copy", [Bpb[b]], [BxT[tt]], out=dst, in_=src)

        def post(l, i, tt, ntok, gslot):
            s = tt
            S.tag = "ln"
            xt = x[0:ntok, tt, :]
            st_ = stt[s]
            V("bn_stats", [Bx[tt]], [Bstt[s]], out=st_[0:ntok, 0:6], in_=x[0:ntok, tt, 0:512])
            V("bn_stats", [Bx[tt]], [Bstt[s]], out=st_[0:ntok, 6:12], in_=x[0:ntok, tt, 512:1024])
            V("bn_aggr", [Bstt[s]], [Bstt[s]], out=st_[0:ntok, 12:14], in_=st_[0:ntok, 0:12])
            V("tensor_scalar", [Bstt[s]], [Bstt[s]], out=st_[0:ntok, 14:15], in0=st_[0:ntok, 13:14], scalar1=LN_EPS, scalar2=None, op0=ALU.add)
            A(st_[0:ntok, 17:18], st_[0:ntok, 14:15], AF.Sqrt, [Bstt[s]], [Bstt[s]])
            V("reciprocal", [Bstt[s]], [Bstt[s]], out=st_[0:ntok, 15:16], in_=st_[0:ntok, 17:18])
            V("scalar_tensor_tensor", [Bstt[s]], [Bstt[s]], out=st_[0:ntok, 16:17], in0=st_[0:ntok, 12:13], scalar=-1.0, in1=st_[0:ntok, 15:16], op0=ALU.mult, op1=ALU.mult)
            A(xt, xt, AF.Identity, [Bx[tt], Bstt[s]], [Bx[tt]], bias=st_[0:ntok, 16:17], scale=st_[0:ntok, 15:16])
            V("tensor_tensor", [Bx[tt], Bgb[gslot]], [Bx[tt]], out=xt, in0=xt, in1=gb[gslot][0:ntok, 0, :], op=ALU.mult)
            V("tensor_tensor", [Bx[tt], Bgb[gslot]], [Bx[tt]], out=xt, in0=xt, in1=gb[gslot][0:ntok, 1, :], op=ALU.add)

        class Delayed:
            def __init__(self):
                self.pending = None
            def push(self, tt, ntok):
                self.flush()
                self.pending = (tt, ntok)
            def flush(self):
                if self.pending is not None:
                    transposes(*self.pending)
                    self.pending = None

        def ffn(l, f, lni, T, tiles):
            gslot = load_gb(l, lni)
            for g in (0, 1):
                blks = (0, 1, 2) if g == 0 else (3, 4, 5)
                jj = 0
                for b in blks:
                    ncols = 512 if b < 5 else 256
                    S.tag = "ffn.up"
                    wg, Bg = wnext("k", f + "gu", l, b * 512, ncols)
                    wu, Bu = wnext("k", f + "gu", l, HID + b * 512, ncols)
                    for cc in range(ncols // 128):
                        pg = rr("pg", 2); pu = 2 + rr("pu", 2)
                        for kc in range(8):
                            mm(pb[pg][:, 0:T], wg[:, kc, cc * 128:(cc + 1) * 128], xT[:, kc, 0:T], kc == 0, kc == 7, [Bg] + BxT, [Bpb[pg]])
                        for kc in range(8):
                            mm(pb[pu][:, 0:T], wu[:, kc, cc * 128:(cc + 1) * 128], xT[:, kc, 0:T], kc == 0, kc == 7, [Bu] + BxT, [Bpb[pu]])
                        sgi = rr("sg", 2)
                        A(sgb[sgi][:, 0:T], pb[pg][:, 0:T], AF.Silu, [Bpb[pg]], [Bsg[sgi]])
                        V("scalar_tensor_tensor", [Bpb[pu], Bsg[sgi]], [BhT[jj]], out=hT[:, jj, 0:T], in0=pb[pu][:, 0:T], scalar=0.5, in1=sgb[sgi][:, 0:T], op0=ALU.mult, op1=ALU.mult)
                        jj += 1
                nj = jj
                import os
                ksub = int(os.environ.get("KSUB", "9"))
                if ksub == 1:
                    return
                wds = []
                S.tag = "ffn.down"
                for b in blks:
                    wds.append(wnext("j", f + "d", l, b * 512, 4 if b < 5 else 2))
                dl = Delayed()
                for (tt, ntok) in tiles:
                    S.tag = "ffn.down"
                    pys = []
                    for nh in range(2):
                        py = 4 + rr("py", 2)
                        pys.append(py)
                        for j in range(nj):
                            wd, Bw = wds[j // 4]
                            mm(pb[py][0:ntok, :], hT[:, j, tt * 128:tt * 128 + ntok], wd[:, j % 4, nh * 512:(nh + 1) * 512], j == 0, j == nj - 1, [BhT[j], Bw], [Bpb[py]])
                    dl.flush()
                    for nh in range(2):
                        py = pys[nh]
                        xs_ = x[0:ntok, tt, nh * 512:(nh + 1) * 512]
                        if g == 0:
                            V("scalar_tensor_tensor", [Bx[tt], Bpb[py]], [Bx[tt]], out=xs_, in0=xs_, scalar=ALPHA, in1=pb[py][0:ntok, :], op0=ALU.mult, op1=ALU.add)
                        else:
                            V("tensor_tensor", [Bx[tt], Bpb[py]], [Bx[tt]], out=xs_, in0=xs_, in1=pb[py][0:ntok, :], op=ALU.add)
                    if g == 1 and ksub >= 4:
                        post(l, lni, tt, ntok, gslot)
                        dl.pending = (tt, ntok)
                dl.flush()
                if ksub == 2:
                    return

        def attention(l, q0, nq, ktiles, mix_c0, mask_first, mask_last):
            nt = len(ktiles)
            S.tag = "attn"
            if nq == 128 and len(ktiles) >= 2 and all(kt[5] == 128 for kt in ktiles) and ktiles[-2][0] == "3" and ktiles[-1][0] == "4":
                return attention_fast(l, q0, ktiles, mix_c0, mask_first)
            offs = [i * 128 for i in range(nt)]
            for h in range(8):
                c, hb = h // 2, (h % 2) * 64
                sbank = 2 * rr("sbank", 2)
                pt = rr("PT", 2)
                for i, (kind, kf, Bk, vap, Bv, nk) in enumerate(ktiles):
                    bank = sbank + (offs[i] // 512)
                    col = offs[i] % 512
                    kap = kf(c)
                    mm(pb[bank][0:nk, col:col + nq], kap[hb:hb + 64, 0:nk], qT[hb:hb + 64, c, q0:q0 + nq], True, True, [Bk, BqT], [Bpb[bank]])
                for i, (kind, kf, Bk, vap, Bv, nk) in enumerate(ktiles):
                    bank = sbank + (offs[i] // 512)
                    col = offs[i] % 512
                    o_ = offs[i]
                    if kind == "c":
                        A(PT[pt][0:nk, o_:o_ + nq], pb[bank][0:nk, col:col + nq], AF.Exp, [Bpb[bank], Bcb], [BPT[pt]], bias=cbt[0:nk, l, h:h + 1], scale=0.125)
                        if i == 0 and mask_first:
                            V("memset", [BPT[pt]], [BPT[pt]], ap=PT[pt][0:64, o_ + 64:o_ + 128], constant=0.0)
                    else:
                        et = rr("etmp", 2)
                        eo = 0 if kind == "3" else 128
                        A(etmp[et][0:nk, 0:nq], pb[bank][0:nk, col:col + nq], AF.Exp, [Bpb[bank]], [Bet[et]], scale=0.125)
                        V("tensor_tensor", [Bet[et], Beb], [BPT[pt]], out=PT[pt][0:nk, o_:o_ + nq], in0=etmp[et][0:nk, 0:nq], in1=eb34[0:nk, h, eo:eo + nq], op=ALU.mult)
                        if kind == "4" and mask_last:
                            V("memset", [BPT[pt]], [BPT[pt]], ap=PT[pt][64:128, o_:o_ + 64], constant=0.0)
                ob = 4 + h // 4
                for i, (kind, kf, Bk, vap, Bv, nk) in enumerate(ktiles):
                    o_ = offs[i]
                    mm(pb[ob][0:nq, (h % 4) * 65:(h % 4) * 65 + 65], PT[pt][0:nk, o_:o_ + nq], vap[0:nk, h, :], i == 0, i == nt - 1, [BPT[pt], Bv], [Bpb[ob]])
            for hh in range(2):
                ob = 4 + hh
                pv3 = pb[ob][0:nq, 0:260].rearrange("p (h d) -> p h d", h=4)
                V("reciprocal", [Bpb[ob]], [Brd], out=rden[0:nq, hh * 4:hh * 4 + 4], in_=pv3[:, :, 64])
                V("tensor_tensor", [Bpb[ob], Brd], [Boa], out=oa[0:nq, hh * 256:(hh + 1) * 256].rearrange("p (h d) -> p h d", h=4), in0=pv3[:, :, 0:64],
                  in1=rden[0:nq, hh * 4:hh * 4 + 4].unsqueeze(2).to_broadcast([nq, 4, 64]), op=ALU.mult)
            for c in range(4):
                tr(pbb[6][:, c * 128:c * 128 + nq], oa[0:nq, c * 128:(c + 1) * 128], identb[0:nq, 0:nq], [Boa, Bid], [Bpb[6]])
            A(hT[:, 0:4, mix_c0:mix_c0 + nq], pbb[6][:, 0:512].rearrange("p (c n) -> p c n", c=4)[:, :, 0:nq], AF.Copy, [Bpb[6]], BhT[0:4])

        def attention_fast(l, q0, ktiles, mix_c0, mask_first):
            nq = 128
            ncst = len(ktiles) - 2
            cts, t3, t4 = ktiles[:ncst], ktiles[-2], ktiles[-1]
            order = list(cts) + [t3, t4]
            pcol = [i * 128 for i in range(ncst)] + [384, 512]
            st = {}

            def issue_S(h):
                c, hb = h // 2, (h % 2) * 64
                sbank = 2 * rr("sbank", 2)
                for i, (kind, kf, Bk, vap, Bv, nk) in enumerate(cts):
                    mm(pb[sbank][:, i * 128:(i + 1) * 128], kf(c)[hb:hb + 64, :], qT[hb:hb + 64, c, q0:q0 + nq], True, True, [Bk, BqT], [Bpb[sbank]])
                for i, (kind, kf, Bk, vap, Bv, nk) in enumerate((t3, t4)):
                    mm(pb[sbank + 1][:, i * 128:(i + 1) * 128], kf(c)[hb:hb + 64, :], qT[hb:hb + 64, c, q0:q0 + nq], True, True, [Bk, BqT], [Bpb[sbank + 1]])
                st[h] = sbank

            def issue_E(h):
                sbank = st[h]
                pt = rr("PT", 2)
                if ncst:
                    A(PT[pt][:, 0:ncst * 128], pb[sbank][:, 0:ncst * 128], AF.Exp, [Bpb[sbank], Bcb], [BPT[pt]], bias=cbt[:, l, h:h + 1], scale=0.125)
                    if mask_first:
                        V("memset", [BPT[pt]], [BPT[pt]], ap=PT[pt][0:64, 64:128], constant=0.0)
                et = rr("etmp", 2)
                A(etmp[et][:, 0:256], pb[sbank + 1][:, 0:256], AF.Exp, [Bpb[sbank + 1]], [Bet[et]], scale=0.125)
                V("tensor_tensor", [Bet[et], Beb], [BPT[pt]], out=PT[pt][:, 384:640], in0=etmp[et][:, 0:256], in1=eb34[:, h, 0:256], op=ALU.mult)
                st[h] = pt

            def issue_PV(h):
                pt = st[h]
                ob = 4 + h // 4
                for i, (kind, kf, Bk, vap, Bv, nk) in enumerate(order):
                    mm(pb[ob][:, (h % 4) * 65:(h % 4) * 65 + 65], PT[pt][:, pcol[i]:pcol[i] + nq], vap[:, h, :], i == 0, i == len(order) - 1, [BPT[pt], Bv], [Bpb[ob]])

            issue_S(0)
            for h in range(8):
                if h + 1 < 8:
                    issue_S(h + 1)
                issue_E(h)
                issue_PV(h)
            for hh in range(2):
                ob = 4 + hh
                pv3 = pb[ob][:, 0:260].rearrange("p (h d) -> p h d", h=4)
                V("reciprocal", [Bpb[ob]], [Brd], out=rden[:, hh * 4:hh * 4 + 4], in_=pv3[:, :, 64])
                V("tensor_tensor", [Bpb[ob], Brd], [Boa], out=oa[:, hh * 256:(hh + 1) * 256].rearrange("p (h d) -> p h d", h=4), in0=pv3[:, :, 0:64],
                  in1=rden[:, hh * 4:hh * 4 + 4].unsqueeze(2).to_broadcast([128, 4, 64]), op=ALU.mult)
            for c in range(4):
                tr(pbb[6][:, c * 128:(c + 1) * 128], oa[:, c * 128:(c + 1) * 128], identb[:], [Boa, Bid], [Bpb[6]])
            A(hT[:, 0:4, mix_c0:mix_c0 + nq], pbb[6][:, 0:512].rearrange("p (c n) -> p c n", c=4), AF.Copy, [Bpb[6]], BhT[0:4])

        def pool_group(l, t0, GT, first_of_seq, hist_dram, out_dram, mixc0, w3):
            S.tag = "pool"
            if hist_dram is not None:
                Dm("sp", "ptok", ptok[0:15, :], hist_dram, [], [Bptok])
                for c in range(2):
                    tr(pb[7][:, c * 16:c * 16 + 16], ptok[0:16, c * 128:(c + 1) * 128], ident[0:16, 0:16], [Bptok, Bid], [Bpb[7]])
                V("tensor_copy", [Bpb[7]], [BuT], out=uT[:, :, 1:16], in_=pb[7][:, 0:32].rearrange("p (c n) -> p c n", c=2)[:, :, 0:15])
            else:
                V("tensor_copy", [Bucar[l]], [BuT], out=uT[:, :, 0:16], in_=ucar[:, l, :, :])
            wv, Bw = w3
            for c in range(2):
                bank = rr("pg", 2)
                for kc in range(8):
                    mm(pb[bank][:, 0:GT], wv[:, kc, c * 128:(c + 1) * 128], xT[:, kc, t0:t0 + GT], kc == 0, kc == 7, [Bw] + BxT, [Bpb[bank]])
                A(uT[:, c, 16:16 + GT], pb[bank][:, 0:GT], AF.Copy, [Bpb[bank]], [BuT])
            n = 16 + GT
            V("tensor_tensor", [BuT], [BsA], out=sA[:, :, 2:n], in0=uT[:, :, 2:n], in1=uT[:, :, 1:n - 1], op=ALU.add)
            V("tensor_tensor", [BsA], [BsB], out=sBb[:, :, 4:n], in0=sA[:, :, 4:n], in1=sA[:, :, 2:n - 2], op=ALU.add)

            def dcalc(src, Bsrc, c, hf, w):
                lo, hi = hf * 64, hf * 64 + 64
                V("scalar_tensor_tensor", [Bsrc, BuT], [BdT], out=dT[lo:hi, c, 0:GT], in0=src[lo:hi, c, 16:n], scalar=1.0 / w, in1=uT[lo:hi, c, 16:n], op0=ALU.mult, op1=ALU.subtract)
                if first_of_seq:
                    V("tensor_tensor", [Bsrc, Brc], [Bptmp], out=ptmp[lo:hi, 0:16], in0=src[lo:hi, c, 16:32], in1=rc16[lo:hi, c, :], op=ALU.mult)
                    V("tensor_tensor", [Bptmp, BuT], [BdT], out=dT[lo:hi, c, 0:16], in0=ptmp[lo:hi, 0:16], in1=uT[lo:hi, c, 16:32], op=ALU.subtract)

            dcalc(sA, BsA, 0, 0, 2)
            dcalc(sBb, BsB, 0, 1, 4)
            V("tensor_tensor", [BsB], [BsA], out=sA[:, :, 8:n], in0=sBb[:, :, 8:n], in1=sBb[:, :, 4:n - 4], op=ALU.add)
            dcalc(sA, BsA, 1, 0, 8)
            V("tensor_tensor", [BsA], [BsB], out=sBb[:, :, 16:n], in0=sA[:, :, 16:n], in1=sA[:, :, 8:n - 8], op=ALU.add)
            dcalc(sBb, BsB, 1, 1, 16)
            for c in range(2):
                bank = rr("pg", 2)
                mm(pb[bank][:, 0:GT], wpool[:, l, c, :], dT[:, c, 0:GT], True, True, [Bwp, BdT], [Bpb[bank]])
                A(hT[:, 4 + c, mixc0:mixc0 + GT], pb[bank][:, 0:GT], AF.Copy, [Bpb[bank], Bps], [BhT[4 + c]], scale=pscale[:, l, c:c + 1])
            if out_dram is not None:
                for c in range(2):
                    tr(pb[7][0:16, 128 + c * 128:256 + c * 128], uT[:, c, n - 16:n], ident[:], [BuT, Bid], [Bpb[7]])
                V("tensor_copy", [Bpb[7]], [Bpst], out=pst[0:16, :], in_=pb[7][0:16, 128:384])
                Dm("sp", "pst", out_dram, pst[1:16, :], [Bpst], [outbuf("o")])
            if hist_dram is None:
                V("tensor_copy", [BuT], [Bucar[l]], out=ucar[:, l, :, :], in_=uT[:, :, GT:GT + 16])

        def hgrn_group(l, t0, GT, L, w3, w4, w5, state_in, state_out, mixc0, chain):
            nblk = GT // L
            S.tag = "hgrn.proj"
            (wv3, Bw3), (wv4, Bw4), (wv5, Bw5) = w3, w4, w5
            rmask = resetm[:, 0, 0:GT] if L == 64 else resetm[:, 1, 0:GT]
            for h in range(4):
                bank = rr("pg", 2)
                for kc in range(8):
                    mm(pb[bank][0:64, 0:GT], wv3[:, kc, 256 + h * 64:256 + h * 64 + 64], xT[:, kc, t0:t0 + GT], kc == 0, kc == 7, [Bw3] + BxT, [Bpb[bank]])
                A(QS[:, h, 0:GT], pb[bank][0:64, 0:GT], AF.Silu, [Bpb[bank]], [BQS])
            for h in range(4):
                bank = rr("pg", 2)
                for kc in range(8):
                    mm(pb[bank][0:64, 0:GT], wv4[:, kc, h * 64:h * 64 + 64], xT[:, kc, t0:t0 + GT], kc == 0, kc == 7, [Bw4] + BxT, [Bpb[bank]])
                A(Fb[:, h, 0:GT], pb[bank][0:64, 0:GT], AF.Sigmoid, [Bpb[bank]], [BF_])
            for blk in range(nblk):
                tb = t0 + blk * L
                bank = 2 + rr("pu", 2)
                for kc in range(8):
                    mm(pb[bank][0:L, 0:256], xT[:, kc, tb:tb + L], wv4[:, kc, 256:512], kc == 0, kc == 7, [Bw4] + BxT, [Bpb[bank]])
                for kc in range(8):
                    mm(pb[bank][0:L, 256:512], xT[:, kc, tb:tb + L], wv5[:, kc, 0:256], kc == 0, kc == 7, [Bw5] + BxT, [Bpb[bank]])
                V("tensor_copy", [Bpb[bank]], [BVT[blk]], out=VTOK[blk][0:L, :], in_=pb[bank][0:L, 0:256])
                A(G2[blk][0:L, :], pb[bank][0:L, 256:512], AF.Silu, [Bpb[bank]], [BG2[blk]])
                V("tensor_tensor", [BG2[blk], Bnw], [BG2[blk]], out=G2[blk][0:L, :].rearrange("p (h d) -> p h d", h=4), in0=G2[blk][0:L, :].rearrange("p (h d) -> p h d", h=4), in1=normwb[0:L, l, :].unsqueeze(1).to_broadcast([L, 4, 64]), op=ALU.mult)
            S.tag = "hgrn.gate"
            for h in range(4):
                V("tensor_scalar", [BF_, Blb], [BF_], out=Fb[:, h, 0:GT], in0=Fb[:, h, 0:GT], scalar1=omlt[:, l, h:h + 1], scalar2=lbt[:, l, h:h + 1], op0=ALU.mult, op1=ALU.add)
            V("tensor_scalar", [BF_], [BKIN], out=KIN[:, :, 0:GT], in0=Fb[:, :, 0:GT], scalar1=-1.0, scalar2=1.0, op0=ALU.mult, op1=ALU.add)
            A(Fb[:, :, 0:GT], Fb[:, :, 0:GT], AF.Ln, [BF_], [BF_])
            for h in range(4):
                V("tensor_tensor_scan", [BF_, Bresetm], [BBc], out=Bc[:, h, 0:GT], data0=rmask, data1=Fb[:, h, 0:GT], initial=0.0, op0=ALU.mult, op1=ALU.add)
            mid = L // 2 - 1
            Bc4 = Bc[:, :, 0:GT].rearrange("p h (b t) -> p h b t", t=L)
            for h in range(4):
                V("tensor_tensor", [BBc, BF_], [BF_], out=Fb[:, h, 0:GT].rearrange("p (b t) -> p b t", t=L), in0=Bc4[:, h, :, :],
                  in1=Bc4[:, h, :, mid:mid + 1].to_broadcast([64, nblk, L]), op=ALU.subtract)
            A(EQ[:, :, 0:GT], Fb[:, :, 0:GT], AF.Exp, [BF_], [BEQ])
            A(EK[:, :, 0:GT], Fb[:, :, 0:GT], AF.Exp, [BF_], [BEK], scale=-1.0)
            A(EBM[:, :, 0:nblk], Bc4[:, :, :, mid], AF.Exp, [BBc], [BEBM])
            A(EBL[:, :, 0:nblk], Bc4[:, :, :, L - 1], AF.Exp, [BBc], [BEBL])
            V("tensor_tensor", [BQS, BEQ], [BQT], out=QTt[:, :, 0:GT], in0=QS[:, :, 0:GT], in1=EQ[:, :, 0:GT], op=ALU.mult)
            V("tensor_tensor", [BKIN, BEK], [BKT], out=KTt[:, :, 0:GT], in0=KIN[:, :, 0:GT], in1=EK[:, :, 0:GT], op=ALU.mult)
            EQ4 = EQ[:, :, 0:GT].rearrange("p h (b t) -> p h b t", t=L)
            S.tag = "hgrn.blk"
            for blk in range(nblk):
                c0 = blk * L
                if not chain:
                    Dm("sp", "hst", Sst[:, l, :, :], state_in[blk].rearrange("h c v -> c h v"), [], [BSst[l]])
                for h in range(4):
                    tr(pbb[7][0:L, h * 64:(h + 1) * 64], KTt[:, h, c0:c0 + L], identb[0:64, 0:64], [BKT, Bid], [Bpb[7]])
                A(KTOK[0:L, :, :], pbb[7][0:L, 0:256].rearrange("p (h c) -> p h c", h=4), AF.Copy, [Bpb[7]], [BKTOK])
                for h in range(4):
                    mm(pb[6][0:L, h * 64:h * 64 + L], KTt[:, h, c0:c0 + L], QTt[:, h, c0:c0 + L], True, True, [BKT, BQT], [Bpb[6]])
                V("tensor_tensor", [Bpb[6], Btri], [BSM], out=SM[0:L, :, 0:L], in0=pb[6][0:L, 0:256].rearrange("p (h t) -> p h t", h=4)[:, :, 0:L],
                  in1=tri[0:L, 0:L].unsqueeze(1).to_broadcast([L, 4, L]), op=ALU.mult)
                V("tensor_tensor", [BSst[l], BEBM], [BSbf], out=Sbf[:], in0=Sst[:, l, :, :], in1=EBM[:, :, blk:blk + 1].to_broadcast([64, 4, 64]), op=ALU.mult)
                ob = 4 + rr("py", 2)
                for h in range(4):
                    mm(pb[ob][0:L, h * 64:(h + 1) * 64], SM[0:L, h, 0:L], VTOK[blk][0:L, h * 64:(h + 1) * 64], True, False, [BSM, BVT[blk]], [Bpb[ob]])
                    mm(pb[ob][0:L, h * 64:(h + 1) * 64], QTt[:, h, c0:c0 + L], Sbf[:, h, :], False, True, [BQT, BSbf], [Bpb[ob]])
                mb = 4 + rr("py", 2)
                for h in range(4):
                    mm(pb[mb][0:64, h * 64:(h + 1) * 64], KTOK[0:L, h, :], VTOK[blk][0:L, h * 64:(h + 1) * 64], True, True, [BKTOK, BVT[blk]], [Bpb[mb]])
                V("tensor_tensor", [Bpb[mb], BEQ], [BT1], out=T1[:], in0=pb[mb][0:64, 0:256].rearrange("p (h v) -> p h v", h=4),
                  in1=EQ4[:, :, blk, L - 1:L].to_broadcast([64, 4, 64]), op=ALU.mult)
                V("tensor_tensor", [BSst[l], BEBL], [BSst[l]], out=Sst[:, l, :, :], in0=Sst[:, l, :, :], in1=EBL[:, :, blk:blk + 1].to_broadcast([64, 4, 64]), op=ALU.mult)
                V("tensor_tensor", [BSst[l], BT1], [BSst[l]], out=Sst[:, l, :, :], in0=Sst[:, l, :, :], in1=T1[:], op=ALU.add)
                if state_out is not None and (not chain or blk == nblk - 1):
                    so = state_out[blk] if not chain else state_out
                    V("tensor_copy", [BSst[l]], [Bhst], out=hst[:].rearrange("p (h v) -> p h v", h=4), in_=Sst[:, l, :, :])
                    Dm("sp", "hso", so.rearrange("h c v -> c h v"), hst[:].rearrange("p (h v) -> p h v", h=4), [Bhst], [outbuf("o")])
                A(osq[0:L, :], pb[ob][0:L, 0:256], AF.Square, [Bpb[ob]], [Bosq])
                V("tensor_reduce", [Bosq], [Boss], out=oss[0:L, 0:4], in_=osq[0:L, :].rearrange("p (h v) -> p h v", h=4), axis=AX.X, op=ALU.add)
                V("tensor_scalar", [Boss], [Boss], out=oss[0:L, 0:4], in0=oss[0:L, 0:4], scalar1=1.0 / 64, scalar2=RMS_EPS, op0=ALU.mult, op1=ALU.add)
                A(oss[0:L, 0:4], oss[0:L, 0:4], AF.Sqrt, [Boss], [Boss])
                V("reciprocal", [Boss], [Boss], out=oss[0:L, 4:8], in_=oss[0:L, 0:4])
                V("tensor_tensor", [Bpb[ob], Boss, Bosq], [Bosq], out=osq[0:L, :].rearrange("p (h v) -> p h v", h=4), in0=pb[ob][0:L, 0:256].rearrange("p (h v) -> p h v", h=4),
                  in1=oss[0:L, 4:8].unsqueeze(2).to_broadcast([L, 4, 64]), op=ALU.mult)
                V("tensor_tensor", [Bosq, BG2[blk]], [Boc], out=octok[0:L, :], in0=osq[0:L, :], in1=G2[blk][0:L, :], op=ALU.mult)
                for c in range(2):
                    tr(pbb[7][:, 512 + c * 64:512 + c * 64 + L], octok[0:L, c * 128:(c + 1) * 128], identb[0:L, 0:L], [Boc, Bid], [Bpb[7]])
                A(hT[:, 6:8, mixc0 + c0:mixc0 + c0 + L], pbb[7][:, 512:640].rearrange("p (c n) -> p c n", c=2)[:, :, 0:L], AF.Copy, [Bpb[7]], BhT[6:8])

        def mixer(l, T, tiles, nseq, first_tile, last_tile, is_prompt):
            import os
            S.tag = "qkv"
            gslot = load_gb(l, 1)
            Dm("sp", "eb", eb34[:], Wd["rb34"][l].rearrange("h r n -> r h n"), [], [Beb])
            A(eb34[:], eb34[:], AF.Exp, [Beb], [Beb])
            V("memset", [Beb], [Beb], ap=eb34[64:128, :, 128:192], constant=0.0)
            wq, Bq_ = wnext("k", "win", l, 0, 512)
            for c in range(4):
                bank = rr("pg", 2)
                for kc in range(8):
                    mm(pb[bank][:, 0:T], wq[:, kc, c * 128:(c + 1) * 128], xT[:, kc, 0:T], kc == 0, kc == 7, [Bq_] + BxT, [Bpb[bank]])
                A(qT[:, c, 0:T], pb[bank][:, 0:T], AF.Copy, [Bpb[bank]], [BqT])
            wk, Bk_ = wnext("k", "win", l, 512, 512)
            for c in range(4):
                bank = rr("pg", 2)
                for kc in range(8):
                    mm(pb[bank][:, 0:T], wk[:, kc, c * 128:(c + 1) * 128], xT[:, kc, 0:T], kc == 0, kc == 7, [Bk_] + BxT, [Bpb[bank]])
                V("tensor_copy", [Bpb[bank]], [Bkcur], out=kcur[:, c, 0:T], in_=pb[bank][:, 0:T])
            omask = int(os.environ.get("KOUTMASK", "7"))
            kout = (last_tile or not is_prompt) and bool(omask & 1)
            kx = int(os.environ.get("KX", "0"))
            if kout and not (kx & 2):
                for (tt, ntok) in tiles:
                    bank = 2 + rr("pu", 2)
                    for kc in range(8):
                        mm(pb[bank][0:ntok, :], xT[:, kc, tt * 128:tt * 128 + ntok], wk[:, kc, :], kc == 0, kc == 7, [Bk_] + BxT, [Bpb[bank]])
                    ks = rr("kvst", 2)
                    A(kvst[ks][0:ntok, :], pb[bank][0:ntok, :], AF.Copy, [Bpb[bank]], [Bkvst[ks]])
                    dst = (pk[l][tt * 128:tt * 128 + ntok, :] if is_prompt else sk[l][tt * 128:tt * 128 + ntok, :])
                    if not (kx & 1):
                        Dm("sp", f"kvo{ks}", dst, kvst[ks][0:ntok, :], [Bkvst[ks]], [outbuf("o")])
            wvv, Bv_ = wnext("k", "win", l, 1024, 512)
            vtiles = [(tt, tt * 128, ntok) for (tt, ntok) in tiles] if is_prompt else [(si, si * 32, 32) for si in range(nseq)]
            for (vi, tk0, ntok) in vtiles:
                bank = 2 + rr("pu", 2)
                for kc in range(8):
                    mm(pb[bank][0:ntok, :], xT[:, kc, tk0:tk0 + ntok], wvv[:, kc, :], kc == 0, kc == 7, [Bv_] + BxT, [Bpb[bank]])
                V("tensor_copy", [Bpb[bank]], [Bvcur[vi]], out=vcur[0:ntok, vi, :, 0:64], in_=pb[bank][0:ntok, :].rearrange("p (h d) -> p h d", h=8))
                if kout:
                    ks = rr("kvst", 2)
                    A(kvst[ks][0:ntok, :], pb[bank][0:ntok, :], AF.Copy, [Bpb[bank], Bvcur[vi]], [Bkvst[ks]])
                    dst = (pv[l][tk0:tk0 + ntok, :] if is_prompt else sv[l][tk0:tk0 + ntok, :])
                    if not (kx & 1):
                        Dm("sp", f"kvo{ks}", dst, kvst[ks][0:ntok, :], [Bkvst[ks]], [outbuf("o")])
            if is_prompt:
                for p in range(4):
                    kts = []
                    for i in range(5):
                        g = p - 4 + i
                        kind = "c" if i < 3 else ("3" if i == 3 else "4")
                        if g < 0:
                            if first_tile:
                                continue
                            ci = 4 + g
                            kts.append((kind, (lambda c, ci=ci: kc_l[l][:, c, ci * 128:(ci + 1) * 128]), Bkc[l][ci], vc_l[l][:, ci, :, :], Bvc[l][ci], 128))
                        else:
                            kts.append((kind, (lambda c, g=g: kcur[:, c, g * 128:(g + 1) * 128]), Bkcur, vcur[:, g, :, :], Bvcur[g], 128))
                    attention(l, p * 128, 128, kts, p * 128, mask_first=(len(kts) == 5), mask_last=True)
            else:
                for si in range(nseq):
                    for ci in range(4):
                        Dm("pool", "ckst", ckst[:], ck[l, si, ci * 128:(ci + 1) * 128, :], [], [Bckst])
                        for c in range(4):
                            tr(pbb[6][:, c * 128:(c + 1) * 128], ckst[:, c * 128:(c + 1) * 128], identb[:], [Bckst, Bid], [Bpb[6]])
                        A(kc_l[l][:, :, ci * 128:(ci + 1) * 128], pbb[6][:, 0:512].rearrange("p (c n) -> p c n", c=4), AF.Copy, [Bpb[6]], [Bkc[l][ci]])
                        Dm("pool", f"vc{ci}", vc_l[l][:, ci, :, 0:64], cv[l, si, ci * 128:(ci + 1) * 128, :].rearrange("p (h d) -> p h d", h=8), [], [Bvc[l][ci]])
                    kts = []
                    for i in range(4):
                        kind = "c" if i < 3 else "3"
                        kts.append((kind, (lambda c, i=i: kc_l[l][:, c, i * 128:(i + 1) * 128]), Bkc[l][i], vc_l[l][:, i, :, :], Bvc[l][i], 128))
                    kts.append(("4", (lambda c, si=si: kcur[:, c, si * 32:si * 32 + 32]), Bkcur, vcur[0:32, si, :, :], Bvcur[si], 32))
                    attention(l, si * 32, 32, kts, si * 32, mask_first=False, mask_last=False)
            import os
            kmix = int(os.environ.get("KMIX", "9"))
            if kmix == 2:
                return
            S.tag = "carry"
            if is_prompt and not last_tile:
                A(kc_l[l][:], kcur[:], AF.Copy, [Bkcur], Bkc[l])
                V("tensor_copy", Bvcur, Bvc[l], out=vc_l[l][:, :, :, 0:64], in_=vcur[:, :, :, 0:64])
            w3 = wnext("k", "win", l, 1536, 512)
            w4 = wnext("k", "win", l, 2048, 512)
            w5 = wnext("k", "win", l, 2560, 256)
            if is_prompt:
                if first_tile:
                    V("memset", [], [Bucar[l]], ap=ucar[:, l, :, :], constant=0.0)
                    V("memset", [], [BSst[l]], ap=Sst[:, l, :, :], constant=0.0)
                for g in range(4):
                    lastg = last_tile and g == 3
                    pool_group(l, g * 128, 128, first_tile and g == 0, None, ppool[l] if (lastg and omask & 2) else None, g * 128, w3)
                    hgrn_group(l, g * 128, 128, 64, w3, w4, w5, None, phg[l] if (lastg and omask & 4) else None, g * 128, True)
            else:
                for si in range(nseq):
                    pool_group(l, si * 32, 32, False, spool[l, si], spool_o[l, si], si * 32, w3)
                hgrn_group(l, 0, 32 * nseq, 32, w3, w4, w5, [shg[l, si] for si in range(nseq)], [shg_o[l, si] for si in range(nseq)], 0, False)
            S.tag = "wout"
            wo = [wnext("k", "wout", l, 0, 512), wnext("k", "wout", l, 512, 512)]
            dl = Delayed()
            for (tt, ntok) in tiles:
                S.tag = "wout"
                pys = []
                for nh in range(2):
                    py = 4 + rr("py", 2)
                    pys.append(py)
                    wv_, Bw_ = wo[nh]
                    for kc in range(8):
                        mm(pb[py][0:ntok, :], hT[:, kc, tt * 128:tt * 128 + ntok], wv_[:, kc, :], kc == 0, kc == 7, [BhT[kc], Bw_], [Bpb[py]])
                dl.flush()
                for nh in range(2):
                    xs_ = x[0:ntok, tt, nh * 512:(nh + 1) * 512]
                    V("scalar_tensor_tensor", [Bx[tt], Bpb[pys[nh]]], [Bx[tt]], out=xs_, in0=xs_, scalar=ALPHA, in1=pb[pys[nh]][0:ntok, :], op0=ALU.mult, op1=ALU.add)
                post(l, 1, tt, ntok, gslot)
                dl.pending = (tt, ntok)
            dl.flush()

        def ple(l, T, tiles, prow0, is_prompt):
            S.tag = "ple"
            gslot = load_gb(l, 3)
            psrc = pp if is_prompt else ps_
            for (tt, ntok) in tiles:
                Dm("sp", "ptokp", ptokp[0:ntok, 0, :], psrc[l, prow0 + tt * 128:prow0 + tt * 128 + ntok, :], [], [Bptokp])
                for c in range(2):
                    tr(pb[7][:, c * 128:c * 128 + ntok], ptokp[0:ntok, 0, c * 128:(c + 1) * 128], ident[0:ntok, 0:ntok], [Bptokp, Bid], [Bpb[7]])
                V("tensor_copy", [Bpb[7]], [BpT], out=pTt[:, :, tt * 128:tt * 128 + ntok], in_=pb[7][:, 0:256].rearrange("p (c n) -> p c n", c=2)[:, :, 0:ntok])
            wg = [wnext("k", "pleg", l, 0, 512), wnext("k", "pleg", l, 512, 512)]
            wp_, Bwp_ = wnext("j", "plep", l, 0, 2)
            dl = Delayed()
            for (tt, ntok) in tiles:
                S.tag = "ple"
                banks = []
                for nh in range(2):
                    pg = rr("pg", 2); pu = 2 + rr("pu", 2)
                    banks.append((pg, pu))
                    wv_, Bw_ = wg[nh]
                    for kc in range(8):
                        mm(pb[pg][0:ntok, :], xT[:, kc, tt * 128:tt * 128 + ntok], wv_[:, kc, :], kc == 0, kc == 7, [BxT[tt], Bw_], [Bpb[pg]])
                    for c in range(2):
                        mm(pb[pu][0:ntok, :], pTt[:, c, tt * 128:tt * 128 + ntok], wp_[:, c, nh * 512:(nh + 1) * 512], c == 0, c == 1, [BpT, Bwp_], [Bpb[pu]])
                dl.flush()
                for nh in range(2):
                    pg, pu = banks[nh]
                    ei = rr("etile", 2)
                    A(etile[ei][0:ntok, :], pb[pg][0:ntok, :], AF.Sigmoid, [Bpb[pg]], [Bet2[ei]])
                    V("tensor_tensor", [Bet2[ei], Bpb[pu]], [Bet2[ei]], out=etile[ei][0:ntok, :], in0=etile[ei][0:ntok, :], in1=pb[pu][0:ntok, :], op=ALU.mult)
                    xs_ = x[0:ntok, tt, nh * 512:(nh + 1) * 512]
                    V("scalar_tensor_tensor", [Bx[tt], Bet2[ei]], [Bx[tt]], out=xs_, in0=xs_, scalar=ALPHA, in1=etile[ei][0:ntok, :], op0=ALU.mult, op1=ALU.add)
                post(l, 3, tt, ntok, gslot)
                dl.pending = (tt, ntok)
            dl.flush()

        def run_tile(xsrc, ydst, row0, T, tiles, is_prompt, first_tile, last_tile, nseq):
            for (tt, ntok) in tiles:
                Dm("sp", f"xin{tt}", x[0:ntok, tt, :], xsrc[row0 + tt * 128:row0 + tt * 128 + ntok, :], [], [Bx[tt]])
                transposes(tt, ntok)
            import os
            stage = int(os.environ.get("KSTAGE", "0"))
            for l in range(depth):
                if stage == -1: break
                ffn(l, "f1", 0, T, tiles)
                if stage == 1: break
                mixer(l, T, tiles, nseq, first_tile, last_tile, is_prompt)
                if stage == 2: break
                ffn(l, "f2", 2, T, tiles)
                if stage == 3: break
                ple(l, T, tiles, row0, is_prompt)
                if stage == 4: break
            for (tt, ntok) in tiles:
                Dm("sp", f"xout{tt}", ydst[row0 + tt * 128:row0 + tt * 128 + ntok, :], x[0:ntok, tt, :], [Bx[tt]], [outbuf("o")])

        import os
        stage = int(os.environ.get("KSTAGE", "0"))
        noout = bool(os.environ.get("KNOOUT"))
        for ti in range(self.n_ptiles):
            run_tile(xp, yp, ti * 512, 512, [(t, 128) for t in range(4)], True, ti == 0, (ti == self.n_ptiles - 1) and not noout, 0)
            if stage: break
        if self.n_samp and RUN_SAMPLE and not os.environ.get("KNOSAMP") and (not stage or os.environ.get("KSAMP")):
            if stage:
                consumed[0] = 0; issued[0] = 0
            run_tile(xs, ys, 0, NS, [(0, NS)], False, False, False, nsm)
        assert stage or (not RUN_SAMPLE) or os.environ.get("KNOSAMP") or consumed[0] == len(plan), (consumed[0], len(plan))
        self.nsem = S.emit(nc, outbufs)
        self.st.close()
        return nc


def _rel_tables(attn_rel_bias):
    depth = attn_rel_bias.shape[0]
    r = np.arange(128)[:, None]
    j = np.arange(128)[None, :]
    d3 = np.clip(128 - r + j, -63, 128) + 63
    d4 = np.clip(j - r, -63, 128) + 63
    rb34 = np.concatenate([attn_rel_bias[:, :, d3], attn_rel_bias[:, :, d4]], axis=-1)
    rbc = attn_rel_bias[:, :, 191]
    return np.ascontiguousarray(rb34, dtype=np.float32), np.ascontiguousarray(rbc, dtype=np.float32)


_CACHE = {}


def run(inputs, depth, n_ptiles, n_samp_per_core, n_cores, prompt_of_core, samp_of_core):
    key = (depth, n_ptiles, n_samp_per_core)
    if key not in _CACHE:
        _CACHE[key] = Prog(depth, n_ptiles, n_samp_per_core).build()
    nc = _CACHE[key]
    f = lambda a: np.ascontiguousarray(np.asarray(a), dtype=np.float32)
    rb34, rbc = _rel_tables(f(inputs["attn_rel_bias"]))
    common = {
        "f1gu": f(inputs["ffn1_w_gu"]), "f1d": f(inputs["ffn1_w_down"]), "win": f(inputs["w_in"]), "wout": f(inputs["w_out"]),
        "f2gu": f(inputs["ffn2_w_gu"]), "f2d": f(inputs["ffn2_w_down"]), "pleg": f(inputs["ple_w_gate"]), "plep": f(inputs["ple_w_proj"]),
        "lng": f(inputs["ln_g"]), "lnb": f(inputs["ln_b"]), "poolw": f(inputs["pool_w"]), "pscale": f(inputs["pool_scale"]),
        "lbraw": f(inputs["hgrn_lower_bounds"]), "normw": f(inputs["hgrn_norm_w"]), "rb34": rb34, "rbc": rbc,
    }
    xp, pp_ = f(inputs["x_prompt"]), f(inputs["p_prompt"])
    xs, ps_ = f(inputs["x_sample"]), f(inputs["p_sample"])
    ck, cv = f(inputs["cache_attn_k"]), f(inputs["cache_attn_v"])
    sp_, sh = f(inputs["state_pool"]), f(inputs["state_hgrn"])
    in_maps = []
    for c in range(n_cores):
        b = prompt_of_core[c]
        sb = samp_of_core[c]
        m = dict(common)
        m["xp"] = xp[b]; m["pp"] = np.ascontiguousarray(pp_[:, b])
        m["xs"] = np.ascontiguousarray(xs[sb].reshape(-1, D)); m["ps"] = np.ascontiguousarray(ps_[:, sb].reshape(depth, -1, 256))
        m["ck"] = np.ascontiguousarray(ck[:, sb].reshape(depth, len(sb), 512, 512)); m["cv"] = np.ascontiguousarray(cv[:, sb].reshape(depth, len(sb), 512, 512))
        m["spool"] = np.ascontiguousarray(sp_[:, sb]); m["shg"] = np.ascontiguousarray(sh[:, sb])
        in_maps.append(m)
    res = run_bass_kernel_spmd(nc, in_maps, core_ids=list(range(n_cores)))
    return res.results


def kernel(**inputs):
    depth = 4
    n_cores = 8
    prompt_of_core = [c % 4 for c in range(n_cores)]
    samp_of_core = [list(range(4 * c, 4 * c + 4)) for c in range(n_cores)]
    r = run(inputs, depth, 16, 4, n_cores, prompt_of_core, samp_of_core)
    B, SEQ = 4, 8192
    y_prompt = np.stack([r[b]["yp"] for b in range(B)]).astype(np.float32)
    y_sample = np.concatenate([r[c]["ys"].reshape(4, 32, D) for c in range(n_cores)]).astype(np.float32)
    pk = np.stack([r[b]["pk"].reshape(depth, 512, 512) for b in range(B)], axis=1).reshape(depth, B, 512, 8, 64).astype(np.float32)
    pv = np.stack([r[b]["pv"].reshape(depth, 512, 512) for b in range(B)], axis=1).reshape(depth, B, 512, 8, 64).astype(np.float32)
    ppool = np.stack([r[b]["ppool"].reshape(depth, 15, 256) for b in range(B)], axis=1).astype(np.float32)
    phg = np.stack([r[b]["phg"].reshape(depth, 4, 64, 64) for b in range(B)], axis=1).astype(np.float32)
    sk = np.concatenate([r[c]["sk"].reshape(depth, 4, 32, 8, 64) for c in range(n_cores)], axis=1).astype(np.float32)
    sv = np.concatenate([r[c]["sv"].reshape(depth, 4, 32, 8, 64) for c in range(n_cores)], axis=1).astype(np.float32)
    spool = np.concatenate([r[c]["spool_o"].reshape(depth, 4, 15, 256) for c in range(n_cores)], axis=1).astype(np.float32)
    shg = np.concatenate([r[c]["shg_o"].reshape(depth, 4, 4, 64, 64) for c in range(n_cores)], axis=1).astype(np.float32)
    return (y_prompt, y_sample, pk, pv, ppool, phg, sk, sv, spool, shg)
```

```python
import contextlib
import numpy as np
import concourse.bass as bass
import concourse.mybir as mybir
from concourse.bass_utils import run_bass_kernel_spmd

F32 = mybir.dt.float32
BF16 = mybir.dt.bfloat16
AF = mybir.ActivationFunctionType
ALU = mybir.AluOpType
AX = mybir.AxisListType

D = 1024
HID = 2816
INW = 2816
ALPHA = float(8 ** 0.25)
LN_EPS = 1e-5
RMS_EPS = 1e-6
NSLOT = 6
RUN_SAMPLE = True
STREAMS = ("pe", "act", "dve", "pool", "sp")


class Buf:
    __slots__ = ("name", "lw", "rd", "excl")

    def __init__(self, name, excl=False):
        self.name = name
        self.lw = None
        self.rd = []
        self.excl = excl


class Op:
    __slots__ = ("stream", "fn", "waits", "sig", "comp", "cidx", "tag")

    def __init__(self, stream, fn, comp, cidx):
        self.stream, self.fn, self.comp, self.cidx = stream, fn, comp, cidx
        self.tag = None
        self.waits = []
        self.sig = False


class Sched:
    def __init__(self):
        self.ops = {s: [] for s in STREAMS}
        self.comp_ops = {}
        self.seen = {s: {} for s in STREAMS}
        self.nops = 0
        self.tag = "init"
        self.annotate = False

    def _add(self, stream, fn, reads, writes, dkey=None):
        comp = stream if dkey is None else "d:" + dkey
        lst = self.comp_ops.setdefault(comp, [])
        op = Op(stream, fn, comp, len(lst))
        op.tag = self.tag
        if dkey is not None:
            op.sig = True
        lst.append(op)
        me = (comp, op.cidx)
        deps = {}

        def need(d):
            if d is None:
                return
            e, i = d
            if e == "pe" and comp == "pe":
                return
            if deps.get(e, -1) < i:
                deps[e] = i

        for b in reads:
            need(b.lw)
            if b.excl:
                for r in b.rd:
                    if r[0] != comp:
                        need(r)
        for b in writes:
            need(b.lw)
            for r in b.rd:
                need(r)
        if dkey is not None and op.cidx > 0:
            need((comp, op.cidx - 1))
        seen = self.seen[stream]
        for e, i in deps.items():
            if seen.get(e, -1) >= i:
                continue
            seen[e] = i
            op.waits.append((e, i))
            self.comp_ops[e][i].sig = True
        for b in reads:
            if len(b.rd) > 64:
                last = {}
                for r in b.rd:
                    if last.get(r[0], -1) < r[1]:
                        last[r[0]] = r[1]
                b.rd = list(last.items())
            b.rd.append(me)
        for b in writes:
            b.lw = me
            b.rd = []
        self.ops[stream].append(op)
        self.nops += 1
        return op

    def mm(self, out, lhsT, rhs, start, stop, reads, writes):
        return self._add("pe", ("matmul", dict(out=out, lhsT=lhsT, rhs=rhs, start=start, stop=stop)), reads, writes)

    def tr(self, out, in_, identity, reads, writes):
        return self._add("pe", ("transpose", dict(out=out, in_=in_, identity=identity)), reads, writes)

    def A(self, out, in_, func, reads, writes, **kw):
        return self._add("act", ("activation", dict(out=out, in_=in_, func=func, **kw)), reads, writes)

    def V(self, name, reads, writes, **kw):
        return self._add("dve", (name, kw), reads, writes)

    def G(self, name, reads, writes, **kw):
        return self._add("pool", (name, kw), reads, writes)

    def D(self, stream, dkey, out, in_, reads, writes, slow=False):
        kw = dict(out=out, in_=in_)
        if slow:
            kw["allow_slow_non_contiguous"] = True
        return self._add(stream, ("dma_start", kw), reads, writes, dkey=dkey)

    def emit(self, nc, final_bufs):
        self._add("sp", None, final_bufs, [])
        rank = {}
        for comp, lst in self.comp_ops.items():
            c = 0
            for op in lst:
                if op.sig:
                    c += 1
                    rank[(comp, op.cidx)] = c
        with contextlib.ExitStack() as st:
            sems = {}
            for comp in self.comp_ops:
                if any(op.sig for op in self.comp_ops[comp]):
                    sems[comp] = st.enter_context(nc.semaphore("s_" + comp.replace(":", "_")))
            block = st.enter_context(nc.Block())

            def run(stream):
                def body(eng):
                    for op in self.ops[stream]:
                        for (e, i) in op.waits:
                            eng.wait_ge(sems[e], rank[(e, i)] * (16 if e.startswith("d:") else 1))
                        if op.fn is None:
                            continue
                        ins = getattr(eng, op.fn[0])(**op.fn[1])
                        if self.annotate:
                            ins.annotate(op.tag)
                        if op.sig:
                            ins.then_inc(sems[op.comp], 16 if op.comp.startswith("d:") else 1)
                return body

            block.tensor(run("pe"))
            block.scalar(run("act"))
            block.vector(run("dve"))
            block.gpsimd(run("pool"))
            block.sync(run("sp"))
        return len(sems)


class Prog:
    def __init__(self, depth, n_ptiles, n_samp):
        self.depth, self.n_ptiles, self.n_samp = depth, n_ptiles, n_samp
        self.nc = bass.Bass("TRN2", target_bir_lowering=False)
        self.S = Sched()
        self.st = contextlib.ExitStack()
        self.rr = {}

    def dram(self, name, shape, kind):
        return self.nc.dram_tensor(name, list(shape), F32, kind=kind).ap()

    def sb(self, name, shape, dt=F32):
        return self.st.enter_context(self.nc.sbuf_tensor("sb_" + name, list(shape), dt))

    def build(self):
        nc, S, depth = self.nc, self.S, self.depth
        NP = self.n_ptiles * 512
        NS = self.n_samp * 32
        nsm = self.n_samp
        I, O = "ExternalInput", "ExternalOutput"
        dr = self.dram
        xp = dr("xp", [max(NP, 1), D], I); pp = dr("pp", [depth, max(NP, 1), 256], I)
        xs = dr("xs", [NS, D], I); ps_ = dr("ps", [depth, NS, 256], I)
        ck = dr("ck", [depth, nsm, 512, 512], I); cv = dr("cv", [depth, nsm, 512, 512], I)
        spool = dr("spool", [depth, nsm, 15, 256], I); shg = dr("shg", [depth, nsm, 4, 64, 64], I)
        Wd = {}
        for nm, shp in [("f1gu", [depth, D, 2 * HID]), ("f1d", [depth, HID, D]), ("win", [depth, D, INW]),
                        ("wout", [depth, D, D]), ("f2gu", [depth, D, 2 * HID]), ("f2d", [depth, HID, D]),
                        ("pleg", [depth, D, D]), ("plep", [depth, 256, D]), ("lng", [depth, 4, D]),
                        ("lnb", [depth, 4, D]), ("poolw", [depth, 4, 64, 64]), ("pscale", [depth, 256]),
                        ("lbraw", [depth, 256]), ("normw", [depth, 64]), ("rb34", [depth, 8, 128, 256]),
                        ("rbc", [depth, 8])]:
            Wd[nm] = dr(nm, shp, I)
        yp = dr("yp", [max(NP, 1), D], O); ys = dr("ys", [NS, D], O)
        pk2 = dr("pk", [depth * 512, 512], O); pv2 = dr("pv", [depth * 512, 512], O)
        ppool2 = dr("ppool", [depth * 15, 256], O); phg2 = dr("phg", [depth * 256, 64], O)
        sk2 = dr("sk", [depth * NS, 512], O); sv2 = dr("sv", [depth * NS, 512], O)
        spool2 = dr("spool_o", [depth * nsm * 15, 256], O); shg2 = dr("shg_o", [depth * nsm * 256, 64], O)
        pk = [pk2[l * 512:(l + 1) * 512, :] for l in range(depth)]; pv = [pv2[l * 512:(l + 1) * 512, :] for l in range(depth)]
        sk = [sk2[l * NS:(l + 1) * NS, :] for l in range(depth)]; sv = [sv2[l * NS:(l + 1) * NS, :] for l in range(depth)]
        ppool = [ppool2[l * 15:(l + 1) * 15, :] for l in range(depth)]
        phg = [phg2[l * 256:(l + 1) * 256, :].rearrange("(h c) v -> h c v", h=4) for l in range(depth)]
        spool_o = {(l, si): spool2[(l * nsm + si) * 15:(l * nsm + si + 1) * 15, :] for l in range(depth) for si in range(nsm)}
        shg_o = {(l, si): shg2[(l * nsm + si) * 256:(l * nsm + si + 1) * 256, :].rearrange("(h c) v -> h c v", h=4) for l in range(depth) for si in range(nsm)}
        outbufs = []

        def outbuf(name):
            b = Buf(name)
            outbufs.append(b)
            return b

        sb = self.sb
        x = sb("x", [128, 4, D]); Bx = [Buf(f"x{t}") for t in range(4)]
        xT = sb("xT", [128, 8, 512], BF16); BxT = [Buf(f"xT{t}") for t in range(4)]
        hT = sb("hT", [128, 12, 512], BF16); BhT = [Buf(f"hT{j}") for j in range(12)]
        ring = [sb(f"ring{i}", [128, 4096], BF16) for i in range(NSLOT)]
        Bring = [Buf(f"ring{i}") for i in range(NSLOT)]
        gb = [sb(f"gb{i}", [128, 2, D]) for i in range(2)]; Bgb = [Buf(f"gb{i}") for i in range(2)]
        ident = sb("ident", [128, 128]); identb = sb("identb", [128, 128], BF16); Bid = Buf("ident")
        tri = sb("tri", [64, 64]); Btri = Buf("tri")
        resetm = sb("resetm", [64, 2, 128]); Bresetm = Buf("resetm")
        ptmp = sb("ptmp", [128, 16]); Bptmp = Buf("ptmp")
        mhalf = sb("mhalf", [128, 1]); Bmh = Buf("mhalf")
        sgb = [sb(f"sg{i}", [128, 512]) for i in range(2)]; Bsg = [Buf(f"sg{i}") for i in range(2)]
        stt = [sb(f"stt{i}", [128, 24]) for i in range(4)]; Bstt = [Buf(f"stt{i}") for i in range(4)]
        kc_l = [sb(f"kcar{l}", [128, 4, 512], BF16) for l in range(depth)]
        vc_l = [sb(f"vcar{l}", [128, 4, 8, 65], BF16) for l in range(depth)]
        Bkc = [[Buf(f"kc{l}_{t}") for t in range(4)] for l in range(depth)]
        Bvc = [[Buf(f"vc{l}_{t}") for t in range(4)] for l in range(depth)]
        kcur = sb("kcur", [128, 4, 512], BF16); Bkcur = Buf("kcur")
        vcur = sb("vcur", [128, 4, 8, 65], BF16); Bvcur = [Buf(f"vcur{t}") for t in range(4)]
        qT = sb("qT", [128, 4, 512], BF16); BqT = Buf("qT")
        eb34 = sb("eb34", [128, 8, 256]); Beb = Buf("eb34")
        cbt = sb("cbt", [128, depth, 8]); Bcb = Buf("cbt")
        PT = [sb(f"PT{i}", [128, 640], BF16) for i in range(3)]; BPT = [Buf(f"PT{i}") for i in range(3)]
        etmp = [sb(f"etmp{i}", [128, 256]) for i in range(2)]; Bet = [Buf(f"etmp{i}") for i in range(2)]
        oa = sb("oa", [128, 512], BF16); Boa = Buf("oa")
        rden = sb("rden", [128, 8]); Brd = Buf("rden")
        kvst = sgb; Bkvst = Bsg
        uT = sb("uT", [128, 2, 144]); BuT = Buf("uT")
        sA = sb("sA", [128, 2, 144]); BsA = Buf("sA")
        sBb = sb("sBb", [128, 2, 144]); BsB = Buf("sBb")
        dT = sb("dT", [128, 2, 128], BF16); BdT = Buf("dT")
        ucar = sb("ucar", [128, depth, 2, 16]); Bucar = [Buf(f"ucar{l}") for l in range(depth)]
        wpool = sb("wpool", [128, depth, 2, 128], BF16); Bwp = Buf("wpool")
        pscale = sb("pscale", [128, depth, 2]); Bps = Buf("pscale")
        rc16 = sb("rc16", [128, 2, 16]); Brc = Buf("rc16")
        pst = sb("pst", [16, 256]); Bpst = Buf("pst")
        ptok = sb("ptok", [16, 256]); Bptok = Buf("ptok")
        lbt = sb("lbt", [64, depth, 4]); omlt = sb("omlt", [64, depth, 4]); Blb = Buf("lb")
        lbtmp = sb("lbtmp", [64, depth, 4])
        lbsum = sb("lbsum", [64, 4])
        normwb = sb("normwb", [64, depth, 64]); Bnw = Buf("normw")
        Sst = sb("Sst", [64, depth, 4, 64]); BSst = [Buf(f"Sst{l}") for l in range(depth)]
        QS = sb("QS", [64, 4, 128]); BQS = Buf("QS")
        Fb = sb("Fb", [64, 4, 128]); BF_ = Buf("F")
        KIN = sb("KIN", [64, 4, 128]); BKIN = Buf("KIN")
        Bc = sb("Bc", [64, 4, 128]); BBc = Buf("Bc")
        EQ = sb("EQ", [64, 4, 128]); BEQ = Buf("EQ")
        EK = sb("EK", [64, 4, 128]); BEK = Buf("EK")
        QTt = sb("QTt", [64, 4, 128], BF16); BQT = Buf("QTt")
        KTt = sb("KTt", [64, 4, 128], BF16); BKT = Buf("KTt")
        EBM = sb("EBM", [64, 4, 8]); BEBM = Buf("EBM")
        EBL = sb("EBL", [64, 4, 8]); BEBL = Buf("EBL")
        VTOK = [sb(f"VTOK{i}", [64, 256], BF16) for i in range(4)]; BVT = [Buf(f"VTOK{i}") for i in range(4)]
        G2 = [sb(f"G2{i}", [64, 256]) for i in range(4)]; BG2 = [Buf(f"G2{i}") for i in range(4)]
        KTOK = sb("KTOK", [64, 4, 64], BF16); BKTOK = Buf("KTOK")
        SM = sb("SM", [64, 4, 64], BF16); BSM = Buf("SM")
        Sbf = sb("Sbf", [64, 4, 64], BF16); BSbf = Buf("Sbf")
        T1 = sb("T1", [64, 4, 64]); BT1 = Buf("T1")
        osq = sb("osq", [64, 256]); Bosq = Buf("osq")
        oss = sb("oss", [64, 8]); Boss = Buf("oss")
        octok = sb("octok", [64, 256], BF16); Boc = Buf("octok")
        hst = sb("hst", [64, 256]); Bhst = Buf("hst")
        ptokp = sb("ptokp", [128, 1, 256]); Bptokp = Buf("ptokp")
        pTt = sb("pTt", [128, 2, 512], BF16); BpT = Buf("pTt")
        etile = sgb; Bet2 = Bsg
        ckst = oa; Bckst = Boa
        pb = [self.st.enter_context(nc.psum_tensor(f"pb{i}", [128, 512], F32)) for i in range(8)]
        Bpb = [Buf(f"pb{i}", excl=True) for i in range(8)]
        pbb = [p.bitcast(BF16) for p in pb]

        def rr(key, n):
            v = self.rr.get(key, 0)
            self.rr[key] = v + 1
            return v % n

        plan = []
        consumed = [0]
        issued = [0]

        def layer_plan(l):
            p = []
            for f in ("f1", "f2"):
                q = []
                for g in (0, 1):
                    for b in ((0, 1, 2) if g == 0 else (3, 4, 5)):
                        nc_ = 512 if b < 5 else 256
                        q.append(("k", f + "gu", l, b * 512, nc_))
                        q.append(("k", f + "gu", l, HID + b * 512, nc_))
                    for b in ((0, 1, 2) if g == 0 else (3, 4, 5)):
                        q.append(("j", f + "d", l, b * 512, 4 if b < 5 else 2))
                if f == "f1":
                    p += q
                    for b in range(6):
                        p.append(("k", "win", l, b * 512, 512 if b < 5 else 256))
                    p.append(("k", "wout", l, 0, 512)); p.append(("k", "wout", l, 512, 512))
                else:
                    p += q
                    p.append(("k", "pleg", l, 0, 512)); p.append(("k", "pleg", l, 512, 512))
                    p.append(("j", "plep", l, 0, 2))
            return p

        ntiles_total = self.n_ptiles + (1 if self.n_samp else 0)
        for _ in range(ntiles_total):
            for l in range(depth):
                plan.extend(layer_plan(l))

        def issue_upto(n):
            while issued[0] < min(n, len(plan)):
                i = issued[0]
                kind, nm, l, a0, n_ = plan[i]
                s = i % NSLOT
                if kind == "k":
                    src = Wd[nm][l][:, a0:a0 + n_].rearrange("(k p) n -> p k n", p=128)
                    dst = ring[s][:, 0:8 * n_].rearrange("p (k n) -> p k n", k=8)
                else:
                    src = Wd[nm][l][a0:a0 + n_ * 128, :].rearrange("(j p) n -> p j n", p=128)
                    dst = ring[s][:, 0:n_ * 1024].rearrange("p (j n) -> p j n", j=n_)
                S.D("pool", f"ring{s}", dst, src, [], [Bring[s]])
                issued[0] += 1

        def wnext(kind, nm, l, a0, n_):
            i = consumed[0]
            assert plan[i] == (kind, nm, l, a0, n_), (plan[i], (kind, nm, l, a0, n_))
            issue_upto(i + NSLOT - 2)
            consumed[0] += 1
            s = i % NSLOT
            if kind == "k":
                return ring[s][:, 0:8 * n_].rearrange("p (k n) -> p k n", k=8), Bring[s]
            return ring[s][:, 0:n_ * 1024].rearrange("p (j n) -> p j n", j=n_), Bring[s]

        mm, tr, A, V, G, Dm = S.mm, S.tr, S.A, S.V, S.G, S.D
        G("memset", [], [Bid], ap=ident[:], constant=1.0)
        G("affine_select", [Bid], [Bid], out=ident[:], in_=ident[:], pattern=[[-1, 128]], compare_op=ALU.is_equal, fill=0.0, base=0, channel_multiplier=1)
        V("tensor_copy", [Bid], [Bid], out=identb[:], in_=ident[:])
        G("memset", [], [Btri], ap=tri[:], constant=1.0)
        G("affine_select", [Btri], [Btri], out=tri[:], in_=tri[:], pattern=[[1, 64]], compare_op=ALU.is_ge, fill=0.0, base=0, channel_multiplier=-1)
        G("memset", [], [Bresetm], ap=resetm[:], constant=1.0)
        for blk in range(2):
            G("memset", [Bresetm], [Bresetm], ap=resetm[:, 0, blk * 64:blk * 64 + 1], constant=0.0)
        for blk in range(4):
            G("memset", [Bresetm], [Bresetm], ap=resetm[:, 1, blk * 32:blk * 32 + 1], constant=0.0)
        for blk in range(8):
            pass
        G("memset", [], [Bmh], ap=mhalf[:], constant=-0.5)
        G("memset", [], [Bptok], ap=ptok[:], constant=0.0)
        for l in range(depth):
            G("memset", [], Bvc[l], ap=vc_l[l][:], constant=1.0)
            G("memset", [], Bkc[l], ap=kc_l[l][:], constant=0.0)
        G("memset", [], Bvcur, ap=vcur[:], constant=1.0)
        G("memset", [], [Brc], ap=rc16[:], constant=1.0)
        for c in range(2):
            for hf in range(2):
                w = (2, 4, 8, 16)[c * 2 + hf]
                for t in range(1, 16):
                    val = 1.0 / min(t + 1, w)
                    if t < w:
                        hi = 16 if t == w - 1 else t + 1
                        G("memset", [Brc], [Brc], ap=rc16[hf * 64:hf * 64 + 64, c, t:hi], constant=val)
        G("memset", [], [Bwp], ap=wpool[:], constant=0.0)
        for l in range(depth):
            for g in range(4):
                c, hf = g // 2, g % 2
                Dm("pool", "cstw", wpool[hf * 64:hf * 64 + 64, l, c, hf * 64:hf * 64 + 64], Wd["poolw"][l, g], [Bwp], [Bwp])
        with nc.allow_non_contiguous_dma(reason="tiny constant loads"):
            for l in range(depth):
                Dm("sp", "cst", pscale[:, l, :], Wd["pscale"][l].rearrange("(c p) -> p c", p=128), [], [Bps], slow=True)
                Dm("sp", "cst", lbtmp[:, l, :], Wd["lbraw"][l].rearrange("(h c) -> c h", c=64), [], [Blb], slow=True)
                Dm("sp", "cst", cbt[:, l, :], Wd["rbc"][l].partition_broadcast(128), [], [Bcb])
                Dm("sp", "cst", normwb[:, l, :], Wd["normw"][l].partition_broadcast(64), [], [Bnw])
        A(lbtmp[:], lbtmp[:], AF.Exp, [Blb], [Blb])
        V("tensor_copy", [Blb], [Blb], out=lbsum[:], in_=lbtmp[:, 0, :])
        for l in range(1, depth):
            V("tensor_tensor", [Blb], [Blb], out=lbsum[:], in0=lbsum[:], in1=lbtmp[:, l, :], op=ALU.add)
        V("reciprocal", [Blb], [Blb], out=lbsum[:], in_=lbsum[:])
        V("memset", [Blb], [Blb], ap=lbt[:, 0, :], constant=0.0)
        for l in range(1, depth):
            V("tensor_tensor", [Blb], [Blb], out=lbtmp[:, l, :], in0=lbtmp[:, l, :], in1=lbsum[:], op=ALU.mult)
            V("tensor_tensor", [Blb], [Blb], out=lbt[:, l, :], in0=lbt[:, l - 1, :], in1=lbtmp[:, l, :], op=ALU.add)
        V("tensor_scalar", [Blb], [Blb], out=omlt[:], in0=lbt[:], scalar1=-1.0, scalar2=1.0, op0=ALU.mult, op1=ALU.add)

        def load_gb(l, i):
            s = rr("gb", 2)
            Dm("sp", f"gb{s}", gb[s][:, 0, :], Wd["lng"][l, i].partition_broadcast(128), [], [Bgb[s]])
            Dm("sp", f"gb{s}", gb[s][:, 1, :], Wd["lnb"][l, i].partition_broadcast(128), [], [Bgb[s]])
            return s

        def transposes(tt, ntok):
            S.tag = "tr"
            for half in range(2):
                b = 6 + half
                for k4 in range(4):
                    kc = half * 4 + k4
                    tr(pb[b][:, k4 * 128:k4 * 128 + ntok], x[0:ntok, tt, kc * 128:(kc + 1) * 128], ident[0:ntok, 0:ntok], [Bx[tt], Bid], [Bpb[b]])
                src = pb[b][:].rearrange("p (k n) -> p k n", k=4)[:, :, 0:ntok]
                dst = xT[:, half * 4:half * 4 + 4, tt * 128:tt * 128 + ntok]
                if half == 0:
                    A(dst, src, AF.Copy, [Bpb[b]], [BxT[tt]])
                else:
                    V("tensor_copy", [Bpb[b]], [BxT[tt]], out=dst, in_=src)

        def post(l, i, tt, ntok, gslot):
            s = tt
            S.tag = "ln"
            xt = x[0:ntok, tt, :]
            st_ = stt[s]
            V("bn_stats", [Bx[tt]], [Bstt[s]], out=st_[0:ntok, 0:6], in_=x[0:ntok, tt, 0:512])
            V("bn_stats", [Bx[tt]], [Bstt[s]], out=st_[0:ntok, 6:12], in_=x[0:ntok, tt, 512:1024])
            V("bn_aggr", [Bstt[s]], [Bstt[s]], out=st_[0:ntok, 12:14], in_=st_[0:ntok, 0:12])
            V("tensor_scalar", [Bstt[s]], [Bstt[s]], out=st_[0:ntok, 14:15], in0=st_[0:ntok, 13:14], scalar1=LN_EPS, scalar2=None, op0=ALU.add)
            A(st_[0:ntok, 17:18], st_[0:ntok, 14:15], AF.Sqrt, [Bstt[s]], [Bstt[s]])
            V("reciprocal", [Bstt[s]], [Bstt[s]], out=st_[0:ntok, 15:16], in_=st_[0:ntok, 17:18])
            V("scalar_tensor_tensor", [Bstt[s]], [Bstt[s]], out=st_[0:ntok, 16:17], in0=st_[0:ntok, 12:13], scalar=-1.0, in1=st_[0:ntok, 15:16], op0=ALU.mult, op1=ALU.mult)
            A(xt, xt, AF.Identity, [Bx[tt], Bstt[s]], [Bx[tt]], bias=st_[0:ntok, 16:17], scale=st_[0:ntok, 15:16])
            V("tensor_tensor", [Bx[tt], Bgb[gslot]], [Bx[tt]], out=xt, in0=xt, in1=gb[gslot][0:ntok, 0, :], op=ALU.mult)
            V("tensor_tensor", [Bx[tt], Bgb[gslot]], [Bx[tt]], out=xt, in0=xt, in1=gb[gslot][0:ntok, 1, :], op=ALU.add)

        class Delayed:
            DEPTH = 2
            def __init__(self):
                self.q = []
            def flush(self, final=False):
                while self.q and (final or len(self.q) >= self.DEPTH):
                    transposes(*self.q.pop(0))
            @property
            def pending(self):
                return None
            @pending.setter
            def pending(self, v):
                self.q.append(v)

        def ffn(l, f, lni, T, tiles):
            gslot = load_gb(l, lni)
            for g in (0, 1):
                blks = (0, 1, 2) if g == 0 else (3, 4, 5)
                jj = 0
                for b in blks:
                    ncols = 512 if b < 5 else 256
                    S.tag = "ffn.up"
                    wg, Bg = wnext("k", f + "gu", l, b * 512, ncols)
                    wu, Bu = wnext("k", f + "gu", l, HID + b * 512, ncols)
                    for cc in range(ncols // 128):
                        pg = rr("pg", 2); pu = 2 + rr("pu", 2)
                        for kc in range(8):
                            mm(pb[pg][:, 0:T], wg[:, kc, cc * 128:(cc + 1) * 128], xT[:, kc, 0:T], kc == 0, kc == 7, [Bg] + BxT, [Bpb[pg]])
                        for kc in range(8):
                            mm(pb[pu][:, 0:T], wu[:, kc, cc * 128:(cc + 1) * 128], xT[:, kc, 0:T], kc == 0, kc == 7, [Bu] + BxT, [Bpb[pu]])
                        sgi = rr("sg", 2)
                        A(sgb[sgi][:, 0:T], pb[pg][:, 0:T], AF.Silu, [Bpb[pg]], [Bsg[sgi]])
                        V("scalar_tensor_tensor", [Bpb[pu], Bsg[sgi]], [BhT[jj]], out=hT[:, jj, 0:T], in0=pb[pu][:, 0:T], scalar=0.5, in1=sgb[sgi][:, 0:T], op0=ALU.mult, op1=ALU.mult)
                        jj += 1
                nj = jj
                import os
                ksub = int(os.environ.get("KSUB", "9"))
                if ksub == 1:
                    return
                wds = []
                S.tag = "ffn.down"
                for b in blks:
                    wds.append(wnext("j", f + "d", l, b * 512, 4 if b < 5 else 2))
                dl = Delayed()
                for (tt, ntok) in tiles:
                    S.tag = "ffn.down"
                    pys = []
                    for nh in range(2):
                        py = 4 + rr("py", 2)
                        pys.append(py)
                        for j in range(nj):
                            wd, Bw = wds[j // 4]
                            mm(pb[py][0:ntok, :], hT[:, j, tt * 128:tt * 128 + ntok], wd[:, j % 4, nh * 512:(nh + 1) * 512], j == 0, j == nj - 1, [BhT[j], Bw], [Bpb[py]])
                    dl.flush()
                    for nh in range(2):
                        py = pys[nh]
                        xs_ = x[0:ntok, tt, nh * 512:(nh + 1) * 512]
                        if g == 0:
                            V("scalar_tensor_tensor", [Bx[tt], Bpb[py]], [Bx[tt]], out=xs_, in0=xs_, scalar=ALPHA, in1=pb[py][0:ntok, :], op0=ALU.mult, op1=ALU.add)
                        else:
                            V("tensor_tensor", [Bx[tt], Bpb[py]], [Bx[tt]], out=xs_, in0=xs_, in1=pb[py][0:ntok, :], op=ALU.add)
                    if g == 1 and ksub >= 4:
                        post(l, lni, tt, ntok, gslot)
                        dl.pending = (tt, ntok)
                dl.flush(final=True)
                if ksub == 2:
                    return

        def attention(l, q0, nq, ktiles, mix_c0, mask_first, mask_last):
            nt = len(ktiles)
            S.tag = "attn"
            if nq == 128 and len(ktiles) >= 2 and all(kt[5] == 128 for kt in ktiles) and ktiles[-2][0] == "3" and ktiles[-1][0] == "4":
                return attention_fast(l, q0, ktiles, mix_c0, mask_first)
            offs = [i * 128 for i in range(nt)]
            for h in range(8):
                c, hb = h // 2, (h % 2) * 64
                sbank = 2 * rr("sbank", 2)
                pt = rr("PT", 3)
                for i, (kind, kf, Bk, vap, Bv, nk) in enumerate(ktiles):
                    bank = sbank + (offs[i] // 512)
                    col = offs[i] % 512
                    kap = kf(c)
                    mm(pb[bank][0:nk, col:col + nq], kap[hb:hb + 64, 0:nk], qT[hb:hb + 64, c, q0:q0 + nq], True, True, [Bk, BqT], [Bpb[bank]])
                for i, (kind, kf, Bk, vap, Bv, nk) in enumerate(ktiles):
                    bank = sbank + (offs[i] // 512)
                    col = offs[i] % 512
                    o_ = offs[i]
                    if kind == "c":
                        A(PT[pt][0:nk, o_:o_ + nq], pb[bank][0:nk, col:col + nq], AF.Exp, [Bpb[bank], Bcb], [BPT[pt]], bias=cbt[0:nk, l, h:h + 1], scale=0.125)
                        if i == 0 and mask_first:
                            V("memset", [BPT[pt]], [BPT[pt]], ap=PT[pt][0:64, o_ + 64:o_ + 128], constant=0.0)
                    else:
                        et = rr("etmp", 2)
                        eo = 0 if kind == "3" else 128
                        A(etmp[et][0:nk, 0:nq], pb[bank][0:nk, col:col + nq], AF.Exp, [Bpb[bank]], [Bet[et]], scale=0.125)
                        V("tensor_tensor", [Bet[et], Beb], [BPT[pt]], out=PT[pt][0:nk, o_:o_ + nq], in0=etmp[et][0:nk, 0:nq], in1=eb34[0:nk, h, eo:eo + nq], op=ALU.mult)
                        if kind == "4" and mask_last:
                            V("memset", [BPT[pt]], [BPT[pt]], ap=PT[pt][64:128, o_:o_ + 64], constant=0.0)
                ob = 4 + h // 4
                for i, (kind, kf, Bk, vap, Bv, nk) in enumerate(ktiles):
                    o_ = offs[i]
                    mm(pb[ob][0:nq, (h % 4) * 65:(h % 4) * 65 + 65], PT[pt][0:nk, o_:o_ + nq], vap[0:nk, h, :], i == 0, i == nt - 1, [BPT[pt], Bv], [Bpb[ob]])
            for hh in range(2):
                ob = 4 + hh
                pv3 = pb[ob][0:nq, 0:260].rearrange("p (h d) -> p h d", h=4)
                V("reciprocal", [Bpb[ob]], [Brd], out=rden[0:nq, hh * 4:hh * 4 + 4], in_=pv3[:, :, 64])
                V("tensor_tensor", [Bpb[ob], Brd], [Boa], out=oa[0:nq, hh * 256:(hh + 1) * 256].rearrange("p (h d) -> p h d", h=4), in0=pv3[:, :, 0:64],
                  in1=rden[0:nq, hh * 4:hh * 4 + 4].unsqueeze(2).to_broadcast([nq, 4, 64]), op=ALU.mult)
            for c in range(4):
                tr(pbb[6][:, c * 128:c * 128 + nq], oa[0:nq, c * 128:(c + 1) * 128], identb[0:nq, 0:nq], [Boa, Bid], [Bpb[6]])
            A(hT[:, 0:4, mix_c0:mix_c0 + nq], pbb[6][:, 0:512].rearrange("p (c n) -> p c n", c=4)[:, :, 0:nq], AF.Copy, [Bpb[6]], BhT[0:4])

        def attention_fast(l, q0, ktiles, mix_c0, mask_first):
            nq = 128
            ncst = len(ktiles) - 2
            cts, t3, t4 = ktiles[:ncst], ktiles[-2], ktiles[-1]
            order = list(cts) + [t3, t4]
            pcol = [i * 128 for i in range(ncst)] + [384, 512]
            st = {}

            def issue_S(h):
                c, hb = h // 2, (h % 2) * 64
                sbank = 2 * rr("sbank", 2)
                for i, (kind, kf, Bk, vap, Bv, nk) in enumerate(cts):
                    mm(pb[sbank][:, i * 128:(i + 1) * 128], kf(c)[hb:hb + 64, :], qT[hb:hb + 64, c, q0:q0 + nq], True, True, [Bk, BqT], [Bpb[sbank]])
                for i, (kind, kf, Bk, vap, Bv, nk) in enumerate((t3, t4)):
                    mm(pb[sbank + 1][:, i * 128:(i + 1) * 128], kf(c)[hb:hb + 64, :], qT[hb:hb + 64, c, q0:q0 + nq], True, True, [Bk, BqT], [Bpb[sbank + 1]])
                st[h] = sbank

            def issue_E(h):
                sbank = st[h]
                pt = rr("PT", 3)
                if ncst:
                    A(PT[pt][:, 0:ncst * 128], pb[sbank][:, 0:ncst * 128], AF.Exp, [Bpb[sbank], Bcb], [BPT[pt]], bias=cbt[:, l, h:h + 1], scale=0.125)
                    if mask_first:
                        V("memset", [BPT[pt]], [BPT[pt]], ap=PT[pt][0:64, 64:128], constant=0.0)
                et = rr("etmp", 2)
                A(etmp[et][:, 0:256], pb[sbank + 1][:, 0:256], AF.Exp, [Bpb[sbank + 1]], [Bet[et]], scale=0.125)
                V("tensor_tensor", [Bet[et], Beb], [BPT[pt]], out=PT[pt][:, 384:640], in0=etmp[et][:, 0:256], in1=eb34[:, h, 0:256], op=ALU.mult)
                st[h] = pt

            def issue_PV(h):
                pt = st[h]
                ob = 4 + h // 4
                for i, (kind, kf, Bk, vap, Bv, nk) in enumerate(order):
                    mm(pb[ob][:, (h % 4) * 65:(h % 4) * 65 + 65], PT[pt][:, pcol[i]:pcol[i] + nq], vap[:, h, :], i == 0, i == len(order) - 1, [BPT[pt], Bv], [Bpb[ob]])

            issue_S(0)
            for h in range(8):
                if h + 1 < 8:
                    issue_S(h + 1)
                issue_E(h)
                issue_PV(h)
            for hh in range(2):
                ob = 4 + hh
                pv3 = pb[ob][:, 0:260].rearrange("p (h d) -> p h d", h=4)
                V("reciprocal", [Bpb[ob]], [Brd], out=rden[:, hh * 4:hh * 4 + 4], in_=pv3[:, :, 64])
                V("tensor_tensor", [Bpb[ob], Brd], [Boa], out=oa[:, hh * 256:(hh + 1) * 256].rearrange("p (h d) -> p h d", h=4), in0=pv3[:, :, 0:64],
                  in1=rden[:, hh * 4:hh * 4 + 4].unsqueeze(2).to_broadcast([128, 4, 64]), op=ALU.mult)
            for c in range(4):
                tr(pbb[6][:, c * 128:(c + 1) * 128], oa[:, c * 128:(c + 1) * 128], identb[:], [Boa, Bid], [Bpb[6]])
            A(hT[:, 0:4, mix_c0:mix_c0 + nq], pbb[6][:, 0:512].rearrange("p (c n) -> p c n", c=4), AF.Copy, [Bpb[6]], BhT[0:4])

        def pool_group(l, t0, GT, first_of_seq, hist_dram, out_dram, mixc0, w3):
            S.tag = "pool"
            if hist_dram is not None:
                Dm("sp", "ptok", ptok[0:15, :], hist_dram, [], [Bptok])
                for c in range(2):
                    tr(pb[7][:, c * 16:c * 16 + 16], ptok[0:16, c * 128:(c + 1) * 128], ident[0:16, 0:16], [Bptok, Bid], [Bpb[7]])
                V("tensor_copy", [Bpb[7]], [BuT], out=uT[:, :, 1:16], in_=pb[7][:, 0:32].rearrange("p (c n) -> p c n", c=2)[:, :, 0:15])
            else:
                V("tensor_copy", [Bucar[l]], [BuT], out=uT[:, :, 0:16], in_=ucar[:, l, :, :])
            wv, Bw = w3
            for c in range(2):
                bank = rr("pg", 2)
                for kc in range(8):
                    mm(pb[bank][:, 0:GT], wv[:, kc, c * 128:(c + 1) * 128], xT[:, kc, t0:t0 + GT], kc == 0, kc == 7, [Bw] + BxT, [Bpb[bank]])
                A(uT[:, c, 16:16 + GT], pb[bank][:, 0:GT], AF.Copy, [Bpb[bank]], [BuT])
            n = 16 + GT
            V("tensor_tensor", [BuT], [BsA], out=sA[:, :, 2:n], in0=uT[:, :, 2:n], in1=uT[:, :, 1:n - 1], op=ALU.add)
            V("tensor_tensor", [BsA], [BsB], out=sBb[:, :, 4:n], in0=sA[:, :, 4:n], in1=sA[:, :, 2:n - 2], op=ALU.add)

            def dcalc(src, Bsrc, c, hf, w):
                lo, hi = hf * 64, hf * 64 + 64
                V("scalar_tensor_tensor", [Bsrc, BuT], [BdT], out=dT[lo:hi, c, 0:GT], in0=src[lo:hi, c, 16:n], scalar=1.0 / w, in1=uT[lo:hi, c, 16:n], op0=ALU.mult, op1=ALU.subtract)
                if first_of_seq:
                    V("tensor_tensor", [Bsrc, Brc], [Bptmp], out=ptmp[lo:hi, 0:16], in0=src[lo:hi, c, 16:32], in1=rc16[lo:hi, c, :], op=ALU.mult)
                    V("tensor_tensor", [Bptmp, BuT], [BdT], out=dT[lo:hi, c, 0:16], in0=ptmp[lo:hi, 0:16], in1=uT[lo:hi, c, 16:32], op=ALU.subtract)

            dcalc(sA, BsA, 0, 0, 2)
            dcalc(sBb, BsB, 0, 1, 4)
            V("tensor_tensor", [BsB], [BsA], out=sA[:, :, 8:n], in0=sBb[:, :, 8:n], in1=sBb[:, :, 4:n - 4], op=ALU.add)
            dcalc(sA, BsA, 1, 0, 8)
            V("tensor_tensor", [BsA], [BsB], out=sBb[:, :, 16:n], in0=sA[:, :, 16:n], in1=sA[:, :, 8:n - 8], op=ALU.add)
            dcalc(sBb, BsB, 1, 1, 16)
            return lambda: pool_part2(l, GT, hist_dram, out_dram, mixc0, n)

        def pool_part2(l, GT, hist_dram, out_dram, mixc0, n):
            S.tag = "pool"
            for c in range(2):
                bank = rr("pg", 2)
                mm(pb[bank][:, 0:GT], wpool[:, l, c, :], dT[:, c, 0:GT], True, True, [Bwp, BdT], [Bpb[bank]])
                A(hT[:, 4 + c, mixc0:mixc0 + GT], pb[bank][:, 0:GT], AF.Copy, [Bpb[bank], Bps], [BhT[4 + c]], scale=pscale[:, l, c:c + 1])
            if out_dram is not None:
                for c in range(2):
                    tr(pb[7][0:16, 128 + c * 128:256 + c * 128], uT[:, c, n - 16:n], ident[:], [BuT, Bid], [Bpb[7]])
                V("tensor_copy", [Bpb[7]], [Bpst], out=pst[0:16, :], in_=pb[7][0:16, 128:384])
                Dm("sp", "pst", out_dram, pst[1:16, :], [Bpst], [outbuf("o")])
            if hist_dram is None:
                V("tensor_copy", [BuT], [Bucar[l]], out=ucar[:, l, :, :], in_=uT[:, :, GT:GT + 16])

        hg = {"pending": None, "gcount": 0}

        def hgrn_drain():
            if hg["pending"] is not None:
                hg["pending"]()
                hg["pending"] = None

        def hgrn_group(l, t0, GT, L, w3, w4, w5, state_in, state_out, mixc0, chain, after_proj=None):
            nblk = GT // L
            gbase = 0 if nblk == 4 else 2 * (hg["gcount"] % 2)
            hg["gcount"] += 1
            S.tag = "hgrn.proj"
            (wv3, Bw3), (wv4, Bw4), (wv5, Bw5) = w3, w4, w5
            rmask = resetm[:, 0, 0:GT] if L == 64 else resetm[:, 1, 0:GT]
            for h in range(4):
                bank = rr("pg", 2)
                for kc in range(8):
                    mm(pb[bank][0:64, 0:GT], wv3[:, kc, 256 + h * 64:256 + h * 64 + 64], xT[:, kc, t0:t0 + GT], kc == 0, kc == 7, [Bw3] + BxT, [Bpb[bank]])
                A(QS[:, h, 0:GT], pb[bank][0:64, 0:GT], AF.Silu, [Bpb[bank]], [BQS])
            for h in range(4):
                bank = rr("pg", 2)
                for kc in range(8):
                    mm(pb[bank][0:64, 0:GT], wv4[:, kc, h * 64:h * 64 + 64], xT[:, kc, t0:t0 + GT], kc == 0, kc == 7, [Bw4] + BxT, [Bpb[bank]])
                A(Fb[:, h, 0:GT], pb[bank][0:64, 0:GT], AF.Sigmoid, [Bpb[bank]], [BF_])
            for blk in range(nblk):
                tb = t0 + blk * L
                bank = 2 + rr("pu", 2)
                for kc in range(8):
                    mm(pb[bank][0:L, 0:256], xT[:, kc, tb:tb + L], wv4[:, kc, 256:512], kc == 0, kc == 7, [Bw4] + BxT, [Bpb[bank]])
                for kc in range(8):
                    mm(pb[bank][0:L, 256:512], xT[:, kc, tb:tb + L], wv5[:, kc, 0:256], kc == 0, kc == 7, [Bw5] + BxT, [Bpb[bank]])
                V("tensor_copy", [Bpb[bank]], [BVT[gbase + blk]], out=VTOK[gbase + blk][0:L, :], in_=pb[bank][0:L, 0:256])
                A(G2[gbase + blk][0:L, :], pb[bank][0:L, 256:512], AF.Silu, [Bpb[bank]], [BG2[gbase + blk]])
                V("tensor_tensor", [BG2[gbase + blk], Bnw], [BG2[gbase + blk]], out=G2[gbase + blk][0:L, :].rearrange("p (h d) -> p h d", h=4), in0=G2[gbase + blk][0:L, :].rearrange("p (h d) -> p h d", h=4), in1=normwb[0:L, l, :].unsqueeze(1).to_broadcast([L, 4, 64]), op=ALU.mult)
            if after_proj is not None:
                after_proj()
            S.tag = "hgrn.gate"
            V("tensor_tensor", [BF_, Blb], [BF_], out=Fb[:, :, 0:GT], in0=Fb[:, :, 0:GT], in1=omlt[:, l, :].unsqueeze(2).to_broadcast([64, 4, GT]), op=ALU.mult)
            V("tensor_tensor", [BF_, Blb], [BF_], out=Fb[:, :, 0:GT], in0=Fb[:, :, 0:GT], in1=lbt[:, l, :].unsqueeze(2).to_broadcast([64, 4, GT]), op=ALU.add)
            V("tensor_scalar", [BF_], [BKIN], out=KIN[:, :, 0:GT], in0=Fb[:, :, 0:GT], scalar1=-1.0, scalar2=1.0, op0=ALU.mult, op1=ALU.add)
            A(Fb[:, :, 0:GT], Fb[:, :, 0:GT], AF.Ln, [BF_], [BF_])
            for h in range(4):
                V("tensor_tensor_scan", [BF_, Bresetm], [BBc], out=Bc[:, h, 0:GT], data0=rmask, data1=Fb[:, h, 0:GT], initial=0.0, op0=ALU.mult, op1=ALU.add)
            mid = L // 2 - 1
            Bc4 = Bc[:, :, 0:GT].rearrange("p h (b t) -> p h b t", t=L)
            V("tensor_tensor", [BBc, BF_], [BF_], out=Fb[:, :, 0:GT].rearrange("p h (b t) -> p (h b) t", t=L), in0=Bc[:, :, 0:GT].rearrange("p h (b t) -> p (h b) t", t=L),
              in1=Bc[:, :, 0:GT].rearrange("p h (b t) -> p (h b) t", t=L)[:, :, mid:mid + 1].to_broadcast([64, 4 * nblk, L]), op=ALU.subtract)
            A(EQ[:, :, 0:GT], Fb[:, :, 0:GT], AF.Exp, [BF_], [BEQ])
            A(EK[:, :, 0:GT], Fb[:, :, 0:GT], AF.Exp, [BF_], [BEK], scale=-1.0)
            A(EBM[:, :, 0:nblk], Bc4[:, :, :, mid], AF.Exp, [BBc], [BEBM])
            A(EBL[:, :, 0:nblk], Bc4[:, :, :, L - 1], AF.Exp, [BBc], [BEBL])
            V("tensor_tensor", [BQS, BEQ], [BQT], out=QTt[:, :, 0:GT], in0=QS[:, :, 0:GT], in1=EQ[:, :, 0:GT], op=ALU.mult)
            V("tensor_tensor", [BKIN, BEK], [BKT], out=KTt[:, :, 0:GT], in0=KIN[:, :, 0:GT], in1=EK[:, :, 0:GT], op=ALU.mult)
            EQ4 = EQ[:, :, 0:GT].rearrange("p h (b t) -> p h b t", t=L)
            S.tag = "hgrn.blk"
            for blk in range(nblk):
                c0 = blk * L
                if not chain:
                    Dm("sp", "hst", Sst[:, l, :, :], state_in[blk].rearrange("h c v -> c h v"), [], [BSst[l]])
                for h in range(4):
                    tr(pbb[7][0:L, h * 64:(h + 1) * 64], KTt[:, h, c0:c0 + L], identb[0:64, 0:64], [BKT, Bid], [Bpb[7]])
                A(KTOK[0:L, :, :], pbb[7][0:L, 0:256].rearrange("p (h c) -> p h c", h=4), AF.Copy, [Bpb[7]], [BKTOK])
                for h in range(4):
                    mm(pb[6][0:L, h * 64:h * 64 + L], KTt[:, h, c0:c0 + L], QTt[:, h, c0:c0 + L], True, True, [BKT, BQT], [Bpb[6]])
                V("tensor_tensor", [Bpb[6], Btri], [BSM], out=SM[0:L, :, 0:L], in0=pb[6][0:L, 0:256].rearrange("p (h t) -> p h t", h=4)[:, :, 0:L],
                  in1=tri[0:L, 0:L].unsqueeze(1).to_broadcast([L, 4, L]), op=ALU.mult)
                V("tensor_tensor", [BSst[l], BEBM], [BSbf], out=Sbf[:], in0=Sst[:, l, :, :], in1=EBM[:, :, blk:blk + 1].to_broadcast([64, 4, 64]), op=ALU.mult)
                ob = 4 + rr("hgo", 2)
                for h in range(4):
                    mm(pb[ob][0:L, h * 64:(h + 1) * 64], SM[0:L, h, 0:L], VTOK[gbase + blk][0:L, h * 64:(h + 1) * 64], True, False, [BSM, BVT[gbase + blk]], [Bpb[ob]])
                    mm(pb[ob][0:L, h * 64:(h + 1) * 64], QTt[:, h, c0:c0 + L], Sbf[:, h, :], False, True, [BQT, BSbf], [Bpb[ob]])
                mb = 6
                for h in range(4):
                    mm(pb[mb][0:64, 256 + h * 64:256 + (h + 1) * 64], KTOK[0:L, h, :], VTOK[gbase + blk][0:L, h * 64:(h + 1) * 64], True, True, [BKTOK, BVT[gbase + blk]], [Bpb[mb]])
                V("tensor_tensor", [Bpb[mb], BEQ], [BT1], out=T1[:], in0=pb[mb][0:64, 256:512].rearrange("p (h v) -> p h v", h=4),
                  in1=EQ4[:, :, blk, L - 1:L].to_broadcast([64, 4, 64]), op=ALU.mult)
                V("tensor_tensor", [BSst[l], BEBL], [BSst[l]], out=Sst[:, l, :, :], in0=Sst[:, l, :, :], in1=EBL[:, :, blk:blk + 1].to_broadcast([64, 4, 64]), op=ALU.mult)
                V("tensor_tensor", [BSst[l], BT1], [BSst[l]], out=Sst[:, l, :, :], in0=Sst[:, l, :, :], in1=T1[:], op=ALU.add)
                if state_out is not None and (not chain or blk == nblk - 1):
                    so = state_out[blk] if not chain else state_out
                    V("tensor_copy", [BSst[l]], [Bhst], out=hst[:].rearrange("p (h v) -> p h v", h=4), in_=Sst[:, l, :, :])
                    Dm("sp", "hso", so.rearrange("h c v -> c h v"), hst[:].rearrange("p (h v) -> p h v", h=4), [Bhst], [outbuf("o")])
                def post_blk(ob=ob, gi=gbase + blk, c0=c0):
                    S.tag = "hgrn.out"
                    A(osq[0:L, :], pb[ob][0:L, 0:256], AF.Square, [Bpb[ob]], [Bosq])
                    V("tensor_reduce", [Bosq], [Boss], out=oss[0:L, 0:4], in_=osq[0:L, :].rearrange("p (h v) -> p h v", h=4), axis=AX.X, op=ALU.add)
                    V("tensor_scalar", [Boss], [Boss], out=oss[0:L, 0:4], in0=oss[0:L, 0:4], scalar1=1.0 / 64, scalar2=RMS_EPS, op0=ALU.mult, op1=ALU.add)
                    A(oss[0:L, 0:4], oss[0:L, 0:4], AF.Sqrt, [Boss], [Boss])
                    V("reciprocal", [Boss], [Boss], out=oss[0:L, 4:8], in_=oss[0:L, 0:4])
                    V("tensor_tensor", [Bpb[ob], Boss, Bosq], [Bosq], out=osq[0:L, :].rearrange("p (h v) -> p h v", h=4), in0=pb[ob][0:L, 0:256].rearrange("p (h v) -> p h v", h=4),
                      in1=oss[0:L, 4:8].unsqueeze(2).to_broadcast([L, 4, 64]), op=ALU.mult)
                    V("tensor_tensor", [Bosq, BG2[gi]], [Boc], out=octok[0:L, :], in0=osq[0:L, :], in1=G2[gi][0:L, :], op=ALU.mult)
                    for c in range(2):
                        tr(pbb[7][:, 512 + c * 64:512 + c * 64 + L], octok[0:L, c * 128:(c + 1) * 128], identb[0:L, 0:L], [Boc, Bid], [Bpb[7]])
                    A(hT[:, 6:8, mixc0 + c0:mixc0 + c0 + L], pbb[7][:, 512:640].rearrange("p (c n) -> p c n", c=2)[:, :, 0:L], AF.Copy, [Bpb[7]], BhT[6:8])
                if hg["pending"] is not None:
                    hg["pending"]()
                hg["pending"] = post_blk

        def mixer(l, T, tiles, nseq, first_tile, last_tile, is_prompt):
            import os
            S.tag = "qkv"
            gslot = load_gb(l, 1)
            Dm("sp", "eb", eb34[:], Wd["rb34"][l].rearrange("h r n -> r h n"), [], [Beb])
            A(eb34[:], eb34[:], AF.Exp, [Beb], [Beb])
            V("memset", [Beb], [Beb], ap=eb34[64:128, :, 128:192], constant=0.0)
            wq, Bq_ = wnext("k", "win", l, 0, 512)
            for c in range(4):
                bank = rr("pg", 2)
                for kc in range(8):
                    mm(pb[bank][:, 0:T], wq[:, kc, c * 128:(c + 1) * 128], xT[:, kc, 0:T], kc == 0, kc == 7, [Bq_] + BxT, [Bpb[bank]])
                A(qT[:, c, 0:T], pb[bank][:, 0:T], AF.Copy, [Bpb[bank]], [BqT])
            wk, Bk_ = wnext("k", "win", l, 512, 512)
            for c in range(4):
                bank = rr("pg", 2)
                for kc in range(8):
                    mm(pb[bank][:, 0:T], wk[:, kc, c * 128:(c + 1) * 128], xT[:, kc, 0:T], kc == 0, kc == 7, [Bk_] + BxT, [Bpb[bank]])
                V("tensor_copy", [Bpb[bank]], [Bkcur], out=kcur[:, c, 0:T], in_=pb[bank][:, 0:T])
            omask = int(os.environ.get("KOUTMASK", "7"))
            kout = (last_tile or not is_prompt) and bool(omask & 1)
            kx = int(os.environ.get("KX", "0"))
            if kout and not (kx & 2):
                for (tt, ntok) in tiles:
                    bank = 2 + rr("pu", 2)
                    for kc in range(8):
                        mm(pb[bank][0:ntok, :], xT[:, kc, tt * 128:tt * 128 + ntok], wk[:, kc, :], kc == 0, kc == 7, [Bk_] + BxT, [Bpb[bank]])
                    ks = rr("kvst", 2)
                    A(kvst[ks][0:ntok, :], pb[bank][0:ntok, :], AF.Copy, [Bpb[bank]], [Bkvst[ks]])
                    dst = (pk[l][tt * 128:tt * 128 + ntok, :] if is_prompt else sk[l][tt * 128:tt * 128 + ntok, :])
                    if not (kx & 1):
                        Dm("sp", f"kvo{ks}", dst, kvst[ks][0:ntok, :], [Bkvst[ks]], [outbuf("o")])
            wvv, Bv_ = wnext("k", "win", l, 1024, 512)
            vtiles = [(tt, tt * 128, ntok) for (tt, ntok) in tiles] if is_prompt else [(si, si * 32, 32) for si in range(nseq)]
            for (vi, tk0, ntok) in vtiles:
                bank = 2 + rr("pu", 2)
                for kc in range(8):
                    mm(pb[bank][0:ntok, :], xT[:, kc, tk0:tk0 + ntok], wvv[:, kc, :], kc == 0, kc == 7, [Bv_] + BxT, [Bpb[bank]])
                V("tensor_copy", [Bpb[bank]], [Bvcur[vi]], out=vcur[0:ntok, vi, :, 0:64], in_=pb[bank][0:ntok, :].rearrange("p (h d) -> p h d", h=8))
                if kout:
                    ks = rr("kvst", 2)
                    A(kvst[ks][0:ntok, :], pb[bank][0:ntok, :], AF.Copy, [Bpb[bank], Bvcur[vi]], [Bkvst[ks]])
                    dst = (pv[l][tk0:tk0 + ntok, :] if is_prompt else sv[l][tk0:tk0 + ntok, :])
                    if not (kx & 1):
                        Dm("sp", f"kvo{ks}", dst, kvst[ks][0:ntok, :], [Bkvst[ks]], [outbuf("o")])
            if is_prompt:
                for p in range(4):
                    kts = []
                    for i in range(5):
                        g = p - 4 + i
                        kind = "c" if i < 3 else ("3" if i == 3 else "4")
                        if g < 0:
                            if first_tile:
                                continue
                            ci = 4 + g
                            kts.append((kind, (lambda c, ci=ci: kc_l[l][:, c, ci * 128:(ci + 1) * 128]), Bkc[l][ci], vc_l[l][:, ci, :, :], Bvc[l][ci], 128))
                        else:
                            kts.append((kind, (lambda c, g=g: kcur[:, c, g * 128:(g + 1) * 128]), Bkcur, vcur[:, g, :, :], Bvcur[g], 128))
                    attention(l, p * 128, 128, kts, p * 128, mask_first=(len(kts) == 5), mask_last=True)
            else:
                for si in range(nseq):
                    for ci in range(4):
                        Dm("pool", "ckst", ckst[:], ck[l, si, ci * 128:(ci + 1) * 128, :], [], [Bckst])
                        for c in range(4):
                            tr(pbb[6][:, c * 128:(c + 1) * 128], ckst[:, c * 128:(c + 1) * 128], identb[:], [Bckst, Bid], [Bpb[6]])
                        A(kc_l[l][:, :, ci * 128:(ci + 1) * 128], pbb[6][:, 0:512].rearrange("p (c n) -> p c n", c=4), AF.Copy, [Bpb[6]], [Bkc[l][ci]])
                        Dm("pool", f"vc{ci}", vc_l[l][:, ci, :, 0:64], cv[l, si, ci * 128:(ci + 1) * 128, :].rearrange("p (h d) -> p h d", h=8), [], [Bvc[l][ci]])
                    kts = []
                    for i in range(4):
                        kind = "c" if i < 3 else "3"
                        kts.append((kind, (lambda c, i=i: kc_l[l][:, c, i * 128:(i + 1) * 128]), Bkc[l][i], vc_l[l][:, i, :, :], Bvc[l][i], 128))
                    kts.append(("4", (lambda c, si=si: kcur[:, c, si * 32:si * 32 + 32]), Bkcur, vcur[0:32, si, :, :], Bvcur[si], 32))
                    attention(l, si * 32, 32, kts, si * 32, mask_first=False, mask_last=False)
            import os
            kmix = int(os.environ.get("KMIX", "9"))
            if kmix == 2:
                return
            S.tag = "carry"
            if is_prompt and not last_tile:
                A(kc_l[l][:], kcur[:], AF.Copy, [Bkcur], Bkc[l])
                V("tensor_copy", Bvcur, Bvc[l], out=vc_l[l][:, :, :, 0:64], in_=vcur[:, :, :, 0:64])
            w3 = wnext("k", "win", l, 1536, 512)
            w4 = wnext("k", "win", l, 2048, 512)
            w5 = wnext("k", "win", l, 2560, 256)
            if is_prompt:
                if first_tile:
                    V("memset", [], [Bucar[l]], ap=ucar[:, l, :, :], constant=0.0)
                    V("memset", [], [BSst[l]], ap=Sst[:, l, :, :], constant=0.0)
                for g in range(4):
                    lastg = last_tile and g == 3
                    p2 = pool_group(l, g * 128, 128, first_tile and g == 0, None, ppool[l] if (lastg and omask & 2) else None, g * 128, w3)
                    hgrn_group(l, g * 128, 128, 64, w3, w4, w5, None, phg[l] if (lastg and omask & 4) else None, g * 128, True, after_proj=p2)
            else:
                for si in range(nseq):
                    pool_group(l, si * 32, 32, False, spool[l, si], spool_o[l, si], si * 32, w3)()
                hgrn_group(l, 0, 32 * nseq, 32, w3, w4, w5, [shg[l, si] for si in range(nseq)], [shg_o[l, si] for si in range(nseq)], 0, False)
            hgrn_drain()
            S.tag = "wout"
            wo = [wnext("k", "wout", l, 0, 512), wnext("k", "wout", l, 512, 512)]
            dl = Delayed()
            for (tt, ntok) in tiles:
                S.tag = "wout"
                pys = []
                for nh in range(2):
                    py = 4 + rr("py", 2)
                    pys.append(py)
                    wv_, Bw_ = wo[nh]
                    for kc in range(8):
                        mm(pb[py][0:ntok, :], hT[:, kc, tt * 128:tt * 128 + ntok], wv_[:, kc, :], kc == 0, kc == 7, [BhT[kc], Bw_], [Bpb[py]])
                dl.flush()
                for nh in range(2):
                    xs_ = x[0:ntok, tt, nh * 512:(nh + 1) * 512]
                    V("scalar_tensor_tensor", [Bx[tt], Bpb[pys[nh]]], [Bx[tt]], out=xs_, in0=xs_, scalar=ALPHA, in1=pb[pys[nh]][0:ntok, :], op0=ALU.mult, op1=ALU.add)
                post(l, 1, tt, ntok, gslot)
                dl.pending = (tt, ntok)
            dl.flush(final=True)

        def ple(l, T, tiles, prow0, is_prompt):
            S.tag = "ple"
            gslot = load_gb(l, 3)
            psrc = pp if is_prompt else ps_
            for (tt, ntok) in tiles:
                Dm("sp", "ptokp", ptokp[0:ntok, 0, :], psrc[l, prow0 + tt * 128:prow0 + tt * 128 + ntok, :], [], [Bptokp])
                for c in range(2):
                    tr(pb[7][:, c * 128:c * 128 + ntok], ptokp[0:ntok, 0, c * 128:(c + 1) * 128], ident[0:ntok, 0:ntok], [Bptokp, Bid], [Bpb[7]])
                V("tensor_copy", [Bpb[7]], [BpT], out=pTt[:, :, tt * 128:tt * 128 + ntok], in_=pb[7][:, 0:256].rearrange("p (c n) -> p c n", c=2)[:, :, 0:ntok])
            wg = [wnext("k", "pleg", l, 0, 512), wnext("k", "pleg", l, 512, 512)]
            wp_, Bwp_ = wnext("j", "plep", l, 0, 2)
            dl = Delayed()
            for (tt, ntok) in tiles:
                S.tag = "ple"
                banks = []
                for nh in range(2):
                    pg = rr("pg", 2); pu = 2 + rr("pu", 2)
                    banks.append((pg, pu))
                    wv_, Bw_ = wg[nh]
                    for kc in range(8):
                        mm(pb[pg][0:ntok, :], xT[:, kc, tt * 128:tt * 128 + ntok], wv_[:, kc, :], kc == 0, kc == 7, [BxT[tt], Bw_], [Bpb[pg]])
                    for c in range(2):
                        mm(pb[pu][0:ntok, :], pTt[:, c, tt * 128:tt * 128 + ntok], wp_[:, c, nh * 512:(nh + 1) * 512], c == 0, c == 1, [BpT, Bwp_], [Bpb[pu]])
                dl.flush()
                for nh in range(2):
                    pg, pu = banks[nh]
                    ei = rr("etile", 2)
                    A(etile[ei][0:ntok, :], pb[pg][0:ntok, :], AF.Sigmoid, [Bpb[pg]], [Bet2[ei]])
                    V("tensor_tensor", [Bet2[ei], Bpb[pu]], [Bet2[ei]], out=etile[ei][0:ntok, :], in0=etile[ei][0:ntok, :], in1=pb[pu][0:ntok, :], op=ALU.mult)
                    xs_ = x[0:ntok, tt, nh * 512:(nh + 1) * 512]
                    V("scalar_tensor_tensor", [Bx[tt], Bet2[ei]], [Bx[tt]], out=xs_, in0=xs_, scalar=ALPHA, in1=etile[ei][0:ntok, :], op0=ALU.mult, op1=ALU.add)
                post(l, 3, tt, ntok, gslot)
                dl.pending = (tt, ntok)
            dl.flush(final=True)

        def run_tile(xsrc, ydst, row0, T, tiles, is_prompt, first_tile, last_tile, nseq):
            for (tt, ntok) in tiles:
                Dm("sp", f"xin{tt}", x[0:ntok, tt, :], xsrc[row0 + tt * 128:row0 + tt * 128 + ntok, :], [], [Bx[tt]])
                transposes(tt, ntok)
            import os
            stage = int(os.environ.get("KSTAGE", "0"))
            for l in range(depth):
                if stage == -1: break
                ffn(l, "f1", 0, T, tiles)
                if stage == 1: break
                mixer(l, T, tiles, nseq, first_tile, last_tile, is_prompt)
                if stage == 2: break
                ffn(l, "f2", 2, T, tiles)
                if stage == 3: break
                ple(l, T, tiles, row0, is_prompt)
                if stage == 4: break
            for (tt, ntok) in tiles:
                Dm("sp", f"xout{tt}", ydst[row0 + tt * 128:row0 + tt * 128 + ntok, :], x[0:ntok, tt, :], [Bx[tt]], [outbuf("o")])

        import os
        stage = int(os.environ.get("KSTAGE", "0"))
        noout = bool(os.environ.get("KNOOUT"))
        for ti in range(self.n_ptiles):
            run_tile(xp, yp, ti * 512, 512, [(t, 128) for t in range(4)], True, ti == 0, (ti == self.n_ptiles - 1) and not noout, 0)
            if stage: break
        if self.n_samp and RUN_SAMPLE and not os.environ.get("KNOSAMP") and (not stage or os.environ.get("KSAMP")):
            if stage:
                consumed[0] = 0; issued[0] = 0
            run_tile(xs, ys, 0, NS, [(0, NS)], False, False, False, nsm)
        assert stage or (not RUN_SAMPLE) or os.environ.get("KNOSAMP") or consumed[0] == len(plan), (consumed[0], len(plan))
        self.nsem = S.emit(nc, outbufs)
        self.st.close()
        return nc


def _rel_tables(attn_rel_bias):
    depth = attn_rel_bias.shape[0]
    r = np.arange(128)[:, None]
    j = np.arange(128)[None, :]
    d3 = np.clip(128 - r + j, -63, 128) + 63
    d4 = np.clip(j - r, -63, 128) + 63
    rb34 = np.concatenate([attn_rel_bias[:, :, d3], attn_rel_bias[:, :, d4]], axis=-1)
    rbc = attn_rel_bias[:, :, 191]
    return np.ascontiguousarray(rb34, dtype=np.float32), np.ascontiguousarray(rbc, dtype=np.float32)


_CACHE = {}


def run(inputs, depth, n_ptiles, n_samp_per_core, n_cores, prompt_of_core, samp_of_core):
    key = (depth, n_ptiles, n_samp_per_core)
    if key not in _CACHE:
        _CACHE[key] = Prog(depth, n_ptiles, n_samp_per_core).build()
    nc = _CACHE[key]
    f = lambda a: np.ascontiguousarray(np.asarray(a), dtype=np.float32)
    rb34, rbc = _rel_tables(f(inputs["attn_rel_bias"]))
    common = {
        "f1gu": f(inputs["ffn1_w_gu"]), "f1d": f(inputs["ffn1_w_down"]), "win": f(inputs["w_in"]), "wout": f(inputs["w_out"]),
        "f2gu": f(inputs["ffn2_w_gu"]), "f2d": f(inputs["ffn2_w_down"]), "pleg": f(inputs["ple_w_gate"]), "plep": f(inputs["ple_w_proj"]),
        "lng": f(inputs["ln_g"]), "lnb": f(inputs["ln_b"]), "poolw": f(inputs["pool_w"]), "pscale": f(inputs["pool_scale"]),
        "lbraw": f(inputs["hgrn_lower_bounds"]), "normw": f(inputs["hgrn_norm_w"]), "rb34": rb34, "rbc": rbc,
    }
    xp, pp_ = f(inputs["x_prompt"]), f(inputs["p_prompt"])
    xs, ps_ = f(inputs["x_sample"]), f(inputs["p_sample"])
    ck, cv = f(inputs["cache_attn_k"]), f(inputs["cache_attn_v"])
    sp_, sh = f(inputs["state_pool"]), f(inputs["state_hgrn"])
    in_maps = []
    for c in range(n_cores):
        b = prompt_of_core[c]
        sb = samp_of_core[c]
        m = dict(common)
        m["xp"] = xp[b]; m["pp"] = np.ascontiguousarray(pp_[:, b])
        m["xs"] = np.ascontiguousarray(xs[sb].reshape(-1, D)); m["ps"] = np.ascontiguousarray(ps_[:, sb].reshape(depth, -1, 256))
        m["ck"] = np.ascontiguousarray(ck[:, sb].reshape(depth, len(sb), 512, 512)); m["cv"] = np.ascontiguousarray(cv[:, sb].reshape(depth, len(sb), 512, 512))
        m["spool"] = np.ascontiguousarray(sp_[:, sb]); m["shg"] = np.ascontiguousarray(sh[:, sb])
        in_maps.append(m)
    res = run_bass_kernel_spmd(nc, in_maps, core_ids=list(range(n_cores)))
    return res.results


def kernel(**inputs):
    depth = 4
    n_cores = 8
    prompt_of_core = [c % 4 for c in range(n_cores)]
    samp_of_core = [list(range(4 * c, 4 * c + 4)) for c in range(n_cores)]
    r = run(inputs, depth, 16, 4, n_cores, prompt_of_core, samp_of_core)
    B, SEQ = 4, 8192
    y_prompt = np.stack([r[b]["yp"] for b in range(B)]).astype(np.float32)
    y_sample = np.concatenate([r[c]["ys"].reshape(4, 32, D) for c in range(n_cores)]).astype(np.float32)
    pk = np.stack([r[b]["pk"].reshape(depth, 512, 512) for b in range(B)], axis=1).reshape(depth, B, 512, 8, 64).astype(np.float32)
    pv = np.stack([r[b]["pv"].reshape(depth, 512, 512) for b in range(B)], axis=1).reshape(depth, B, 512, 8, 64).astype(np.float32)
    ppool = np.stack([r[b]["ppool"].reshape(depth, 15, 256) for b in range(B)], axis=1).astype(np.float32)
    phg = np.stack([r[b]["phg"].reshape(depth, 4, 64, 64) for b in range(B)], axis=1).astype(np.float32)
    sk = np.concatenate([r[c]["sk"].reshape(depth, 4, 32, 8, 64) for c in range(n_cores)], axis=1).astype(np.float32)
    sv = np.concatenate([r[c]["sv"].reshape(depth, 4, 32, 8, 64) for c in range(n_cores)], axis=1).astype(np.float32)
    spool = np.concatenate([r[c]["spool_o"].reshape(depth, 4, 15, 256) for c in range(n_cores)], axis=1).astype(np.float32)
    shg = np.concatenate([r[c]["shg_o"].reshape(depth, 4, 4, 64, 64) for c in range(n_cores)], axis=1).astype(np.float32)
    return (y_prompt, y_sample, pk, pv, ppool, phg, sk, sv, spool, shg)
```

```python
import contextlib
import numpy as np
import concourse.bass as bass
import concourse.mybir as mybir
from concourse.bass_utils import run_bass_kernel_spmd

F32 = mybir.dt.float32
BF16 = mybir.dt.bfloat16
AF = mybir.ActivationFunctionType
ALU = mybir.AluOpType
AX = mybir.AxisListType

D = 1024
HID = 2816
INW = 2816
ALPHA = float(8 ** 0.25)
LN_EPS = 1e-5
RMS_EPS = 1e-6
NSLOT = 6
STREAMS = ("pe", "act", "dve", "pool", "sp")


class Buf:
    __slots__ = ("name", "lw", "rd", "excl")

    def __init__(self, name, excl=False):
        self.name = name
        self.lw = None
        self.rd = []
        self.excl = excl


class Op:
    __slots__ = ("stream", "fn", "waits", "sig", "comp", "cidx", "tag")

    def __init__(self, stream, fn, comp, cidx):
        self.stream, self.fn, self.comp, self.cidx = stream, fn, comp, cidx
        self.tag = None
        self.waits = []
        self.sig = False


class Sched:
    def __init__(self):
        self.ops = {s: [] for s in STREAMS}
        self.comp_ops = {}
        self.seen = {s: {} for s in STREAMS}
        self.nops = 0
        self.tag = "init"
        self.annotate = False

    def _add(self, stream, fn, reads, writes, dkey=None):
        comp = stream if dkey is None else "d:" + dkey
        lst = self.comp_ops.setdefault(comp, [])
        op = Op(stream, fn, comp, len(lst))
        op.tag = self.tag
        if dkey is not None:
            op.sig = True
        lst.append(op)
        me = (comp, op.cidx)
        deps = {}

        def need(d):
            if d is None:
                return
            e, i = d
            if e == "pe" and comp == "pe":
                return
            if deps.get(e, -1) < i:
                deps[e] = i

        for b in reads:
            need(b.lw)
            if b.excl:
                for r in b.rd:
                    if r[0] != comp:
                        need(r)
        for b in writes:
            need(b.lw)
            for r in b.rd:
                need(r)
        if dkey is not None and op.cidx > 0:
            need((comp, op.cidx - 1))
        seen = self.seen[stream]
        for e, i in deps.items():
            if seen.get(e, -1) >= i:
                continue
            seen[e] = i
            op.waits.append((e, i))
            self.comp_ops[e][i].sig = True
        for b in reads:
            if len(b.rd) > 64:
                last = {}
                for r in b.rd:
                    if last.get(r[0], -1) < r[1]:
                        last[r[0]] = r[1]
                b.rd = list(last.items())
            b.rd.append(me)
        for b in writes:
            b.lw = me
            b.rd = []
        self.ops[stream].append(op)
        self.nops += 1
        return op

    def mm(self, out, lhsT, rhs, start, stop, reads, writes):
        return self._add("pe", ("matmul", dict(out=out, lhsT=lhsT, rhs=rhs, start=start, stop=stop)), reads, writes)

    def tr(self, out, in_, identity, reads, writes):
        return self._add("pe", ("transpose", dict(out=out, in_=in_, identity=identity)), reads, writes)

    def A(self, out, in_, func, reads, writes, **kw):
        return self._add("act", ("activation", dict(out=out, in_=in_, func=func, **kw)), reads, writes)

    def V(self, name, reads, writes, **kw):
        return self._add("dve", (name, kw), reads, writes)

    def G(self, name, reads, writes, **kw):
        return self._add("pool", (name, kw), reads, writes)

    def D(self, stream, dkey, out, in_, reads, writes, slow=False):
        kw = dict(out=out, in_=in_)
        if slow:
            kw["allow_slow_non_contiguous"] = True
        return self._add(stream, ("dma_start", kw), reads, writes, dkey=dkey)

    def emit(self, nc, final_bufs):
        self._add("sp", None, final_bufs, [])
        rank = {}
        for comp, lst in self.comp_ops.items():
            c = 0
            for op in lst:
                if op.sig:
                    c += 1
                    rank[(comp, op.cidx)] = c
        with contextlib.ExitStack() as st:
            sems = {}
            for comp in self.comp_ops:
                if any(op.sig for op in self.comp_ops[comp]):
                    sems[comp] = st.enter_context(nc.semaphore("s_" + comp.replace(":", "_")))
            block = st.enter_context(nc.Block())

            def run(stream):
                def body(eng):
                    for op in self.ops[stream]:
                        for (e, i) in op.waits:
                            eng.wait_ge(sems[e], rank[(e, i)] * (16 if e.startswith("d:") else 1))
                        if op.fn is None:
                            continue
                        ins = getattr(eng, op.fn[0])(**op.fn[1])
                        if self.annotate:
                            ins.annotate(op.tag)
                        if op.sig:
                            ins.then_inc(sems[op.comp], 16 if op.comp.startswith("d:") else 1)
                return body

            block.tensor(run("pe"))
            block.scalar(run("act"))
            block.vector(run("dve"))
            block.gpsimd(run("pool"))
            block.sync(run("sp"))
        return len(sems)


class Prog:
    def __init__(self, depth, n_ptiles, n_samp):
        self.depth, self.n_ptiles, self.n_samp = depth, n_ptiles, n_samp
        self.nc = bass.Bass("TRN2", target_bir_lowering=False)
        self.S = Sched()
        self.st = contextlib.ExitStack()
        self.rr = {}

    def dram(self, name, shape, kind):
        return self.nc.dram_tensor(name, list(shape), F32, kind=kind).ap()

    def sb(self, name, shape, dt=F32):
        return self.st.enter_context(self.nc.sbuf_tensor("sb_" + name, list(shape), dt))

    def build(self):
        nc, S, depth = self.nc, self.S, self.depth
        NP = self.n_ptiles * 512
        NS = self.n_samp * 32
        nsm = self.n_samp
        I, O = "ExternalInput", "ExternalOutput"
        dr = self.dram
        xp = dr("xp", [max(NP, 1), D], I); pp = dr("pp", [depth, max(NP, 1), 256], I)
        xs = dr("xs", [NS, D], I); ps_ = dr("ps", [depth, NS, 256], I)
        ck = dr("ck", [depth, nsm, 512, 512], I); cv = dr("cv", [depth, nsm, 512, 512], I)
        spool = dr("spool", [depth, nsm, 15, 256], I); shg = dr("shg", [depth, nsm, 4, 64, 64], I)
        Wd = {}
        for nm, shp in [("f1gu", [depth, D, 2 * HID]), ("f1d", [depth, HID, D]), ("win", [depth, D, INW]),
                        ("wout", [depth, D, D]), ("f2gu", [depth, D, 2 * HID]), ("f2d", [depth, HID, D]),
                        ("pleg", [depth, D, D]), ("plep", [depth, 256, D]), ("lng", [depth, 4, D]),
                        ("lnb", [depth, 4, D]), ("poolw", [depth, 4, 64, 64]), ("pscale", [depth, 256]),
                        ("lbraw", [depth, 256]), ("normw", [depth, 64]), ("rb34", [depth, 8, 128, 256]),
                        ("rbc", [depth, 8])]:
            Wd[nm] = dr(nm, shp, I)
        yp = dr("yp", [max(NP, 1), D], O); ys = dr("ys", [NS, D], O)
        pk2 = dr("pk", [depth * 512, 512], O); pv2 = dr("pv", [depth * 512, 512], O)
        ppool2 = dr("ppool", [depth * 15, 256], O); phg2 = dr("phg", [depth * 256, 64], O)
        sk2 = dr("sk", [depth * NS, 512], O); sv2 = dr("sv", [depth * NS, 512], O)
        spool2 = dr("spool_o", [depth * nsm * 15, 256], O); shg2 = dr("shg_o", [depth * nsm * 256, 64], O)
        pk = [pk2[l * 512:(l + 1) * 512, :] for l in range(depth)]; pv = [pv2[l * 512:(l + 1) * 512, :] for l in range(depth)]
        sk = [sk2[l * NS:(l + 1) * NS, :] for l in range(depth)]; sv = [sv2[l * NS:(l + 1) * NS, :] for l in range(depth)]
        ppool = [ppool2[l * 15:(l + 1) * 15, :] for l in range(depth)]
        phg = [phg2[l * 256:(l + 1) * 256, :].rearrange("(h c) v -> h c v", h=4) for l in range(depth)]
        spool_o = {(l, si): spool2[(l * nsm + si) * 15:(l * nsm + si + 1) * 15, :] for l in range(depth) for si in range(nsm)}
        shg_o = {(l, si): shg2[(l * nsm + si) * 256:(l * nsm + si + 1) * 256, :].rearrange("(h c) v -> h c v", h=4) for l in range(depth) for si in range(nsm)}
        outbufs = []

        def outbuf(name):
            b = Buf(name)
            outbufs.append(b)
            return b

        sb = self.sb
        x = sb("x", [128, 4, D]); Bx = [Buf(f"x{t}") for t in range(4)]
        xT = sb("xT", [128, 8, 512], BF16); BxT = [Buf(f"xT{t}") for t in range(4)]
        hT = sb("hT", [128, 12, 512], BF16); BhT = [Buf(f"hT{j}") for j in range(12)]
        ring = [sb(f"ring{i}", [128, 4096], BF16) for i in range(NSLOT)]
        Bring = [Buf(f"ring{i}") for i in range(NSLOT)]
        gb = [sb(f"gb{i}", [128, 2, D]) for i in range(2)]; Bgb = [Buf(f"gb{i}") for i in range(2)]
        ident = sb("ident", [128, 128]); identb = sb("identb", [128, 128], BF16); Bid = Buf("ident")
        tri = sb("tri", [64, 64]); Btri = Buf("tri")
        resetm = sb("resetm", [64, 2, 128]); Bresetm = Buf("resetm")
        ptmp = sb("ptmp", [128, 16]); Bptmp = Buf("ptmp")
        mhalf = sb("mhalf", [128, 1]); Bmh = Buf("mhalf")
        sgb = [sb(f"sg{i}", [128, 512]) for i in range(2)]; Bsg = [Buf(f"sg{i}") for i in range(2)]
        stt = [sb(f"stt{i}", [128, 24]) for i in range(4)]; Bstt = [Buf(f"stt{i}") for i in range(4)]
        kc_l = [sb(f"kcar{l}", [128, 4, 512], BF16) for l in range(depth)]
        vc_l = [sb(f"vcar{l}", [128, 4, 8, 65], BF16) for l in range(depth)]
        Bkc = [[Buf(f"kc{l}_{t}") for t in range(4)] for l in range(depth)]
        Bvc = [[Buf(f"vc{l}_{t}") for t in range(4)] for l in range(depth)]
        kcur = sb("kcur", [128, 4, 512], BF16); Bkcur = Buf("kcur")
        vcur = sb("vcur", [128, 4, 8, 65], BF16); Bvcur = [Buf(f"vcur{t}") for t in range(4)]
        qT = sb("qT", [128, 4, 512], BF16); BqT = Buf("qT")
        eb34 = sb("eb34", [128, 8, 256]); Beb = Buf("eb34")
        cbt = sb("cbt", [128, depth, 8]); Bcb = Buf("cbt")
        PT = [sb(f"PT{i}", [128, 640], BF16) for i in range(3)]; BPT = [Buf(f"PT{i}") for i in range(3)]
        etmp = [sb(f"etmp{i}", [128, 256]) for i in range(2)]; Bet = [Buf(f"etmp{i}") for i in range(2)]
        oa = sb("oa", [128, 512], BF16); Boa = Buf("oa")
        rden = sb("rden", [128, 8]); Brd = Buf("rden")
        kvst = sgb; Bkvst = Bsg
        uT = sb("uT", [128, 2, 144]); BuT = Buf("uT")
        sA = sb("sA", [128, 2, 144]); BsA = Buf("sA")
        sBb = sb("sBb", [128, 2, 144]); BsB = Buf("sBb")
        dT = sb("dT", [128, 2, 128], BF16); BdT = Buf("dT")
        ucar = sb("ucar", [128, depth, 2, 16]); Bucar = [Buf(f"ucar{l}") for l in range(depth)]
        wpool = sb("wpool", [128, depth, 2, 128], BF16); Bwp = Buf("wpool")
        pscale = sb("pscale", [128, depth, 2]); Bps = Buf("pscale")
        rc16 = sb("rc16", [128, 2, 16]); Brc = Buf("rc16")
        pst = sb("pst", [16, 256]); Bpst = Buf("pst")
        ptok = sb("ptok", [16, 256]); Bptok = Buf("ptok")
        lbt = sb("lbt", [64, depth, 4]); omlt = sb("omlt", [64, depth, 4]); Blb = Buf("lb")
        lbtmp = sb("lbtmp", [64, depth, 4])
        lbsum = sb("lbsum", [64, 4])
        normwb = sb("normwb", [64, depth, 64]); Bnw = Buf("normw")
        Sst = sb("Sst", [64, depth, 4, 64]); BSst = [Buf(f"Sst{l}") for l in range(depth)]
        QS = sb("QS", [64, 4, 128]); BQS = Buf("QS")
        Fb = sb("Fb", [64, 4, 128]); BF_ = Buf("F")
        KIN = sb("KIN", [64, 4, 128]); BKIN = Buf("KIN")
        Bc = sb("Bc", [64, 4, 128]); BBc = Buf("Bc")
        EQ = sb("EQ", [64, 4, 128]); BEQ = Buf("EQ")
        EK = sb("EK", [64, 4, 128]); BEK = Buf("EK")
        QTt = sb("QTt", [64, 4, 128], BF16); BQT = Buf("QTt")
        KTt = sb("KTt", [64, 4, 128], BF16); BKT = Buf("KTt")
        EBM = sb("EBM", [64, 4, 8]); BEBM = Buf("EBM")
        EBL = sb("EBL", [64, 4, 8]); BEBL = Buf("EBL")
        VTOK = [sb(f"VTOK{i}", [64, 256], BF16) for i in range(4)]; BVT = [Buf(f"VTOK{i}") for i in range(4)]
        G2 = [sb(f"G2{i}", [64, 256]) for i in range(4)]; BG2 = [Buf(f"G2{i}") for i in range(4)]
        KTOK = sb("KTOK", [64, 4, 64], BF16); BKTOK = Buf("KTOK")
        SM = sb("SM", [64, 4, 64], BF16); BSM = Buf("SM")
        Sbf = sb("Sbf", [64, 4, 64], BF16); BSbf = Buf("Sbf")
        T1 = sb("T1", [64, 4, 64]); BT1 = Buf("T1")
        osq = sb("osq", [64, 256]); Bosq = Buf("osq")
        oss = sb("oss", [64, 8]); Boss = Buf("oss")
        octok = sb("octok", [64, 256], BF16); Boc = Buf("octok")
        hst = sb("hst", [64, 256]); Bhst = Buf("hst")
        ptokp = sb("ptokp", [128, 1, 256]); Bptokp = Buf("ptokp")
        pTt = sb("pTt", [128, 2, 512], BF16); BpT = Buf("pTt")
        etile = sgb; Bet2 = Bsg
        ckst = oa; Bckst = Boa
        pb = [self.st.enter_context(nc.psum_tensor(f"pb{i}", [128, 512], F32)) for i in range(8)]
        Bpb = [Buf(f"pb{i}", excl=True) for i in range(8)]
        pbb = [p.bitcast(BF16) for p in pb]

        def rr(key, n):
            v = self.rr.get(key, 0)
            self.rr[key] = v + 1
            return v % n

        plan = []
        consumed = [0]
        issued = [0]

        def layer_plan(l):
            p = []
            for f in ("f1", "f2"):
                q = []
                for g in (0, 1):
                    for b in ((0, 1, 2) if g == 0 else (3, 4, 5)):
                        nc_ = 512 if b < 5 else 256
                        q.append(("k", f + "gu", l, b * 512, nc_))
                        q.append(("k", f + "gu", l, HID + b * 512, nc_))
                    for b in ((0, 1, 2) if g == 0 else (3, 4, 5)):
                        q.append(("j", f + "d", l, b * 512, 4 if b < 5 else 2))
                if f == "f1":
                    p += q
                    for b in range(6):
                        p.append(("k", "win", l, b * 512, 512 if b < 5 else 256))
                    p.append(("k", "wout", l, 0, 512)); p.append(("k", "wout", l, 512, 512))
                else:
                    p += q
                    p.append(("k", "pleg", l, 0, 512)); p.append(("k", "pleg", l, 512, 512))
                    p.append(("j", "plep", l, 0, 2))
            return p

        ntiles_total = self.n_ptiles + (1 if self.n_samp else 0)
        for _ in range(ntiles_total):
            for l in range(depth):
                plan.extend(layer_plan(l))

        def issue_upto(n):
            while issued[0] < min(n, len(plan)):
                i = issued[0]
                kind, nm, l, a0, n_ = plan[i]
                s = i % NSLOT
                if kind == "k":
                    src = Wd[nm][l][:, a0:a0 + n_].rearrange("(k p) n -> p k n", p=128)
                    dst = ring[s][:, 0:8 * n_].rearrange("p (k n) -> p k n", k=8)
                else:
                    src = Wd[nm][l][a0:a0 + n_ * 128, :].rearrange("(j p) n -> p j n", p=128)
                    dst = ring[s][:, 0:n_ * 1024].rearrange("p (j n) -> p j n", j=n_)
                S.D("pool", f"ring{s}", dst, src, [], [Bring[s]])
                issued[0] += 1

        def wnext(kind, nm, l, a0, n_):
            i = consumed[0]
            assert plan[i] == (kind, nm, l, a0, n_), (plan[i], (kind, nm, l, a0, n_))
            issue_upto(i + NSLOT - 2)
            consumed[0] += 1
            s = i % NSLOT
            if kind == "k":
                return ring[s][:, 0:8 * n_].rearrange("p (k n) -> p k n", k=8), Bring[s]
            return ring[s][:, 0:n_ * 1024].rearrange("p (j n) -> p j n", j=n_), Bring[s]

        mm, tr, A, V, G, Dm = S.mm, S.tr, S.A, S.V, S.G, S.D
        G("memset", [], [Bid], ap=ident[:], constant=1.0)
        G("affine_select", [Bid], [Bid], out=ident[:], in_=ident[:], pattern=[[-1, 128]], compare_op=ALU.is_equal, fill=0.0, base=0, channel_multiplier=1)
        V("tensor_copy", [Bid], [Bid], out=identb[:], in_=ident[:])
        G("memset", [], [Btri], ap=tri[:], constant=1.0)
        G("affine_select", [Btri], [Btri], out=tri[:], in_=tri[:], pattern=[[1, 64]], compare_op=ALU.is_ge, fill=0.0, base=0, channel_multiplier=-1)
        G("memset", [], [Bresetm], ap=resetm[:], constant=1.0)
        for blk in range(2):
            G("memset", [Bresetm], [Bresetm], ap=resetm[:, 0, blk * 64:blk * 64 + 1], constant=0.0)
        for blk in range(4):
            G("memset", [Bresetm], [Bresetm], ap=resetm[:, 1, blk * 32:blk * 32 + 1], constant=0.0)
        for blk in range(8):
            pass
        G("memset", [], [Bmh], ap=mhalf[:], constant=-0.5)
        G("memset", [], [Bptok], ap=ptok[:], constant=0.0)
        for l in range(depth):
            G("memset", [], Bvc[l], ap=vc_l[l][:], constant=1.0)
            G("memset", [], Bkc[l], ap=kc_l[l][:], constant=0.0)
        G("memset", [], Bvcur, ap=vcur[:], constant=1.0)
        G("memset", [], [Brc], ap=rc16[:], constant=1.0)
        for c in range(2):
            for hf in range(2):
                w = (2, 4, 8, 16)[c * 2 + hf]
                for t in range(1, 16):
                    val = 1.0 / min(t + 1, w)
                    if t < w:
                        hi = 16 if t == w - 1 else t + 1
                        G("memset", [Brc], [Brc], ap=rc16[hf * 64:hf * 64 + 64, c, t:hi], constant=val)
        G("memset", [], [Bwp], ap=wpool[:], constant=0.0)
        for l in range(depth):
            for g in range(4):
                c, hf = g // 2, g % 2
                Dm("pool", "cstw", wpool[hf * 64:hf * 64 + 64, l, c, hf * 64:hf * 64 + 64], Wd["poolw"][l, g], [Bwp], [Bwp])
        with nc.allow_non_contiguous_dma(reason="tiny constant loads"):
            for l in range(depth):
                Dm("sp", "cst", pscale[:, l, :], Wd["pscale"][l].rearrange("(c p) -> p c", p=128), [], [Bps], slow=True)
                Dm("sp", "cst", lbtmp[:, l, :], Wd["lbraw"][l].rearrange("(h c) -> c h", c=64), [], [Blb], slow=True)
                Dm("sp", "cst", cbt[:, l, :], Wd["rbc"][l].partition_broadcast(128), [], [Bcb])
                Dm("sp", "cst", normwb[:, l, :], Wd["normw"][l].partition_broadcast(64), [], [Bnw])
        A(lbtmp[:], lbtmp[:], AF.Exp, [Blb], [Blb])
        V("tensor_copy", [Blb], [Blb], out=lbsum[:], in_=lbtmp[:, 0, :])
        for l in range(1, depth):
            V("tensor_tensor", [Blb], [Blb], out=lbsum[:], in0=lbsum[:], in1=lbtmp[:, l, :], op=ALU.add)
        V("reciprocal", [Blb], [Blb], out=lbsum[:], in_=lbsum[:])
        V("memset", [Blb], [Blb], ap=lbt[:, 0, :], constant=0.0)
        for l in range(1, depth):
            V("tensor_tensor", [Blb], [Blb], out=lbtmp[:, l, :], in0=lbtmp[:, l, :], in1=lbsum[:], op=ALU.mult)
            V("tensor_tensor", [Blb], [Blb], out=lbt[:, l, :], in0=lbt[:, l - 1, :], in1=lbtmp[:, l, :], op=ALU.add)
        V("tensor_scalar", [Blb], [Blb], out=omlt[:], in0=lbt[:], scalar1=-1.0, scalar2=1.0, op0=ALU.mult, op1=ALU.add)

        def load_gb(l, i):
            s = rr("gb", 2)
            Dm("sp", f"gb{s}", gb[s][:, 0, :], Wd["lng"][l, i].partition_broadcast(128), [], [Bgb[s]])
            Dm("sp", f"gb{s}", gb[s][:, 1, :], Wd["lnb"][l, i].partition_broadcast(128), [], [Bgb[s]])
            return s

        def transposes(tt, ntok):
            S.tag = "tr"
            for half in range(2):
                b = 6 + half
                for k4 in range(4):
                    kc = half * 4 + k4
                    tr(pb[b][:, k4 * 128:k4 * 128 + ntok], x[0:ntok, tt, kc * 128:(kc + 1) * 128], ident[0:ntok, 0:ntok], [Bx[tt], Bid], [Bpb[b]])
                src = pb[b][:].rearrange("p (k n) -> p k n", k=4)[:, :, 0:ntok]
                dst = xT[:, half * 4:half * 4 + 4, tt * 128:tt * 128 + ntok]
                if half == 0:
                    A(dst, src, AF.Copy, [Bpb[b]], [BxT[tt]])
                else:
                    V("tensor_copy", [Bpb[b]], [BxT[tt]], out=dst, in_=src)

        def post(l, i, tt, ntok, gslot):
            post_a(l, i, tt, ntok, gslot)
            post_b(l, i, tt, ntok, gslot)

        def post_a(l, i, tt, ntok, gslot):
            s = tt
            S.tag = "ln"
            xt = x[0:ntok, tt, :]
            st_ = stt[s]
            V("bn_stats", [Bx[tt]], [Bstt[s]], out=st_[0:ntok, 0:6], in_=x[0:ntok, tt, 0:512])
            V("bn_stats", [Bx[tt]], [Bstt[s]], out=st_[0:ntok, 6:12], in_=x[0:ntok, tt, 512:1024])
            V("bn_aggr", [Bstt[s]], [Bstt[s]], out=st_[0:ntok, 12:14], in_=st_[0:ntok, 0:12])
            V("tensor_scalar", [Bstt[s]], [Bstt[s]], out=st_[0:ntok, 14:15], in0=st_[0:ntok, 13:14], scalar1=LN_EPS, scalar2=None, op0=ALU.add)
            A(st_[0:ntok, 17:18], st_[0:ntok, 14:15], AF.Sqrt, [Bstt[s]], [Bstt[s]])

        def post_b(l, i, tt, ntok, gslot):
            s = tt
            S.tag = "ln"
            xt = x[0:ntok, tt, :]
            st_ = stt[s]
            V("reciprocal", [Bstt[s]], [Bstt[s]], out=st_[0:ntok, 15:16], in_=st_[0:ntok, 17:18])
            V("scalar_tensor_tensor", [Bstt[s]], [Bstt[s]], out=st_[0:ntok, 16:17], in0=st_[0:ntok, 12:13], scalar=-1.0, in1=st_[0:ntok, 15:16], op0=ALU.mult, op1=ALU.mult)
            A(xt, xt, AF.Identity, [Bx[tt], Bstt[s]], [Bx[tt]], bias=st_[0:ntok, 16:17], scale=st_[0:ntok, 15:16])
            V("tensor_tensor", [Bx[tt], Bgb[gslot]], [Bx[tt]], out=xt, in0=xt, in1=gb[gslot][0:ntok, 0, :], op=ALU.mult)
            V("tensor_tensor", [Bx[tt], Bgb[gslot]], [Bx[tt]], out=xt, in0=xt, in1=gb[gslot][0:ntok, 1, :], op=ALU.add)

        class LNPipe:
            def __init__(self, l, i, gslot):
                self.l, self.i, self.g = l, i, gslot
                self.prev = None
                self.dl = Delayed()
            def before_acc(self):
                self.dl.flush()
            def step(self, tt, ntok):
                post_a(self.l, self.i, tt, ntok, self.g)
                if self.prev is not None:
                    post_b(self.l, self.i, *self.prev, self.g)
                    self.dl.pending = self.prev
                self.prev = (tt, ntok)
            def finish(self):
                if self.prev is not None:
                    post_b(self.l, self.i, *self.prev, self.g)
                    self.dl.pending = self.prev
                    self.prev = None
                self.dl.flush(final=True)

        class Delayed:
            DEPTH = 2
            def __init__(self):
                self.q = []
            def flush(self, final=False):
                while self.q and (final or len(self.q) >= self.DEPTH):
                    transposes(*self.q.pop(0))
            @property
            def pending(self):
                return None
            @pending.setter
            def pending(self, v):
                self.q.append(v)

        def ffn(l, f, lni, T, tiles):
            gslot = load_gb(l, lni)
            for g in (0, 1):
                blks = (0, 1, 2) if g == 0 else (3, 4, 5)
                jj = 0
                for b in blks:
                    ncols = 512 if b < 5 else 256
                    S.tag = "ffn.up"
                    wg, Bg = wnext("k", f + "gu", l, b * 512, ncols)
                    wu, Bu = wnext("k", f + "gu", l, HID + b * 512, ncols)
                    for cc in range(ncols // 128):
                        pg = rr("pg", 2); pu = 2 + rr("pu", 2)
                        for kc in range(8):
                            mm(pb[pg][:, 0:T], wg[:, kc, cc * 128:(cc + 1) * 128], xT[:, kc, 0:T], kc == 0, kc == 7, [Bg] + BxT, [Bpb[pg]])
                        for kc in range(8):
                            mm(pb[pu][:, 0:T], wu[:, kc, cc * 128:(cc + 1) * 128], xT[:, kc, 0:T], kc == 0, kc == 7, [Bu] + BxT, [Bpb[pu]])
                        sgi = rr("sg", 2)
                        A(sgb[sgi][:, 0:T], pb[pg][:, 0:T], AF.Silu, [Bpb[pg]], [Bsg[sgi]])
                        V("scalar_tensor_tensor", [Bpb[pu], Bsg[sgi]], [BhT[jj]], out=hT[:, jj, 0:T], in0=pb[pu][:, 0:T], scalar=0.5, in1=sgb[sgi][:, 0:T], op0=ALU.mult, op1=ALU.mult)
                        jj += 1
                nj = jj
                wds = []
                S.tag = "ffn.down"
                for b in blks:
                    wds.append(wnext("j", f + "d", l, b * 512, 4 if b < 5 else 2))
                lp = LNPipe(l, lni, gslot)
                for (tt, ntok) in tiles:
                    S.tag = "ffn.down"
                    pys = []
                    for nh in range(2):
                        py = 4 + rr("py", 2)
                        pys.append(py)
                        for j in range(nj):
                            wd, Bw = wds[j // 4]
                            mm(pb[py][0:ntok, :], hT[:, j, tt * 128:tt * 128 + ntok], wd[:, j % 4, nh * 512:(nh + 1) * 512], j == 0, j == nj - 1, [BhT[j], Bw], [Bpb[py]])
                    lp.before_acc()
                    for nh in range(2):
                        py = pys[nh]
                        xs_ = x[0:ntok, tt, nh * 512:(nh + 1) * 512]
                        if g == 0:
                            V("scalar_tensor_tensor", [Bx[tt], Bpb[py]], [Bx[tt]], out=xs_, in0=xs_, scalar=ALPHA, in1=pb[py][0:ntok, :], op0=ALU.mult, op1=ALU.add)
                        else:
                            V("tensor_tensor", [Bx[tt], Bpb[py]], [Bx[tt]], out=xs_, in0=xs_, in1=pb[py][0:ntok, :], op=ALU.add)
                    if g == 1:
                        lp.step(tt, ntok)
                lp.finish()

        def attention(l, q0, nq, ktiles, mix_c0, mask_first, mask_last):
            nt = len(ktiles)
            S.tag = "attn"
            if nq == 128 and len(ktiles) >= 2 and all(kt[5] == 128 for kt in ktiles) and ktiles[-2][0] == "3" and ktiles[-1][0] == "4":
                return attention_fast(l, q0, ktiles, mix_c0, mask_first)
            offs = [i * 128 for i in range(nt)]
            for h in range(8):
                c, hb = h // 2, (h % 2) * 64
                sbank = 2 * rr("sbank", 2)
                pt = rr("PT", 3)
                for i, (kind, kf, Bk, vap, Bv, nk) in enumerate(ktiles):
                    bank = sbank + (offs[i] // 512)
                    col = offs[i] % 512
                    kap = kf(c)
                    mm(pb[bank][0:nk, col:col + nq], kap[hb:hb + 64, 0:nk], qT[hb:hb + 64, c, q0:q0 + nq], True, True, [Bk, BqT], [Bpb[bank]])
                for i, (kind, kf, Bk, vap, Bv, nk) in enumerate(ktiles):
                    bank = sbank + (offs[i] // 512)
                    col = offs[i] % 512
                    o_ = offs[i]
                    if kind == "c":
                        A(PT[pt][0:nk, o_:o_ + nq], pb[bank][0:nk, col:col + nq], AF.Exp, [Bpb[bank], Bcb], [BPT[pt]], bias=cbt[0:nk, l, h:h + 1], scale=0.125)
                        if i == 0 and mask_first:
                            V("memset", [BPT[pt]], [BPT[pt]], ap=PT[pt][0:64, o_ + 64:o_ + 128], constant=0.0)
                    else:
                        et = rr("etmp", 2)
                        eo = 0 if kind == "3" else 128
                        A(etmp[et][0:nk, 0:nq], pb[bank][0:nk, col:col + nq], AF.Exp, [Bpb[bank]], [Bet[et]], scale=0.125)
                        V("tensor_tensor", [Bet[et], Beb], [BPT[pt]], out=PT[pt][0:nk, o_:o_ + nq], in0=etmp[et][0:nk, 0:nq], in1=eb34[0:nk, h, eo:eo + nq], op=ALU.mult)
                        if kind == "4" and mask_last:
                            V("memset", [BPT[pt]], [BPT[pt]], ap=PT[pt][64:128, o_:o_ + 64], constant=0.0)
                ob = 4 + h // 4
                for i, (kind, kf, Bk, vap, Bv, nk) in enumerate(ktiles):
                    o_ = offs[i]
                    mm(pb[ob][0:nq, (h % 4) * 65:(h % 4) * 65 + 65], PT[pt][0:nk, o_:o_ + nq], vap[0:nk, h, :], i == 0, i == nt - 1, [BPT[pt], Bv], [Bpb[ob]])
            for hh in range(2):
                ob = 4 + hh
                pv3 = pb[ob][0:nq, 0:260].rearrange("p (h d) -> p h d", h=4)
                V("reciprocal", [Bpb[ob]], [Brd], out=rden[0:nq, hh * 4:hh * 4 + 4], in_=pv3[:, :, 64])
                V("tensor_tensor", [Bpb[ob], Brd], [Boa], out=oa[0:nq, hh * 256:(hh + 1) * 256].rearrange("p (h d) -> p h d", h=4), in0=pv3[:, :, 0:64],
                  in1=rden[0:nq, hh * 4:hh * 4 + 4].unsqueeze(2).to_broadcast([nq, 4, 64]), op=ALU.mult)
            for c in range(4):
                tr(pbb[6][:, c * 128:c * 128 + nq], oa[0:nq, c * 128:(c + 1) * 128], identb[0:nq, 0:nq], [Boa, Bid], [Bpb[6]])
            A(hT[:, 0:4, mix_c0:mix_c0 + nq], pbb[6][:, 0:512].rearrange("p (c n) -> p c n", c=4)[:, :, 0:nq], AF.Copy, [Bpb[6]], BhT[0:4])

        def attention_fast(l, q0, ktiles, mix_c0, mask_first):
            nq = 128
            ncst = len(ktiles) - 2
            cts, t3, t4 = ktiles[:ncst], ktiles[-2], ktiles[-1]
            order = list(cts) + [t3, t4]
            pcol = [i * 128 for i in range(ncst)] + [384, 512]
            st = {}

            def issue_S(h):
                c, hb = h // 2, (h % 2) * 64
                sbank = 2 * rr("sbank", 2)
                for i, (kind, kf, Bk, vap, Bv, nk) in enumerate(cts):
                    mm(pb[sbank][:, i * 128:(i + 1) * 128], kf(c)[hb:hb + 64, :], qT[hb:hb + 64, c, q0:q0 + nq], True, True, [Bk, BqT], [Bpb[sbank]])
                for i, (kind, kf, Bk, vap, Bv, nk) in enumerate((t3, t4)):
                    mm(pb[sbank + 1][:, i * 128:(i + 1) * 128], kf(c)[hb:hb + 64, :], qT[hb:hb + 64, c, q0:q0 + nq], True, True, [Bk, BqT], [Bpb[sbank + 1]])
                st[h] = sbank

            def issue_E(h):
                sbank = st[h]
                pt = rr("PT", 3)
                if ncst:
                    A(PT[pt][:, 0:ncst * 128], pb[sbank][:, 0:ncst * 128], AF.Exp, [Bpb[sbank], Bcb], [BPT[pt]], bias=cbt[:, l, h:h + 1], scale=0.125)
                    if mask_first:
                        V("memset", [BPT[pt]], [BPT[pt]], ap=PT[pt][0:64, 64:128], constant=0.0)
                et = rr("etmp", 2)
                A(etmp[et][:, 0:256], pb[sbank + 1][:, 0:256], AF.Exp, [Bpb[sbank + 1]], [Bet[et]], scale=0.125)
                V("tensor_tensor", [Bet[et], Beb], [BPT[pt]], out=PT[pt][:, 384:640], in0=etmp[et][:, 0:256], in1=eb34[:, h, 0:256], op=ALU.mult)
                st[h] = pt

            def issue_PV(h):
                pt = st[h]
                ob = 4 + h // 4
                for i, (kind, kf, Bk, vap, Bv, nk) in enumerate(order):
                    mm(pb[ob][:, (h % 4) * 65:(h % 4) * 65 + 65], PT[pt][:, pcol[i]:pcol[i] + nq], vap[:, h, :], i == 0, i == len(order) - 1, [BPT[pt], Bv], [Bpb[ob]])

            issue_S(0)
            for h in range(8):
                if h + 1 < 8:
                    issue_S(h + 1)
                issue_E(h)
                issue_PV(h)
            for hh in range(2):
                ob = 4 + hh
                pv3 = pb[ob][:, 0:260].rearrange("p (h d) -> p h d", h=4)
                V("reciprocal", [Bpb[ob]], [Brd], out=rden[:, hh * 4:hh * 4 + 4], in_=pv3[:, :, 64])
                V("tensor_tensor", [Bpb[ob], Brd], [Boa], out=oa[:, hh * 256:(hh + 1) * 256].rearrange("p (h d) -> p h d", h=4), in0=pv3[:, :, 0:64],
                  in1=rden[:, hh * 4:hh * 4 + 4].unsqueeze(2).to_broadcast([128, 4, 64]), op=ALU.mult)
            for c in range(4):
                tr(pbb[6][:, c * 128:(c + 1) * 128], oa[:, c * 128:(c + 1) * 128], identb[:], [Boa, Bid], [Bpb[6]])
            A(hT[:, 0:4, mix_c0:mix_c0 + nq], pbb[6][:, 0:512].rearrange("p (c n) -> p c n", c=4), AF.Copy, [Bpb[6]], BhT[0:4])

        def pool_group(l, t0, GT, first_of_seq, hist_dram, out_dram, mixc0, w3):
            S.tag = "pool"
            if hist_dram is not None:
                Dm("sp", "ptok", ptok[0:15, :], hist_dram, [], [Bptok])
                for c in range(2):
                    tr(pb[7][:, c * 16:c * 16 + 16], ptok[0:16, c * 128:(c + 1) * 128], ident[0:16, 0:16], [Bptok, Bid], [Bpb[7]])
                V("tensor_copy", [Bpb[7]], [BuT], out=uT[:, :, 1:16], in_=pb[7][:, 0:32].rearrange("p (c n) -> p c n", c=2)[:, :, 0:15])
            else:
                V("tensor_copy", [Bucar[l]], [BuT], out=uT[:, :, 0:16], in_=ucar[:, l, :, :])
            wv, Bw = w3
            for c in range(2):
                bank = rr("pg", 2)
                for kc in range(8):
                    mm(pb[bank][:, 0:GT], wv[:, kc, c * 128:(c + 1) * 128], xT[:, kc, t0:t0 + GT], kc == 0, kc == 7, [Bw] + BxT, [Bpb[bank]])
                A(uT[:, c, 16:16 + GT], pb[bank][:, 0:GT], AF.Copy, [Bpb[bank]], [BuT])
            n = 16 + GT
            V("tensor_tensor", [BuT], [BsA], out=sA[:, :, 2:n], in0=uT[:, :, 2:n], in1=uT[:, :, 1:n - 1], op=ALU.add)
            V("tensor_tensor", [BsA], [BsB], out=sBb[:, :, 4:n], in0=sA[:, :, 4:n], in1=sA[:, :, 2:n - 2], op=ALU.add)

            def dcalc(src, Bsrc, c, hf, w):
                lo, hi = hf * 64, hf * 64 + 64
                V("scalar_tensor_tensor", [Bsrc, BuT], [BdT], out=dT[lo:hi, c, 0:GT], in0=src[lo:hi, c, 16:n], scalar=1.0 / w, in1=uT[lo:hi, c, 16:n], op0=ALU.mult, op1=ALU.subtract)
                if first_of_seq:
                    V("tensor_tensor", [Bsrc, Brc], [Bptmp], out=ptmp[lo:hi, 0:16], in0=src[lo:hi, c, 16:32], in1=rc16[lo:hi, c, :], op=ALU.mult)
                    V("tensor_tensor", [Bptmp, BuT], [BdT], out=dT[lo:hi, c, 0:16], in0=ptmp[lo:hi, 0:16], in1=uT[lo:hi, c, 16:32], op=ALU.subtract)

            dcalc(sA, BsA, 0, 0, 2)
            dcalc(sBb, BsB, 0, 1, 4)
            V("tensor_tensor", [BsB], [BsA], out=sA[:, :, 8:n], in0=sBb[:, :, 8:n], in1=sBb[:, :, 4:n - 4], op=ALU.add)
            dcalc(sA, BsA, 1, 0, 8)
            V("tensor_tensor", [BsA], [BsB], out=sBb[:, :, 16:n], in0=sA[:, :, 16:n], in1=sA[:, :, 8:n - 8], op=ALU.add)
            dcalc(sBb, BsB, 1, 1, 16)
            return lambda: pool_part2(l, GT, hist_dram, out_dram, mixc0, n)

        def pool_part2(l, GT, hist_dram, out_dram, mixc0, n):
            S.tag = "pool"
            for c in range(2):
                bank = rr("pg", 2)
                mm(pb[bank][:, 0:GT], wpool[:, l, c, :], dT[:, c, 0:GT], True, True, [Bwp, BdT], [Bpb[bank]])
                A(hT[:, 4 + c, mixc0:mixc0 + GT], pb[bank][:, 0:GT], AF.Copy, [Bpb[bank], Bps], [BhT[4 + c]], scale=pscale[:, l, c:c + 1])
            if out_dram is not None:
                for c in range(2):
                    tr(pb[7][0:16, 128 + c * 128:256 + c * 128], uT[:, c, n - 16:n], ident[:], [BuT, Bid], [Bpb[7]])
                V("tensor_copy", [Bpb[7]], [Bpst], out=pst[0:16, :], in_=pb[7][0:16, 128:384])
                Dm("sp", "pst", out_dram, pst[1:16, :], [Bpst], [outbuf("o")])
            if hist_dram is None:
                V("tensor_copy", [BuT], [Bucar[l]], out=ucar[:, l, :, :], in_=uT[:, :, GT:GT + 16])

        hg = {"pending": None, "gcount": 0}

        def hgrn_drain():
            if hg["pending"] is not None:
                hg["pending"]()
                hg["pending"] = None

        def hgrn_group(l, t0, GT, L, w3, w4, w5, state_in, state_out, mixc0, chain, after_proj=None):
            nblk = GT // L
            gbase = 0 if nblk == 4 else 2 * (hg["gcount"] % 2)
            hg["gcount"] += 1
            S.tag = "hgrn.proj"
            (wv3, Bw3), (wv4, Bw4), (wv5, Bw5) = w3, w4, w5
            rmask = resetm[:, 0, 0:GT] if L == 64 else resetm[:, 1, 0:GT]
            for h in range(4):
                bank = rr("pg", 2)
                for kc in range(8):
                    mm(pb[bank][0:64, 0:GT], wv3[:, kc, 256 + h * 64:256 + h * 64 + 64], xT[:, kc, t0:t0 + GT], kc == 0, kc == 7, [Bw3] + BxT, [Bpb[bank]])
                A(QS[:, h, 0:GT], pb[bank][0:64, 0:GT], AF.Silu, [Bpb[bank]], [BQS])
            for h in range(4):
                bank = rr("pg", 2)
                for kc in range(8):
                    mm(pb[bank][0:64, 0:GT], wv4[:, kc, h * 64:h * 64 + 64], xT[:, kc, t0:t0 + GT], kc == 0, kc == 7, [Bw4] + BxT, [Bpb[bank]])
                A(Fb[:, h, 0:GT], pb[bank][0:64, 0:GT], AF.Sigmoid, [Bpb[bank]], [BF_])
            for blk in range(nblk):
                tb = t0 + blk * L
                bank = 2 + rr("pu", 2)
                for kc in range(8):
                    mm(pb[bank][0:L, 0:256], xT[:, kc, tb:tb + L], wv4[:, kc, 256:512], kc == 0, kc == 7, [Bw4] + BxT, [Bpb[bank]])
                for kc in range(8):
                    mm(pb[bank][0:L, 256:512], xT[:, kc, tb:tb + L], wv5[:, kc, 0:256], kc == 0, kc == 7, [Bw5] + BxT, [Bpb[bank]])
                V("tensor_copy", [Bpb[bank]], [BVT[gbase + blk]], out=VTOK[gbase + blk][0:L, :], in_=pb[bank][0:L, 0:256])
                A(G2[gbase + blk][0:L, :], pb[bank][0:L, 256:512], AF.Silu, [Bpb[bank]], [BG2[gbase + blk]])
                V("tensor_tensor", [BG2[gbase + blk], Bnw], [BG2[gbase + blk]], out=G2[gbase + blk][0:L, :].rearrange("p (h d) -> p h d", h=4), in0=G2[gbase + blk][0:L, :].rearrange("p (h d) -> p h d", h=4), in1=normwb[0:L, l, :].unsqueeze(1).to_broadcast([L, 4, 64]), op=ALU.mult)
            if after_proj is not None:
                after_proj()
            S.tag = "hgrn.gate"
            V("tensor_tensor", [BF_, Blb], [BF_], out=Fb[:, :, 0:GT], in0=Fb[:, :, 0:GT], in1=omlt[:, l, :].unsqueeze(2).to_broadcast([64, 4, GT]), op=ALU.mult)
            V("tensor_tensor", [BF_, Blb], [BF_], out=Fb[:, :, 0:GT], in0=Fb[:, :, 0:GT], in1=lbt[:, l, :].unsqueeze(2).to_broadcast([64, 4, GT]), op=ALU.add)
            V("tensor_scalar", [BF_], [BKIN], out=KIN[:, :, 0:GT], in0=Fb[:, :, 0:GT], scalar1=-1.0, scalar2=1.0, op0=ALU.mult, op1=ALU.add)
            A(Fb[:, :, 0:GT], Fb[:, :, 0:GT], AF.Ln, [BF_], [BF_])
            for h in range(4):
                V("tensor_tensor_scan", [BF_, Bresetm], [BBc], out=Bc[:, h, 0:GT], data0=rmask, data1=Fb[:, h, 0:GT], initial=0.0, op0=ALU.mult, op1=ALU.add)
            mid = L // 2 - 1
            Bc4 = Bc[:, :, 0:GT].rearrange("p h (b t) -> p h b t", t=L)
            V("tensor_tensor", [BBc, BF_], [BF_], out=Fb[:, :, 0:GT].rearrange("p h (b t) -> p (h b) t", t=L), in0=Bc[:, :, 0:GT].rearrange("p h (b t) -> p (h b) t", t=L),
              in1=Bc[:, :, 0:GT].rearrange("p h (b t) -> p (h b) t", t=L)[:, :, mid:mid + 1].to_broadcast([64, 4 * nblk, L]), op=ALU.subtract)
            A(EQ[:, :, 0:GT], Fb[:, :, 0:GT], AF.Exp, [BF_], [BEQ])
            A(EK[:, :, 0:GT], Fb[:, :, 0:GT], AF.Exp, [BF_], [BEK], scale=-1.0)
            A(EBM[:, :, 0:nblk], Bc4[:, :, :, mid], AF.Exp, [BBc], [BEBM])
            A(EBL[:, :, 0:nblk], Bc4[:, :, :, L - 1], AF.Exp, [BBc], [BEBL])
            V("tensor_tensor", [BQS, BEQ], [BQT], out=QTt[:, :, 0:GT], in0=QS[:, :, 0:GT], in1=EQ[:, :, 0:GT], op=ALU.mult)
            V("tensor_tensor", [BKIN, BEK], [BKT], out=KTt[:, :, 0:GT], in0=KIN[:, :, 0:GT], in1=EK[:, :, 0:GT], op=ALU.mult)
            EQ4 = EQ[:, :, 0:GT].rearrange("p h (b t) -> p h b t", t=L)
            S.tag = "hgrn.blk"
            for blk in range(nblk):
                c0 = blk * L
                if not chain:
                    Dm("sp", "hst", Sst[:, l, :, :], state_in[blk].rearrange("h c v -> c h v"), [], [BSst[l]])
                for h in range(4):
                    tr(pbb[7][0:L, h * 64:(h + 1) * 64], KTt[:, h, c0:c0 + L], identb[0:64, 0:64], [BKT, Bid], [Bpb[7]])
                A(KTOK[0:L, :, :], pbb[7][0:L, 0:256].rearrange("p (h c) -> p h c", h=4), AF.Copy, [Bpb[7]], [BKTOK])
                for h in range(4):
                    mm(pb[6][0:L, h * 64:h * 64 + L], KTt[:, h, c0:c0 + L], QTt[:, h, c0:c0 + L], True, True, [BKT, BQT], [Bpb[6]])
                V("tensor_tensor", [Bpb[6], Btri], [BSM], out=SM[0:L, :, 0:L], in0=pb[6][0:L, 0:256].rearrange("p (h t) -> p h t", h=4)[:, :, 0:L],
                  in1=tri[0:L, 0:L].unsqueeze(1).to_broadcast([L, 4, L]), op=ALU.mult)
                V("tensor_tensor", [BSst[l], BEBM], [BSbf], out=Sbf[:], in0=Sst[:, l, :, :], in1=EBM[:, :, blk:blk + 1].to_broadcast([64, 4, 64]), op=ALU.mult)
                ob = 4 + rr("hgo", 2)
                for h in range(4):
                    mm(pb[ob][0:L, h * 64:(h + 1) * 64], SM[0:L, h, 0:L], VTOK[gbase + blk][0:L, h * 64:(h + 1) * 64], True, False, [BSM, BVT[gbase + blk]], [Bpb[ob]])
                    mm(pb[ob][0:L, h * 64:(h + 1) * 64], QTt[:, h, c0:c0 + L], Sbf[:, h, :], False, True, [BQT, BSbf], [Bpb[ob]])
                mb = 6
                for h in range(4):
                    mm(pb[mb][0:64, 256 + h * 64:256 + (h + 1) * 64], KTOK[0:L, h, :], VTOK[gbase + blk][0:L, h * 64:(h + 1) * 64], True, True, [BKTOK, BVT[gbase + blk]], [Bpb[mb]])
                V("tensor_tensor", [Bpb[mb], BEQ], [BT1], out=T1[:], in0=pb[mb][0:64, 256:512].rearrange("p (h v) -> p h v", h=4),
                  in1=EQ4[:, :, blk, L - 1:L].to_broadcast([64, 4, 64]), op=ALU.mult)
                V("tensor_tensor", [BSst[l], BEBL], [BSst[l]], out=Sst[:, l, :, :], in0=Sst[:, l, :, :], in1=EBL[:, :, blk:blk + 1].to_broadcast([64, 4, 64]), op=ALU.mult)
                V("tensor_tensor", [BSst[l], BT1], [BSst[l]], out=Sst[:, l, :, :], in0=Sst[:, l, :, :], in1=T1[:], op=ALU.add)
                if state_out is not None and (not chain or blk == nblk - 1):
                    so = state_out[blk] if not chain else state_out
                    V("tensor_copy", [BSst[l]], [Bhst], out=hst[:].rearrange("p (h v) -> p h v", h=4), in_=Sst[:, l, :, :])
                    Dm("sp", "hso", so.rearrange("h c v -> c h v"), hst[:].rearrange("p (h v) -> p h v", h=4), [Bhst], [outbuf("o")])
                def post_blk(ob=ob, gi=gbase + blk, c0=c0):
                    S.tag = "hgrn.out"
                    A(osq[0:L, :], pb[ob][0:L, 0:256], AF.Square, [Bpb[ob]], [Bosq])
                    V("tensor_reduce", [Bosq], [Boss], out=oss[0:L, 0:4], in_=osq[0:L, :].rearrange("p (h v) -> p h v", h=4), axis=AX.X, op=ALU.add)
                    V("tensor_scalar", [Boss], [Boss], out=oss[0:L, 0:4], in0=oss[0:L, 0:4], scalar1=1.0 / 64, scalar2=RMS_EPS, op0=ALU.mult, op1=ALU.add)
                    A(oss[0:L, 0:4], oss[0:L, 0:4], AF.Sqrt, [Boss], [Boss])
                    V("reciprocal", [Boss], [Boss], out=oss[0:L, 4:8], in_=oss[0:L, 0:4])
                    V("tensor_tensor", [Bpb[ob], Boss, Bosq], [Bosq], out=osq[0:L, :].rearrange("p (h v) -> p h v", h=4), in0=pb[ob][0:L, 0:256].rearrange("p (h v) -> p h v", h=4),
                      in1=oss[0:L, 4:8].unsqueeze(2).to_broadcast([L, 4, 64]), op=ALU.mult)
                    V("tensor_tensor", [Bosq, BG2[gi]], [Boc], out=octok[0:L, :], in0=osq[0:L, :], in1=G2[gi][0:L, :], op=ALU.mult)
                    for c in range(2):
                        tr(pbb[7][:, 512 + c * 64:512 + c * 64 + L], octok[0:L, c * 128:(c + 1) * 128], identb[0:L, 0:L], [Boc, Bid], [Bpb[7]])
                    A(hT[:, 6:8, mixc0 + c0:mixc0 + c0 + L], pbb[7][:, 512:640].rearrange("p (c n) -> p c n", c=2)[:, :, 0:L], AF.Copy, [Bpb[7]], BhT[6:8])
                if hg["pending"] is not None:
                    hg["pending"]()
                hg["pending"] = post_blk

        def mixer(l, T, tiles, nseq, first_tile, last_tile, is_prompt):
            S.tag = "qkv"
            gslot = load_gb(l, 1)
            Dm("sp", "eb", eb34[:], Wd["rb34"][l].rearrange("h r n -> r h n"), [], [Beb])
            A(eb34[:], eb34[:], AF.Exp, [Beb], [Beb])
            V("memset", [Beb], [Beb], ap=eb34[64:128, :, 128:192], constant=0.0)
            wq, Bq_ = wnext("k", "win", l, 0, 512)
            for c in range(4):
                bank = rr("pg", 2)
                for kc in range(8):
                    mm(pb[bank][:, 0:T], wq[:, kc, c * 128:(c + 1) * 128], xT[:, kc, 0:T], kc == 0, kc == 7, [Bq_] + BxT, [Bpb[bank]])
                A(qT[:, c, 0:T], pb[bank][:, 0:T], AF.Copy, [Bpb[bank]], [BqT])
            wk, Bk_ = wnext("k", "win", l, 512, 512)
            for c in range(4):
                bank = rr("pg", 2)
                for kc in range(8):
                    mm(pb[bank][:, 0:T], wk[:, kc, c * 128:(c + 1) * 128], xT[:, kc, 0:T], kc == 0, kc == 7, [Bk_] + BxT, [Bpb[bank]])
                V("tensor_copy", [Bpb[bank]], [Bkcur], out=kcur[:, c, 0:T], in_=pb[bank][:, 0:T])
            kout = (last_tile or not is_prompt)
            if kout:
                for (tt, ntok) in tiles:
                    bank = 2 + rr("pu", 2)
                    for kc in range(8):
                        mm(pb[bank][0:ntok, :], xT[:, kc, tt * 128:tt * 128 + ntok], wk[:, kc, :], kc == 0, kc == 7, [Bk_] + BxT, [Bpb[bank]])
                    ks = rr("kvst", 2)
                    A(kvst[ks][0:ntok, :], pb[bank][0:ntok, :], AF.Copy, [Bpb[bank]], [Bkvst[ks]])
                    dst = (pk[l][tt * 128:tt * 128 + ntok, :] if is_prompt else sk[l][tt * 128:tt * 128 + ntok, :])
                    Dm("sp", f"kvo{ks}", dst, kvst[ks][0:ntok, :], [Bkvst[ks]], [outbuf("o")])
            wvv, Bv_ = wnext("k", "win", l, 1024, 512)
            vtiles = [(tt, tt * 128, ntok) for (tt, ntok) in tiles] if is_prompt else [(si, si * 32, 32) for si in range(nseq)]
            for (vi, tk0, ntok) in vtiles:
                bank = 2 + rr("pu", 2)
                for kc in range(8):
                    mm(pb[bank][0:ntok, :], xT[:, kc, tk0:tk0 + ntok], wvv[:, kc, :], kc == 0, kc == 7, [Bv_] + BxT, [Bpb[bank]])
                V("tensor_copy", [Bpb[bank]], [Bvcur[vi]], out=vcur[0:ntok, vi, :, 0:64], in_=pb[bank][0:ntok, :].rearrange("p (h d) -> p h d", h=8))
                if kout:
                    ks = rr("kvst", 2)
                    A(kvst[ks][0:ntok, :], pb[bank][0:ntok, :], AF.Copy, [Bpb[bank], Bvcur[vi]], [Bkvst[ks]])
                    dst = (pv[l][tk0:tk0 + ntok, :] if is_prompt else sv[l][tk0:tk0 + ntok, :])
                    Dm("sp", f"kvo{ks}", dst, kvst[ks][0:ntok, :], [Bkvst[ks]], [outbuf("o")])
            if is_prompt:
                for p in range(4):
                    kts = []
                    for i in range(5):
                        g = p - 4 + i
                        kind = "c" if i < 3 else ("3" if i == 3 else "4")
                        if g < 0:
                            if first_tile:
                                continue
                            ci = 4 + g
                            kts.append((kind, (lambda c, ci=ci: kc_l[l][:, c, ci * 128:(ci + 1) * 128]), Bkc[l][ci], vc_l[l][:, ci, :, :], Bvc[l][ci], 128))
                        else:
                            kts.append((kind, (lambda c, g=g: kcur[:, c, g * 128:(g + 1) * 128]), Bkcur, vcur[:, g, :, :], Bvcur[g], 128))
                    attention(l, p * 128, 128, kts, p * 128, mask_first=(len(kts) == 5), mask_last=True)
            else:
                for si in range(nseq):
                    for ci in range(4):
                        Dm("pool", "ckst", ckst[:], ck[l, si, ci * 128:(ci + 1) * 128, :], [], [Bckst])
                        for c in range(4):
                            tr(pbb[6][:, c * 128:(c + 1) * 128], ckst[:, c * 128:(c + 1) * 128], identb[:], [Bckst, Bid], [Bpb[6]])
                        A(kc_l[l][:, :, ci * 128:(ci + 1) * 128], pbb[6][:, 0:512].rearrange("p (c n) -> p c n", c=4), AF.Copy, [Bpb[6]], [Bkc[l][ci]])
                        Dm("pool", f"vc{ci}", vc_l[l][:, ci, :, 0:64], cv[l, si, ci * 128:(ci + 1) * 128, :].rearrange("p (h d) -> p h d", h=8), [], [Bvc[l][ci]])
                    kts = []
                    for i in range(4):
                        kind = "c" if i < 3 else "3"
                        kts.append((kind, (lambda c, i=i: kc_l[l][:, c, i * 128:(i + 1) * 128]), Bkc[l][i], vc_l[l][:, i, :, :], Bvc[l][i], 128))
                    kts.append(("4", (lambda c, si=si: kcur[:, c, si * 32:si * 32 + 32]), Bkcur, vcur[0:32, si, :, :], Bvcur[si], 32))
                    attention(l, si * 32, 32, kts, si * 32, mask_first=False, mask_last=False)
            S.tag = "carry"
            if is_prompt and not last_tile:
                A(kc_l[l][:], kcur[:], AF.Copy, [Bkcur], Bkc[l])
                V("tensor_copy", Bvcur, Bvc[l], out=vc_l[l][:, :, :, 0:64], in_=vcur[:, :, :, 0:64])
            w3 = wnext("k", "win", l, 1536, 512)
            w4 = wnext("k", "win", l, 2048, 512)
            w5 = wnext("k", "win", l, 2560, 256)
            if is_prompt:
                if first_tile:
                    V("memset", [], [Bucar[l]], ap=ucar[:, l, :, :], constant=0.0)
                    V("memset", [], [BSst[l]], ap=Sst[:, l, :, :], constant=0.0)
                for g in range(4):
                    lastg = last_tile and g == 3
                    p2 = pool_group(l, g * 128, 128, first_tile and g == 0, None, ppool[l] if lastg else None, g * 128, w3)
                    hgrn_group(l, g * 128, 128, 64, w3, w4, w5, None, phg[l] if lastg else None, g * 128, True, after_proj=p2)
            else:
                for si in range(nseq):
                    pool_group(l, si * 32, 32, False, spool[l, si], spool_o[l, si], si * 32, w3)()
                hgrn_group(l, 0, 32 * nseq, 32, w3, w4, w5, [shg[l, si] for si in range(nseq)], [shg_o[l, si] for si in range(nseq)], 0, False)
            hgrn_drain()
            S.tag = "wout"
            wo = [wnext("k", "wout", l, 0, 512), wnext("k", "wout", l, 512, 512)]
            lp = LNPipe(l, 1, gslot)
            for (tt, ntok) in tiles:
                S.tag = "wout"
                pys = []
                for nh in range(2):
                    py = 4 + rr("py", 2)
                    pys.append(py)
                    wv_, Bw_ = wo[nh]
                    for kc in range(8):
                        mm(pb[py][0:ntok, :], hT[:, kc, tt * 128:tt * 128 + ntok], wv_[:, kc, :], kc == 0, kc == 7, [BhT[kc], Bw_], [Bpb[py]])
                lp.before_acc()
                for nh in range(2):
                    xs_ = x[0:ntok, tt, nh * 512:(nh + 1) * 512]
                    V("scalar_tensor_tensor", [Bx[tt], Bpb[pys[nh]]], [Bx[tt]], out=xs_, in0=xs_, scalar=ALPHA, in1=pb[pys[nh]][0:ntok, :], op0=ALU.mult, op1=ALU.add)
                lp.step(tt, ntok)
            lp.finish()

        def ple(l, T, tiles, prow0, is_prompt):
            S.tag = "ple"
            gslot = load_gb(l, 3)
            psrc = pp if is_prompt else ps_
            for (tt, ntok) in tiles:
                Dm("sp", "ptokp", ptokp[0:ntok, 0, :], psrc[l, prow0 + tt * 128:prow0 + tt * 128 + ntok, :], [], [Bptokp])
                for c in range(2):
                    tr(pb[7][:, c * 128:c * 128 + ntok], ptokp[0:ntok, 0, c * 128:(c + 1) * 128], ident[0:ntok, 0:ntok], [Bptokp, Bid], [Bpb[7]])
                V("tensor_copy", [Bpb[7]], [BpT], out=pTt[:, :, tt * 128:tt * 128 + ntok], in_=pb[7][:, 0:256].rearrange("p (c n) -> p c n", c=2)[:, :, 0:ntok])
            wg = [wnext("k", "pleg", l, 0, 512), wnext("k", "pleg", l, 512, 512)]
            wp_, Bwp_ = wnext("j", "plep", l, 0, 2)
            lp = LNPipe(l, 3, gslot)
            for (tt, ntok) in tiles:
                S.tag = "ple"
                banks = []
                for nh in range(2):
                    pg = rr("pg", 2); pu = 2 + rr("pu", 2)
                    banks.append((pg, pu))
                    wv_, Bw_ = wg[nh]
                    for kc in range(8):
                        mm(pb[pg][0:ntok, :], xT[:, kc, tt * 128:tt * 128 + ntok], wv_[:, kc, :], kc == 0, kc == 7, [BxT[tt], Bw_], [Bpb[pg]])
                    for c in range(2):
                        mm(pb[pu][0:ntok, :], pTt[:, c, tt * 128:tt * 128 + ntok], wp_[:, c, nh * 512:(nh + 1) * 512], c == 0, c == 1, [BpT, Bwp_], [Bpb[pu]])
                lp.before_acc()
                for nh in range(2):
                    pg, pu = banks[nh]
                    ei = rr("etile", 2)
                    A(etile[ei][0:ntok, :], pb[pg][0:ntok, :], AF.Sigmoid, [Bpb[pg]], [Bet2[ei]])
                    V("tensor_tensor", [Bet2[ei], Bpb[pu]], [Bet2[ei]], out=etile[ei][0:ntok, :], in0=etile[ei][0:ntok, :], in1=pb[pu][0:ntok, :], op=ALU.mult)
                    xs_ = x[0:ntok, tt, nh * 512:(nh + 1) * 512]
                    V("scalar_tensor_tensor", [Bx[tt], Bet2[ei]], [Bx[tt]], out=xs_, in0=xs_, scalar=ALPHA, in1=etile[ei][0:ntok, :], op0=ALU.mult, op1=ALU.add)
                lp.step(tt, ntok)
            lp.finish()

        def run_tile(xsrc, ydst, row0, T, tiles, is_prompt, first_tile, last_tile, nseq):
            for (tt, ntok) in tiles:
                Dm("sp", f"xin{tt}", x[0:ntok, tt, :], xsrc[row0 + tt * 128:row0 + tt * 128 + ntok, :], [], [Bx[tt]])
                transposes(tt, ntok)
            for l in range(depth):
                ffn(l, "f1", 0, T, tiles)
                mixer(l, T, tiles, nseq, first_tile, last_tile, is_prompt)
                ffn(l, "f2", 2, T, tiles)
                ple(l, T, tiles, row0, is_prompt)
            for (tt, ntok) in tiles:
                Dm("sp", f"xout{tt}", ydst[row0 + tt * 128:row0 + tt * 128 + ntok, :], x[0:ntok, tt, :], [Bx[tt]], [outbuf("o")])

        for ti in range(self.n_ptiles):
            run_tile(xp, yp, ti * 512, 512, [(t, 128) for t in range(4)], True, ti == 0, ti == self.n_ptiles - 1, 0)
        if self.n_samp:
            run_tile(xs, ys, 0, NS, [(0, NS)], False, False, False, nsm)
        assert consumed[0] == len(plan), (consumed[0], len(plan))
        self.nsem = S.emit(nc, outbufs)
        self.st.close()
        return nc


def _rel_tables(attn_rel_bias):
    depth = attn_rel_bias.shape[0]
    r = np.arange(128)[:, None]
    j = np.arange(128)[None, :]
    d3 = np.clip(128 - r + j, -63, 128) + 63
    d4 = np.clip(j - r, -63, 128) + 63
    rb34 = np.concatenate([attn_rel_bias[:, :, d3], attn_rel_bias[:, :, d4]], axis=-1)
    rbc = attn_rel_bias[:, :, 191]
    return np.ascontiguousarray(rb34, dtype=np.float32), np.ascontiguousarray(rbc, dtype=np.float32)


_CACHE = {}


def run(inputs, depth, n_ptiles, n_samp_per_core, n_cores, prompt_of_core, samp_of_core):
    key = (depth, n_ptiles, n_samp_per_core)
    if key not in _CACHE:
        _CACHE[key] = Prog(depth, n_ptiles, n_samp_per_core).build()
    nc = _CACHE[key]
    f = lambda a: np.ascontiguousarray(np.asarray(a), dtype=np.float32)
    rb34, rbc = _rel_tables(f(inputs["attn_rel_bias"]))
    common = {
        "f1gu": f(inputs["ffn1_w_gu"]), "f1d": f(inputs["ffn1_w_down"]), "win": f(inputs["w_in"]), "wout": f(inputs["w_out"]),
        "f2gu": f(inputs["ffn2_w_gu"]), "f2d": f(inputs["ffn2_w_down"]), "pleg": f(inputs["ple_w_gate"]), "plep": f(inputs["ple_w_proj"]),
        "lng": f(inputs["ln_g"]), "lnb": f(inputs["ln_b"]), "poolw": f(inputs["pool_w"]), "pscale": f(inputs["pool_scale"]),
        "lbraw": f(inputs["hgrn_lower_bounds"]), "normw": f(inputs["hgrn_norm_w"]), "rb34": rb34, "rbc": rbc,
    }
    xp, pp_ = f(inputs["x_prompt"]), f(inputs["p_prompt"])
    xs, ps_ = f(inputs["x_sample"]), f(inputs["p_sample"])
    ck, cv = f(inputs["cache_attn_k"]), f(inputs["cache_attn_v"])
    sp_, sh = f(inputs["state_pool"]), f(inputs["state_hgrn"])
    in_maps = []
    for c in range(n_cores):
        b = prompt_of_core[c]
        sb = samp_of_core[c]
        m = dict(common)
        m["xp"] = xp[b]; m["pp"] = np.ascontiguousarray(pp_[:, b])
        m["xs"] = np.ascontiguousarray(xs[sb].reshape(-1, D)); m["ps"] = np.ascontiguousarray(ps_[:, sb].reshape(depth, -1, 256))
        m["ck"] = np.ascontiguousarray(ck[:, sb].reshape(depth, len(sb), 512, 512)); m["cv"] = np.ascontiguousarray(cv[:, sb].reshape(depth, len(sb), 512, 512))
        m["spool"] = np.ascontiguousarray(sp_[:, sb]); m["shg"] = np.ascontiguousarray(sh[:, sb])
        in_maps.append(m)
    res = run_bass_kernel_spmd(nc, in_maps, core_ids=list(range(n_cores)))
    return res.results


def kernel(**inputs):
    depth = 4
    n_cores = 8
    prompt_of_core = [c % 4 for c in range(n_cores)]
    samp_of_core = [list(range(4 * c, 4 * c + 4)) for c in range(n_cores)]
    r = run(inputs, depth, 16, 4, n_cores, prompt_of_core, samp_of_core)
    B, SEQ = 4, 8192
    y_prompt = np.stack([r[b]["yp"] for b in range(B)]).astype(np.float32)
    y_sample = np.concatenate([r[c]["ys"].reshape(4, 32, D) for c in range(n_cores)]).astype(np.float32)
    pk = np.stack([r[b]["pk"].reshape(depth, 512, 512) for b in range(B)], axis=1).reshape(depth, B, 512, 8, 64).astype(np.float32)
    pv = np.stack([r[b]["pv"].reshape(depth, 512, 512) for b in range(B)], axis=1).reshape(depth, B, 512, 8, 64).astype(np.float32)
    ppool = np.stack([r[b]["ppool"].reshape(depth, 15, 256) for b in range(B)], axis=1).astype(np.float32)
    phg = np.stack([r[b]["phg"].reshape(depth, 4, 64, 64) for b in range(B)], axis=1).astype(np.float32)
    sk = np.concatenate([r[c]["sk"].reshape(depth, 4, 32, 8, 64) for c in range(n_cores)], axis=1).astype(np.float32)
    sv = np.concatenate([r[c]["sv"].reshape(depth, 4, 32, 8, 64) for c in range(n_cores)], axis=1).astype(np.float32)
    spool = np.concatenate([r[c]["spool_o"].reshape(depth, 4, 15, 256) for c in range(n_cores)], axis=1).astype(np.float32)
    shg = np.concatenate([r[c]["shg_o"].reshape(depth, 4, 4, 64, 64) for c in range(n_cores)], axis=1).astype(np.float32)
    return (y_prompt, y_sample, pk, pv, ppool, phg, sk, sv, spool, shg)
```

```python
import contextlib
import numpy as np
import concourse.bass as bass
import concourse.mybir as mybir
from concourse.bass_utils import run_bass_kernel_spmd

F32 = mybir.dt.float32
BF16 = mybir.dt.bfloat16
AF = mybir.ActivationFunctionType
ALU = mybir.AluOpType
AX = mybir.AxisListType

D = 1024
HID = 2816
INW = 2816
ALPHA = float(8 ** 0.25)
LN_EPS = 1e-5
RMS_EPS = 1e-6
NSLOT = 6
STREAMS = ("pe", "act", "dve", "pool", "sp")


class Buf:
    __slots__ = ("name", "lw", "rd", "excl")

    def __init__(self, name, excl=False):
        self.name = name
        self.lw = None
        self.rd = []
        self.excl = excl


class Op:
    __slots__ = ("stream", "fn", "waits", "sig", "comp", "cidx", "tag")

    def __init__(self, stream, fn, comp, cidx):
        self.stream, self.fn, self.comp, self.cidx = stream, fn, comp, cidx
        self.tag = None
        self.waits = []
        self.sig = False


class Sched:
    def __init__(self):
        self.ops = {s: [] for s in STREAMS}
        self.comp_ops = {}
        self.seen = {s: {} for s in STREAMS}
        self.nops = 0
        self.tag = "init"
        self.annotate = False

    def _add(self, stream, fn, reads, writes, dkey=None):
        comp = stream if dkey is None else "d:" + dkey
        lst = self.comp_ops.setdefault(comp, [])
        op = Op(stream, fn, comp, len(lst))
        op.tag = self.tag
        if dkey is not None:
            op.sig = True
        lst.append(op)
        me = (comp, op.cidx)
        deps = {}

        def need(d):
            if d is None:
                return
            e, i = d
            if e == "pe" and comp == "pe":
                return
            if deps.get(e, -1) < i:
                deps[e] = i

        for b in reads:
            need(b.lw)
            if b.excl:
                for r in b.rd:
                    if r[0] != comp:
                        need(r)
        for b in writes:
            need(b.lw)
            for r in b.rd:
                need(r)
        if dkey is not None and op.cidx > 0:
            need((comp, op.cidx - 1))
        seen = self.seen[stream]
        for e, i in deps.items():
            if seen.get(e, -1) >= i:
                continue
            seen[e] = i
            op.waits.append((e, i))
            self.comp_ops[e][i].sig = True
        for b in reads:
            if len(b.rd) > 64:
                last = {}
                for r in b.rd:
                    if last.get(r[0], -1) < r[1]:
                        last[r[0]] = r[1]
                b.rd = list(last.items())
            b.rd.append(me)
        for b in writes:
            b.lw = me
            b.rd = []
        self.ops[stream].append(op)
        self.nops += 1
        return op

    def mm(self, out, lhsT, rhs, start, stop, reads, writes):
        return self._add("pe", ("matmul", dict(out=out, lhsT=lhsT, rhs=rhs, start=start, stop=stop)), reads, writes)

    def tr(self, out, in_, identity, reads, writes):
        return self._add("pe", ("transpose", dict(out=out, in_=in_, identity=identity)), reads, writes)

    def A(self, out, in_, func, reads, writes, **kw):
        return self._add("act", ("activation", dict(out=out, in_=in_, func=func, **kw)), reads, writes)

    def V(self, name, reads, writes, **kw):
        return self._add("dve", (name, kw), reads, writes)

    def G(self, name, reads, writes, **kw):
        return self._add("pool", (name, kw), reads, writes)

    def D(self, stream, dkey, out, in_, reads, writes, slow=False):
        kw = dict(out=out, in_=in_)
        if slow:
            kw["allow_slow_non_contiguous"] = True
        return self._add(stream, ("dma_start", kw), reads, writes, dkey=dkey)

    def emit(self, nc, final_bufs):
        self._add("sp", None, final_bufs, [])
        rank = {}
        for comp, lst in self.comp_ops.items():
            c = 0
            for op in lst:
                if op.sig:
                    c += 1
                    rank[(comp, op.cidx)] = c
        with contextlib.ExitStack() as st:
            sems = {}
            for comp in self.comp_ops:
                if any(op.sig for op in self.comp_ops[comp]):
                    sems[comp] = st.enter_context(nc.semaphore("s_" + comp.replace(":", "_")))
            block = st.enter_context(nc.Block())

            def run(stream):
                def body(eng):
                    for op in self.ops[stream]:
                        for (e, i) in op.waits:
                            eng.wait_ge(sems[e], rank[(e, i)] * (16 if e.startswith("d:") else 1))
                        if op.fn is None:
                            continue
                        ins = getattr(eng, op.fn[0])(**op.fn[1])
                        if self.annotate:
                            ins.annotate(op.tag)
                        if op.sig:
                            ins.then_inc(sems[op.comp], 16 if op.comp.startswith("d:") else 1)
                return body

            block.tensor(run("pe"))
            block.scalar(run("act"))
            block.vector(run("dve"))
            block.gpsimd(run("pool"))
            block.sync(run("sp"))
        return len(sems)


class Prog:
    def __init__(self, depth, n_ptiles, n_samp):
        self.depth, self.n_ptiles, self.n_samp = depth, n_ptiles, n_samp
        self.nc = bass.Bass("TRN2", target_bir_lowering=False)
        self.S = Sched()
        self.st = contextlib.ExitStack()
        self.rr = {}

    def dram(self, name, shape, kind):
        return self.nc.dram_tensor(name, list(shape), F32, kind=kind).ap()

    def sb(self, name, shape, dt=F32):
        return self.st.enter_context(self.nc.sbuf_tensor("sb_" + name, list(shape), dt))

    def build(self):
        nc, S, depth = self.nc, self.S, self.depth
        NP = self.n_ptiles * 512
        NS = self.n_samp * 32
        nsm = self.n_samp
        I, O = "ExternalInput", "ExternalOutput"
        dr = self.dram
        xp = dr("xp", [max(NP, 1), D], I); pp = dr("pp", [depth, max(NP, 1), 256], I)
        xs = dr("xs", [NS, D], I); ps_ = dr("ps", [depth, NS, 256], I)
        ck = dr("ck", [depth, nsm, 512, 512], I); cv = dr("cv", [depth, nsm, 512, 512], I)
        spool = dr("spool", [depth, nsm, 15, 256], I); shg = dr("shg", [depth, nsm, 4, 64, 64], I)
        Wd = {}
        for nm, shp in [("f1gu", [depth, D, 2 * HID]), ("f1d", [depth, HID, D]), ("win", [depth, D, INW]),
                        ("wout", [depth, D, D]), ("f2gu", [depth, D, 2 * HID]), ("f2d", [depth, HID, D]),
                        ("pleg", [depth, D, D]), ("plep", [depth, 256, D]), ("lng", [depth, 4, D]),
                        ("lnb", [depth, 4, D]), ("poolw", [depth, 4, 64, 64]), ("pscale", [depth, 256]),
                        ("lbraw", [depth, 256]), ("normw", [depth, 64]), ("rb34", [depth, 8, 128, 256]),
                        ("rbc", [depth, 8])]:
            Wd[nm] = dr(nm, shp, I)
        yp = dr("yp", [max(NP, 1), D], O); ys = dr("ys", [NS, D], O)
        pk2 = dr("pk", [depth * 512, 512], O); pv2 = dr("pv", [depth * 512, 512], O)
        ppool2 = dr("ppool", [depth * 15, 256], O); phg2 = dr("phg", [depth * 256, 64], O)
        sk2 = dr("sk", [depth * NS, 512], O); sv2 = dr("sv", [depth * NS, 512], O)
        spool2 = dr("spool_o", [depth * nsm * 15, 256], O); shg2 = dr("shg_o", [depth * nsm * 256, 64], O)
        pk = [pk2[l * 512:(l + 1) * 512, :] for l in range(depth)]; pv = [pv2[l * 512:(l + 1) * 512, :] for l in range(depth)]
        sk = [sk2[l * NS:(l + 1) * NS, :] for l in range(depth)]; sv = [sv2[l * NS:(l + 1) * NS, :] for l in range(depth)]
        ppool = [ppool2[l * 15:(l + 1) * 15, :] for l in range(depth)]
        phg = [phg2[l * 256:(l + 1) * 256, :].rearrange("(h c) v -> h c v", h=4) for l in range(depth)]
        spool_o = {(l, si): spool2[(l * nsm + si) * 15:(l * nsm + si + 1) * 15, :] for l in range(depth) for si in range(nsm)}
        shg_o = {(l, si): shg2[(l * nsm + si) * 256:(l * nsm + si + 1) * 256, :].rearrange("(h c) v -> h c v", h=4) for l in range(depth) for si in range(nsm)}
        outbufs = []

        def outbuf(name):
            b = Buf(name)
            outbufs.append(b)
            return b

        sb = self.sb
        x = sb("x", [128, 4, D]); Bx = [Buf(f"x{t}") for t in range(4)]
        xT = sb("xT", [128, 8, 512], BF16); BxT = [Buf(f"xT{t}") for t in range(4)]
        hT = sb("hT", [128, 12, 512], BF16); BhT = [Buf(f"hT{j}") for j in range(12)]
        ring = [sb(f"ring{i}", [128, 4096], BF16) for i in range(NSLOT)]
        Bring = [Buf(f"ring{i}") for i in range(NSLOT)]
        gb = [sb(f"gb{i}", [128, 2, D]) for i in range(2)]; Bgb = [Buf(f"gb{i}") for i in range(2)]
        ident = sb("ident", [128, 128]); identb = sb("identb", [128, 128], BF16); Bid = Buf("ident")
        tri = sb("tri", [64, 64]); Btri = Buf("tri")
        resetm = sb("resetm", [64, 2, 128]); Bresetm = Buf("resetm")
        ptmp = sb("ptmp", [128, 16]); Bptmp = Buf("ptmp")
        mhalf = sb("mhalf", [128, 1]); Bmh = Buf("mhalf")
        sgb = [sb(f"sg{i}", [128, 512]) for i in range(2)]; Bsg = [Buf(f"sg{i}") for i in range(2)]
        stt = [sb(f"stt{i}", [128, 24]) for i in range(4)]; Bstt = [Buf(f"stt{i}") for i in range(4)]
        kc_l = [sb(f"kcar{l}", [128, 4, 512], BF16) for l in range(depth)]
        vc_l = [sb(f"vcar{l}", [128, 4, 8, 65], BF16) for l in range(depth)]
        Bkc = [[Buf(f"kc{l}_{t}") for t in range(4)] for l in range(depth)]
        Bvc = [[Buf(f"vc{l}_{t}") for t in range(4)] for l in range(depth)]
        kcur = sb("kcur", [128, 4, 512], BF16); Bkcur = Buf("kcur")
        vcur = sb("vcur", [128, 4, 8, 65], BF16); Bvcur = [Buf(f"vcur{t}") for t in range(4)]
        qT = sb("qT", [128, 4, 512], BF16); BqT = Buf("qT")
        eb34 = sb("eb34", [128, 8, 256]); Beb = Buf("eb34")
        cbt = sb("cbt", [128, depth, 8]); Bcb = Buf("cbt")
        PT = [sb(f"PT{i}", [128, 640], BF16) for i in range(3)]; BPT = [Buf(f"PT{i}") for i in range(3)]
        etmp = [sb(f"etmp{i}", [128, 256]) for i in range(2)]; Bet = [Buf(f"etmp{i}") for i in range(2)]
        oa = sb("oa", [128, 512], BF16); Boa = Buf("oa")
        rden = sb("rden", [128, 8]); Brd = Buf("rden")
        kvst = sgb; Bkvst = Bsg
        uT = sb("uT", [128, 2, 144]); BuT = Buf("uT")
        sA = sb("sA", [128, 2, 144]); BsA = Buf("sA")
        sBb = sb("sBb", [128, 2, 144]); BsB = Buf("sBb")
        dT = sb("dT", [128, 2, 128], BF16); BdT = Buf("dT")
        ucar = sb("ucar", [128, depth, 2, 16]); Bucar = [Buf(f"ucar{l}") for l in range(depth)]
        wpool = sb("wpool", [128, depth, 2, 128], BF16); Bwp = Buf("wpool")
        pscale = sb("pscale", [128, depth, 2]); Bps = Buf("pscale")
        rc16 = sb("rc16", [128, 2, 16]); Brc = Buf("rc16")
        pst = sb("pst", [16, 256]); Bpst = Buf("pst")
        ptok = sb("ptok", [16, 256]); Bptok = Buf("ptok")
        lbt = sb("lbt", [64, depth, 4]); omlt = sb("omlt", [64, depth, 4]); Blb = Buf("lb")
        lbtmp = sb("lbtmp", [64, depth, 4])
        lbsum = sb("lbsum", [64, 4])
        normwb = sb("normwb", [64, depth, 64]); Bnw = Buf("normw")
        Sst = sb("Sst", [64, depth, 4, 64]); BSst = [Buf(f"Sst{l}") for l in range(depth)]
        QS = sb("QS", [64, 4, 128]); BQS = Buf("QS")
        Fb = sb("Fb", [64, 4, 128]); BF_ = Buf("F")
        KIN = sb("KIN", [64, 4, 128]); BKIN = Buf("KIN")
        Bc = sb("Bc", [64, 4, 128]); BBc = Buf("Bc")
        EQ = sb("EQ", [64, 4, 128]); BEQ = Buf("EQ")
        EK = sb("EK", [64, 4, 128]); BEK = Buf("EK")
        QTt = sb("QTt", [64, 4, 128], BF16); BQT = Buf("QTt")
        KTt = sb("KTt", [64, 4, 128], BF16); BKT = Buf("KTt")
        EBM = sb("EBM", [64, 4, 8]); BEBM = Buf("EBM")
        EBL = sb("EBL", [64, 4, 8]); BEBL = Buf("EBL")
        VTOK = [sb(f"VTOK{i}", [64, 256], BF16) for i in range(4)]; BVT = [Buf(f"VTOK{i}") for i in range(4)]
        G2 = [sb(f"G2{i}", [64, 256]) for i in range(4)]; BG2 = [Buf(f"G2{i}") for i in range(4)]
        KTOK = sb("KTOK", [64, 4, 64], BF16); BKTOK = Buf("KTOK")
        SM = sb("SM", [64, 4, 64], BF16); BSM = Buf("SM")
        Sbf = sb("Sbf", [64, 4, 64], BF16); BSbf = Buf("Sbf")
        T1 = sb("T1", [64, 4, 64]); BT1 = Buf("T1")
        osq = sb("osq", [64, 256]); Bosq = Buf("osq")
        oss = sb("oss", [64, 8]); Boss = Buf("oss")
        octok = sb("octok", [64, 256], BF16); Boc = Buf("octok")
        hst = sb("hst", [64, 256]); Bhst = Buf("hst")
        ptokp = sb("ptokp", [128, 1, 256]); Bptokp = Buf("ptokp")
        pTt = sb("pTt", [128, 2, 512], BF16); BpT = Buf("pTt")
        etile = sgb; Bet2 = Bsg
        ckst = oa; Bckst = Boa
        pb = [self.st.enter_context(nc.psum_tensor(f"pb{i}", [128, 512], F32)) for i in range(8)]
        Bpb = [Buf(f"pb{i}", excl=True) for i in range(8)]
        pbb = [p.bitcast(BF16) for p in pb]

        def rr(key, n):
            v = self.rr.get(key, 0)
            self.rr[key] = v + 1
            return v % n

        plan = []
        consumed = [0]
        issued = [0]

        def layer_plan(l):
            p = []
            for f in ("f1", "f2"):
                q = []
                for g in (0, 1):
                    for b in ((0, 1, 2) if g == 0 else (3, 4, 5)):
                        nc_ = 512 if b < 5 else 256
                        q.append(("k", f + "gu", l, b * 512, nc_))
                        q.append(("k", f + "gu", l, HID + b * 512, nc_))
                    for b in ((0, 1, 2) if g == 0 else (3, 4, 5)):
                        q.append(("j", f + "d", l, b * 512, 4 if b < 5 else 2))
                if f == "f1":
                    p += q
                    for b in range(6):
                        p.append(("k", "win", l, b * 512, 512 if b < 5 else 256))
                    p.append(("k", "wout", l, 0, 512)); p.append(("k", "wout", l, 512, 512))
                else:
                    p += q
                    p.append(("k", "pleg", l, 0, 512)); p.append(("k", "pleg", l, 512, 512))
                    p.append(("j", "plep", l, 0, 2))
            return p

        ntiles_total = self.n_ptiles + (1 if self.n_samp else 0)
        for _ in range(ntiles_total):
            for l in range(depth):
                plan.extend(layer_plan(l))

        def issue_upto(n):
            while issued[0] < min(n, len(plan)):
                i = issued[0]
                kind, nm, l, a0, n_ = plan[i]
                s = i % NSLOT
                if kind == "k":
                    src = Wd[nm][l][:, a0:a0 + n_].rearrange("(k p) n -> p k n", p=128)
                    dst = ring[s][:, 0:8 * n_].rearrange("p (k n) -> p k n", k=8)
                else:
                    src = Wd[nm][l][a0:a0 + n_ * 128, :].rearrange("(j p) n -> p j n", p=128)
                    dst = ring[s][:, 0:n_ * 1024].rearrange("p (j n) -> p j n", j=n_)
                S.D("pool", f"ring{s}", dst, src, [], [Bring[s]])
                issued[0] += 1

        def wnext(kind, nm, l, a0, n_):
            i = consumed[0]
            assert plan[i] == (kind, nm, l, a0, n_), (plan[i], (kind, nm, l, a0, n_))
            issue_upto(i + NSLOT - 2)
            consumed[0] += 1
            s = i % NSLOT
            if kind == "k":
                return ring[s][:, 0:8 * n_].rearrange("p (k n) -> p k n", k=8), Bring[s]
            return ring[s][:, 0:n_ * 1024].rearrange("p (j n) -> p j n", j=n_), Bring[s]

        mm, tr, A, V, G, Dm = S.mm, S.tr, S.A, S.V, S.G, S.D
        G("memset", [], [Bid], ap=ident[:], constant=1.0)
        G("affine_select", [Bid], [Bid], out=ident[:], in_=ident[:], pattern=[[-1, 128]], compare_op=ALU.is_equal, fill=0.0, base=0, channel_multiplier=1)
        V("tensor_copy", [Bid], [Bid], out=identb[:], in_=ident[:])
        G("memset", [], [Btri], ap=tri[:], constant=1.0)
        G("affine_select", [Btri], [Btri], out=tri[:], in_=tri[:], pattern=[[1, 64]], compare_op=ALU.is_ge, fill=0.0, base=0, channel_multiplier=-1)
        G("memset", [], [Bresetm], ap=resetm[:], constant=1.0)
        for blk in range(2):
            G("memset", [Bresetm], [Bresetm], ap=resetm[:, 0, blk * 64:blk * 64 + 1], constant=0.0)
        for blk in range(4):
            G("memset", [Bresetm], [Bresetm], ap=resetm[:, 1, blk * 32:blk * 32 + 1], constant=0.0)
        for blk in range(8):
            pass
        G("memset", [], [Bmh], ap=mhalf[:], constant=-0.5)
        G("memset", [], [Bptok], ap=ptok[:], constant=0.0)
        for l in range(depth):
            G("memset", [], Bvc[l], ap=vc_l[l][:], constant=1.0)
            G("memset", [], Bkc[l], ap=kc_l[l][:], constant=0.0)
        G("memset", [], Bvcur, ap=vcur[:], constant=1.0)
        G("memset", [], [Brc], ap=rc16[:], constant=1.0)
        for c in range(2):
            for hf in range(2):
                w = (2, 4, 8, 16)[c * 2 + hf]
                for t in range(1, 16):
                    val = 1.0 / min(t + 1, w)
                    if t < w:
                        hi = 16 if t == w - 1 else t + 1
                        G("memset", [Brc], [Brc], ap=rc16[hf * 64:hf * 64 + 64, c, t:hi], constant=val)
        G("memset", [], [Bwp], ap=wpool[:], constant=0.0)
        for l in range(depth):
            for g in range(4):
                c, hf = g // 2, g % 2
                Dm("pool", "cstw", wpool[hf * 64:hf * 64 + 64, l, c, hf * 64:hf * 64 + 64], Wd["poolw"][l, g], [Bwp], [Bwp])
        with nc.allow_non_contiguous_dma(reason="tiny constant loads"):
            for l in range(depth):
                Dm("sp", "cst", pscale[:, l, :], Wd["pscale"][l].rearrange("(c p) -> p c", p=128), [], [Bps], slow=True)
                Dm("sp", "cst", lbtmp[:, l, :], Wd["lbraw"][l].rearrange("(h c) -> c h", c=64), [], [Blb], slow=True)
                Dm("sp", "cst", cbt[:, l, :], Wd["rbc"][l].partition_broadcast(128), [], [Bcb])
                Dm("sp", "cst", normwb[:, l, :], Wd["normw"][l].partition_broadcast(64), [], [Bnw])
        A(lbtmp[:], lbtmp[:], AF.Exp, [Blb], [Blb])
        V("tensor_copy", [Blb], [Blb], out=lbsum[:], in_=lbtmp[:, 0, :])
        for l in range(1, depth):
            V("tensor_tensor", [Blb], [Blb], out=lbsum[:], in0=lbsum[:], in1=lbtmp[:, l, :], op=ALU.add)
        V("reciprocal", [Blb], [Blb], out=lbsum[:], in_=lbsum[:])
        V("memset", [Blb], [Blb], ap=lbt[:, 0, :], constant=0.0)
        for l in range(1, depth):
            V("tensor_tensor", [Blb], [Blb], out=lbtmp[:, l, :], in0=lbtmp[:, l, :], in1=lbsum[:], op=ALU.mult)
            V("tensor_tensor", [Blb], [Blb], out=lbt[:, l, :], in0=lbt[:, l - 1, :], in1=lbtmp[:, l, :], op=ALU.add)
        V("tensor_scalar", [Blb], [Blb], out=omlt[:], in0=lbt[:], scalar1=-1.0, scalar2=1.0, op0=ALU.mult, op1=ALU.add)

        def load_gb(l, i):
            s = rr("gb", 2)
            Dm("sp", f"gb{s}", gb[s][:, 0, :], Wd["lng"][l, i].partition_broadcast(128), [], [Bgb[s]])
            Dm("sp", f"gb{s}", gb[s][:, 1, :], Wd["lnb"][l, i].partition_broadcast(128), [], [Bgb[s]])
            return s

        def transposes(tt, ntok):
            S.tag = "tr"
            for half in range(2):
                b = 6 + half
                for k4 in range(4):
                    kc = half * 4 + k4
                    tr(pb[b][:, k4 * 128:k4 * 128 + ntok], x[0:ntok, tt, kc * 128:(kc + 1) * 128], ident[0:ntok, 0:ntok], [Bx[tt], Bid], [Bpb[b]])
                src = pb[b][:].rearrange("p (k n) -> p k n", k=4)[:, :, 0:ntok]
                dst = xT[:, half * 4:half * 4 + 4, tt * 128:tt * 128 + ntok]
                if half == 0:
                    A(dst, src, AF.Copy, [Bpb[b]], [BxT[tt]])
                else:
                    V("tensor_copy", [Bpb[b]], [BxT[tt]], out=dst, in_=src)

        def post(l, i, tt, ntok, gslot):
            post_a(l, i, tt, ntok, gslot)
            post_b(l, i, tt, ntok, gslot)

        def post_a(l, i, tt, ntok, gslot):
            s = tt
            S.tag = "ln"
            xt = x[0:ntok, tt, :]
            st_ = stt[s]
            V("bn_stats", [Bx[tt]], [Bstt[s]], out=st_[0:ntok, 0:6], in_=x[0:ntok, tt, 0:512])
            V("bn_stats", [Bx[tt]], [Bstt[s]], out=st_[0:ntok, 6:12], in_=x[0:ntok, tt, 512:1024])
            V("bn_aggr", [Bstt[s]], [Bstt[s]], out=st_[0:ntok, 12:14], in_=st_[0:ntok, 0:12])
            V("tensor_scalar", [Bstt[s]], [Bstt[s]], out=st_[0:ntok, 14:15], in0=st_[0:ntok, 13:14], scalar1=LN_EPS, scalar2=None, op0=ALU.add)
            A(st_[0:ntok, 17:18], st_[0:ntok, 14:15], AF.Sqrt, [Bstt[s]], [Bstt[s]])

        def post_b(l, i, tt, ntok, gslot):
            s = tt
            S.tag = "ln"
            xt = x[0:ntok, tt, :]
            st_ = stt[s]
            V("reciprocal", [Bstt[s]], [Bstt[s]], out=st_[0:ntok, 15:16], in_=st_[0:ntok, 17:18])
            V("scalar_tensor_tensor", [Bstt[s]], [Bstt[s]], out=st_[0:ntok, 16:17], in0=st_[0:ntok, 12:13], scalar=-1.0, in1=st_[0:ntok, 15:16], op0=ALU.mult, op1=ALU.mult)
            A(xt, xt, AF.Identity, [Bx[tt], Bstt[s]], [Bx[tt]], bias=st_[0:ntok, 16:17], scale=st_[0:ntok, 15:16])
            V("tensor_tensor", [Bx[tt], Bgb[gslot]], [Bx[tt]], out=xt, in0=xt, in1=gb[gslot][0:ntok, 0, :], op=ALU.mult)
            V("tensor_tensor", [Bx[tt], Bgb[gslot]], [Bx[tt]], out=xt, in0=xt, in1=gb[gslot][0:ntok, 1, :], op=ALU.add)

        class LNPipe:
            def __init__(self, l, i, gslot):
                self.l, self.i, self.g = l, i, gslot
                self.prev = None
                self.dl = Delayed()
            def before_acc(self):
                self.dl.flush()
            def step(self, tt, ntok):
                post_a(self.l, self.i, tt, ntok, self.g)
                if self.prev is not None:
                    post_b(self.l, self.i, *self.prev, self.g)
                    self.dl.pending = self.prev
                self.prev = (tt, ntok)
            def finish(self):
                if self.prev is not None:
                    post_b(self.l, self.i, *self.prev, self.g)
                    self.dl.pending = self.prev
                    self.prev = None
                self.dl.flush(final=True)

        class Delayed:
            DEPTH = 2
            def __init__(self):
                self.q = []
            def flush(self, final=False):
                while self.q and (final or len(self.q) >= self.DEPTH):
                    transposes(*self.q.pop(0))
            @property
            def pending(self):
                return None
            @pending.setter
            def pending(self, v):
                self.q.append(v)

        def ffn(l, f, lni, T, tiles):
            gslot = load_gb(l, lni)
            for g in (0, 1):
                blks = (0, 1, 2) if g == 0 else (3, 4, 5)
                jj = 0
                for b in blks:
                    ncols = 512 if b < 5 else 256
                    S.tag = "ffn.up"
                    wg, Bg = wnext("k", f + "gu", l, b * 512, ncols)
                    wu, Bu = wnext("k", f + "gu", l, HID + b * 512, ncols)
                    for cc in range(ncols // 128):
                        pg = rr("pg", 2); pu = 2 + rr("pu", 2)
                        for kc in range(8):
                            mm(pb[pg][:, 0:T], wg[:, kc, cc * 128:(cc + 1) * 128], xT[:, kc, 0:T], kc == 0, kc == 7, [Bg] + BxT, [Bpb[pg]])
                        for kc in range(8):
                            mm(pb[pu][:, 0:T], wu[:, kc, cc * 128:(cc + 1) * 128], xT[:, kc, 0:T], kc == 0, kc == 7, [Bu] + BxT, [Bpb[pu]])
                        sgi = rr("sg", 2)
                        A(sgb[sgi][:, 0:T], pb[pg][:, 0:T], AF.Silu, [Bpb[pg]], [Bsg[sgi]])
                        V("scalar_tensor_tensor", [Bpb[pu], Bsg[sgi]], [BhT[jj]], out=hT[:, jj, 0:T], in0=pb[pu][:, 0:T], scalar=0.5, in1=sgb[sgi][:, 0:T], op0=ALU.mult, op1=ALU.mult)
                        jj += 1
                nj = jj
                wds = []
                S.tag = "ffn.down"
                for b in blks:
                    wds.append(wnext("j", f + "d", l, b * 512, 4 if b < 5 else 2))
                lp = LNPipe(l, lni, gslot)
                for (tt, ntok) in tiles:
                    S.tag = "ffn.down"
                    pys = []
                    for nh in range(2):
                        py = 4 + rr("py", 2)
                        pys.append(py)
                        for j in range(nj):
                            wd, Bw = wds[j // 4]
                            mm(pb[py][0:ntok, :], hT[:, j, tt * 128:tt * 128 + ntok], wd[:, j % 4, nh * 512:(nh + 1) * 512], j == 0, j == nj - 1, [BhT[j], Bw], [Bpb[py]])
                    lp.before_acc()
                    for nh in range(2):
                        py = pys[nh]
                        xs_ = x[0:ntok, tt, nh * 512:(nh + 1) * 512]
                        if g == 0:
                            V("scalar_tensor_tensor", [Bx[tt], Bpb[py]], [Bx[tt]], out=xs_, in0=xs_, scalar=ALPHA, in1=pb[py][0:ntok, :], op0=ALU.mult, op1=ALU.add)
                        else:
                            V("tensor_tensor", [Bx[tt], Bpb[py]], [Bx[tt]], out=xs_, in0=xs_, in1=pb[py][0:ntok, :], op=ALU.add)
                    if g == 1:
                        lp.step(tt, ntok)
                lp.finish()

        def attention(l, q0, nq, ktiles, mix_c0, mask_first, mask_last):
            nt = len(ktiles)
            S.tag = "attn"
            if nq == 128 and len(ktiles) >= 2 and all(kt[5] == 128 for kt in ktiles) and ktiles[-2][0] == "3" and ktiles[-1][0] == "4":
                return attention_fast(l, q0, ktiles, mix_c0, mask_first)
            offs = [i * 128 for i in range(nt)]
            for h in range(8):
                c, hb = h // 2, (h % 2) * 64
                sbank = 2 * rr("sbank", 2)
                pt = rr("PT", 3)
                for i, (kind, kf, Bk, vap, Bv, nk) in enumerate(ktiles):
                    bank = sbank + (offs[i] // 512)
                    col = offs[i] % 512
                    kap = kf(c)
                    mm(pb[bank][0:nk, col:col + nq], kap[hb:hb + 64, 0:nk], qT[hb:hb + 64, c, q0:q0 + nq], True, True, [Bk, BqT], [Bpb[bank]])
                for i, (kind, kf, Bk, vap, Bv, nk) in enumerate(ktiles):
                    bank = sbank + (offs[i] // 512)
                    col = offs[i] % 512
                    o_ = offs[i]
                    if kind == "c":
                        A(PT[pt][0:nk, o_:o_ + nq], pb[bank][0:nk, col:col + nq], AF.Exp, [Bpb[bank], Bcb], [BPT[pt]], bias=cbt[0:nk, l, h:h + 1], scale=0.125)
                        if i == 0 and mask_first:
                            V("memset", [BPT[pt]], [BPT[pt]], ap=PT[pt][0:64, o_ + 64:o_ + 128], constant=0.0)
                    else:
                        et = rr("etmp", 2)
                        eo = 0 if kind == "3" else 128
                        A(etmp[et][0:nk, 0:nq], pb[bank][0:nk, col:col + nq], AF.Exp, [Bpb[bank]], [Bet[et]], scale=0.125)
                        V("tensor_tensor", [Bet[et], Beb], [BPT[pt]], out=PT[pt][0:nk, o_:o_ + nq], in0=etmp[et][0:nk, 0:nq], in1=eb34[0:nk, h, eo:eo + nq], op=ALU.mult)
                        if kind == "4" and mask_last:
                            V("memset", [BPT[pt]], [BPT[pt]], ap=PT[pt][64:128, o_:o_ + 64], constant=0.0)
                ob = 4 + h // 4
                for i, (kind, kf, Bk, vap, Bv, nk) in enumerate(ktiles):
                    o_ = offs[i]
                    mm(pb[ob][0:nq, (h % 4) * 65:(h % 4) * 65 + 65], PT[pt][0:nk, o_:o_ + nq], vap[0:nk, h, :], i == 0, i == nt - 1, [BPT[pt], Bv], [Bpb[ob]])
            for hh in range(2):
                ob = 4 + hh
                pv3 = pb[ob][0:nq, 0:260].rearrange("p (h d) -> p h d", h=4)
                V("reciprocal", [Bpb[ob]], [Brd], out=rden[0:nq, hh * 4:hh * 4 + 4], in_=pv3[:, :, 64])
                V("tensor_tensor", [Bpb[ob], Brd], [Boa], out=oa[0:nq, hh * 256:(hh + 1) * 256].rearrange("p (h d) -> p h d", h=4), in0=pv3[:, :, 0:64],
                  in1=rden[0:nq, hh * 4:hh * 4 + 4].unsqueeze(2).to_broadcast([nq, 4, 64]), op=ALU.mult)
            for c in range(4):
                tr(pbb[6][:, c * 128:c * 128 + nq], oa[0:nq, c * 128:(c + 1) * 128], identb[0:nq, 0:nq], [Boa, Bid], [Bpb[6]])
            A(hT[:, 0:4, mix_c0:mix_c0 + nq], pbb[6][:, 0:512].rearrange("p (c n) -> p c n", c=4)[:, :, 0:nq], AF.Copy, [Bpb[6]], BhT[0:4])

        def attention_fast(l, q0, ktiles, mix_c0, mask_first):
            nq = 128
            ncst = len(ktiles) - 2
            cts, t3, t4 = ktiles[:ncst], ktiles[-2], ktiles[-1]
            order = list(cts) + [t3, t4]
            pcol = [i * 128 for i in range(ncst)] + [384, 512]
            st = {}

            def issue_S(h):
                c, hb = h // 2, (h % 2) * 64
                sbank = 2 * rr("sbank", 2)
                for i, (kind, kf, Bk, vap, Bv, nk) in enumerate(cts):
                    mm(pb[sbank][:, i * 128:(i + 1) * 128], kf(c)[hb:hb + 64, :], qT[hb:hb + 64, c, q0:q0 + nq], True, True, [Bk, BqT], [Bpb[sbank]])
                for i, (kind, kf, Bk, vap, Bv, nk) in enumerate((t3, t4)):
                    mm(pb[sbank + 1][:, i * 128:(i + 1) * 128], kf(c)[hb:hb + 64, :], qT[hb:hb + 64, c, q0:q0 + nq], True, True, [Bk, BqT], [Bpb[sbank + 1]])
                st[h] = sbank

            def issue_E(h):
                sbank = st[h]
                pt = rr("PT", 3)
                if ncst:
                    A(PT[pt][:, 0:ncst * 128], pb[sbank][:, 0:ncst * 128], AF.Exp, [Bpb[sbank], Bcb], [BPT[pt]], bias=cbt[:, l, h:h + 1], scale=0.125)
                    if mask_first:
                        V("memset", [BPT[pt]], [BPT[pt]], ap=PT[pt][0:64, 64:128], constant=0.0)
                et = rr("etmp", 2)
                A(etmp[et][:, 0:256], pb[sbank + 1][:, 0:256], AF.Exp, [Bpb[sbank + 1]], [Bet[et]], scale=0.125)
                V("tensor_tensor", [Bet[et], Beb], [BPT[pt]], out=PT[pt][:, 384:640], in0=etmp[et][:, 0:256], in1=eb34[:, h, 0:256], op=ALU.mult)
                st[h] = pt

            def issue_PV(h):
                pt = st[h]
                ob = 4 + h // 4
                for i, (kind, kf, Bk, vap, Bv, nk) in enumerate(order):
                    mm(pb[ob][:, (h % 4) * 65:(h % 4) * 65 + 65], PT[pt][:, pcol[i]:pcol[i] + nq], vap[:, h, :], i == 0, i == len(order) - 1, [BPT[pt], Bv], [Bpb[ob]])

            issue_S(0)
            for h in range(8):
                if h + 1 < 8:
                    issue_S(h + 1)
                issue_E(h)
                issue_PV(h)
            for hh in range(2):
                ob = 4 + hh
                pv3 = pb[ob][:, 0:260].rearrange("p (h d) -> p h d", h=4)
                V("reciprocal", [Bpb[ob]], [Brd], out=rden[:, hh * 4:hh * 4 + 4], in_=pv3[:, :, 64])
                V("tensor_tensor", [Bpb[ob], Brd], [Boa], out=oa[:, hh * 256:(hh + 1) * 256].rearrange("p (h d) -> p h d", h=4), in0=pv3[:, :, 0:64],
                  in1=rden[:, hh * 4:hh * 4 + 4].unsqueeze(2).to_broadcast([128, 4, 64]), op=ALU.mult)
            for c in range(4):
                tr(pbb[6][:, c * 128:(c + 1) * 128], oa[:, c * 128:(c + 1) * 128], identb[:], [Boa, Bid], [Bpb[6]])
            A(hT[:, 0:4, mix_c0:mix_c0 + nq], pbb[6][:, 0:512].rearrange("p (c n) -> p c n", c=4), AF.Copy, [Bpb[6]], BhT[0:4])

        def pool_group(l, t0, GT, first_of_seq, hist_dram, out_dram, mixc0, w3):
            S.tag = "pool"
            if hist_dram is not None:
                Dm("sp", "ptok", ptok[0:15, :], hist_dram, [], [Bptok])
                for c in range(2):
                    tr(pb[7][:, c * 16:c * 16 + 16], ptok[0:16, c * 128:(c + 1) * 128], ident[0:16, 0:16], [Bptok, Bid], [Bpb[7]])
                V("tensor_copy", [Bpb[7]], [BuT], out=uT[:, :, 1:16], in_=pb[7][:, 0:32].rearrange("p (c n) -> p c n", c=2)[:, :, 0:15])
            else:
                V("tensor_copy", [Bucar[l]], [BuT], out=uT[:, :, 0:16], in_=ucar[:, l, :, :])
            wv, Bw = w3
            for c in range(2):
                bank = rr("pg", 2)
                for kc in range(8):
                    mm(pb[bank][:, 0:GT], wv[:, kc, c * 128:(c + 1) * 128], xT[:, kc, t0:t0 + GT], kc == 0, kc == 7, [Bw] + BxT, [Bpb[bank]])
                A(uT[:, c, 16:16 + GT], pb[bank][:, 0:GT], AF.Copy, [Bpb[bank]], [BuT])
            n = 16 + GT
            V("tensor_tensor", [BuT], [BsA], out=sA[:, :, 2:n], in0=uT[:, :, 2:n], in1=uT[:, :, 1:n - 1], op=ALU.add)
            V("tensor_tensor", [BsA], [BsB], out=sBb[:, :, 4:n], in0=sA[:, :, 4:n], in1=sA[:, :, 2:n - 2], op=ALU.add)

            def dcalc(src, Bsrc, c, hf, w):
                lo, hi = hf * 64, hf * 64 + 64
                V("scalar_tensor_tensor", [Bsrc, BuT], [BdT], out=dT[lo:hi, c, 0:GT], in0=src[lo:hi, c, 16:n], scalar=1.0 / w, in1=uT[lo:hi, c, 16:n], op0=ALU.mult, op1=ALU.subtract)
                if first_of_seq:
                    V("tensor_tensor", [Bsrc, Brc], [Bptmp], out=ptmp[lo:hi, 0:16], in0=src[lo:hi, c, 16:32], in1=rc16[lo:hi, c, :], op=ALU.mult)
                    V("tensor_tensor", [Bptmp, BuT], [BdT], out=dT[lo:hi, c, 0:16], in0=ptmp[lo:hi, 0:16], in1=uT[lo:hi, c, 16:32], op=ALU.subtract)

            dcalc(sA, BsA, 0, 0, 2)
            dcalc(sBb, BsB, 0, 1, 4)
            V("tensor_tensor", [BsB], [BsA], out=sA[:, :, 8:n], in0=sBb[:, :, 8:n], in1=sBb[:, :, 4:n - 4], op=ALU.add)
            dcalc(sA, BsA, 1, 0, 8)
            V("tensor_tensor", [BsA], [BsB], out=sBb[:, :, 16:n], in0=sA[:, :, 16:n], in1=sA[:, :, 8:n - 8], op=ALU.add)
            dcalc(sBb, BsB, 1, 1, 16)
            return lambda: pool_part2(l, GT, hist_dram, out_dram, mixc0, n)

        def pool_part2(l, GT, hist_dram, out_dram, mixc0, n):
            S.tag = "pool"
            for c in range(2):
                bank = rr("pg", 2)
                mm(pb[bank][:, 0:GT], wpool[:, l, c, :], dT[:, c, 0:GT], True, True, [Bwp, BdT], [Bpb[bank]])
                A(hT[:, 4 + c, mixc0:mixc0 + GT], pb[bank][:, 0:GT], AF.Copy, [Bpb[bank], Bps], [BhT[4 + c]], scale=pscale[:, l, c:c + 1])
            if out_dram is not None:
                for c in range(2):
                    tr(pb[7][0:16, 128 + c * 128:256 + c * 128], uT[:, c, n - 16:n], ident[:], [BuT, Bid], [Bpb[7]])
                V("tensor_copy", [Bpb[7]], [Bpst], out=pst[0:16, :], in_=pb[7][0:16, 128:384])
                Dm("sp", "pst", out_dram, pst[1:16, :], [Bpst], [outbuf("o")])
            if hist_dram is None:
                V("tensor_copy", [BuT], [Bucar[l]], out=ucar[:, l, :, :], in_=uT[:, :, GT:GT + 16])

        hg = {"pending": None, "gcount": 0}

        def hgrn_drain():
            if hg["pending"] is not None:
                hg["pending"]()
                hg["pending"] = None

        def hgrn_group(l, t0, GT, L, w3, w4, w5, state_in, state_out, mixc0, chain, after_proj=None):
            nblk = GT // L
            gbase = 0 if nblk == 4 else 2 * (hg["gcount"] % 2)
            hg["gcount"] += 1
            S.tag = "hgrn.proj"
            (wv3, Bw3), (wv4, Bw4), (wv5, Bw5) = w3, w4, w5
            rmask = resetm[:, 0, 0:GT] if L == 64 else resetm[:, 1, 0:GT]
            for h in range(4):
                bank = rr("pg", 2)
                for kc in range(8):
                    mm(pb[bank][0:64, 0:GT], wv3[:, kc, 256 + h * 64:256 + h * 64 + 64], xT[:, kc, t0:t0 + GT], kc == 0, kc == 7, [Bw3] + BxT, [Bpb[bank]])
                A(QS[:, h, 0:GT], pb[bank][0:64, 0:GT], AF.Silu, [Bpb[bank]], [BQS])
            for blk in range(nblk):
                tb = t0 + blk * L
                bank = 2 + rr("pu", 2)
                for kc in range(8):
                    mm(pb[bank][0:L, 0:256], xT[:, kc, tb:tb + L], wv4[:, kc, 256:512], kc == 0, kc == 7, [Bw4] + BxT, [Bpb[bank]])
                for kc in range(8):
                    mm(pb[bank][0:L, 256:512], xT[:, kc, tb:tb + L], wv5[:, kc, 0:256], kc == 0, kc == 7, [Bw5] + BxT, [Bpb[bank]])
                V("tensor_copy", [Bpb[bank]], [BVT[gbase + blk]], out=VTOK[gbase + blk][0:L, :], in_=pb[bank][0:L, 0:256])
                A(G2[gbase + blk][0:L, :], pb[bank][0:L, 256:512], AF.Silu, [Bpb[bank]], [BG2[gbase + blk]])
                V("tensor_tensor", [BG2[gbase + blk], Bnw], [BG2[gbase + blk]], out=G2[gbase + blk][0:L, :].rearrange("p (h d) -> p h d", h=4), in0=G2[gbase + blk][0:L, :].rearrange("p (h d) -> p h d", h=4), in1=normwb[0:L, l, :].unsqueeze(1).to_broadcast([L, 4, 64]), op=ALU.mult)
            for h in range(4):
                bank = rr("pg", 2)
                for kc in range(8):
                    mm(pb[bank][0:64, 0:GT], wv4[:, kc, h * 64:h * 64 + 64], xT[:, kc, t0:t0 + GT], kc == 0, kc == 7, [Bw4] + BxT, [Bpb[bank]])
                A(Fb[:, h, 0:GT], pb[bank][0:64, 0:GT], AF.Sigmoid, [Bpb[bank]], [BF_])
            if after_proj is not None:
                after_proj()
            S.tag = "hgrn.gate"
            V("tensor_tensor", [BF_, Blb], [BF_], out=Fb[:, :, 0:GT], in0=Fb[:, :, 0:GT], in1=omlt[:, l, :].unsqueeze(2).to_broadcast([64, 4, GT]), op=ALU.mult)
            V("tensor_tensor", [BF_, Blb], [BF_], out=Fb[:, :, 0:GT], in0=Fb[:, :, 0:GT], in1=lbt[:, l, :].unsqueeze(2).to_broadcast([64, 4, GT]), op=ALU.add)
            V("tensor_scalar", [BF_], [BKIN], out=KIN[:, :, 0:GT], in0=Fb[:, :, 0:GT], scalar1=-1.0, scalar2=1.0, op0=ALU.mult, op1=ALU.add)
            A(Fb[:, :, 0:GT], Fb[:, :, 0:GT], AF.Ln, [BF_], [BF_])
            for h in range(4):
                V("tensor_tensor_scan", [BF_, Bresetm], [BBc], out=Bc[:, h, 0:GT], data0=rmask, data1=Fb[:, h, 0:GT], initial=0.0, op0=ALU.mult, op1=ALU.add)
            mid = L // 2 - 1
            Bc4 = Bc[:, :, 0:GT].rearrange("p h (b t) -> p h b t", t=L)
            V("tensor_tensor", [BBc, BF_], [BF_], out=Fb[:, :, 0:GT].rearrange("p h (b t) -> p (h b) t", t=L), in0=Bc[:, :, 0:GT].rearrange("p h (b t) -> p (h b) t", t=L),
              in1=Bc[:, :, 0:GT].rearrange("p h (b t) -> p (h b) t", t=L)[:, :, mid:mid + 1].to_broadcast([64, 4 * nblk, L]), op=ALU.subtract)
            A(EQ[:, :, 0:GT], Fb[:, :, 0:GT], AF.Exp, [BF_], [BEQ])
            A(EK[:, :, 0:GT], Fb[:, :, 0:GT], AF.Exp, [BF_], [BEK], scale=-1.0)
            A(EBM[:, :, 0:nblk], Bc4[:, :, :, mid], AF.Exp, [BBc], [BEBM])
            A(EBL[:, :, 0:nblk], Bc4[:, :, :, L - 1], AF.Exp, [BBc], [BEBL])
            V("tensor_tensor", [BQS, BEQ], [BQT], out=QTt[:, :, 0:GT], in0=QS[:, :, 0:GT], in1=EQ[:, :, 0:GT], op=ALU.mult)
            V("tensor_tensor", [BKIN, BEK], [BKT], out=KTt[:, :, 0:GT], in0=KIN[:, :, 0:GT], in1=EK[:, :, 0:GT], op=ALU.mult)
            EQ4 = EQ[:, :, 0:GT].rearrange("p h (b t) -> p h b t", t=L)
            S.tag = "hgrn.blk"
            for blk in range(nblk):
                c0 = blk * L
                if not chain:
                    Dm("sp", "hst", Sst[:, l, :, :], state_in[blk].rearrange("h c v -> c h v"), [], [BSst[l]])
                for h in range(4):
                    tr(pbb[7][0:L, h * 64:(h + 1) * 64], KTt[:, h, c0:c0 + L], identb[0:64, 0:64], [BKT, Bid], [Bpb[7]])
                A(KTOK[0:L, :, :], pbb[7][0:L, 0:256].rearrange("p (h c) -> p h c", h=4), AF.Copy, [Bpb[7]], [BKTOK])
                for h in range(4):
                    mm(pb[6][0:L, h * 64:h * 64 + L], KTt[:, h, c0:c0 + L], QTt[:, h, c0:c0 + L], True, True, [BKT, BQT], [Bpb[6]])
                V("tensor_tensor", [Bpb[6], Btri], [BSM], out=SM[0:L, :, 0:L], in0=pb[6][0:L, 0:256].rearrange("p (h t) -> p h t", h=4)[:, :, 0:L],
                  in1=tri[0:L, 0:L].unsqueeze(1).to_broadcast([L, 4, L]), op=ALU.mult)
                V("tensor_tensor", [BSst[l], BEBM], [BSbf], out=Sbf[:], in0=Sst[:, l, :, :], in1=EBM[:, :, blk:blk + 1].to_broadcast([64, 4, 64]), op=ALU.mult)
                ob = 4 + rr("hgo", 2)
                for h in range(4):
                    mm(pb[ob][0:L, h * 64:(h + 1) * 64], SM[0:L, h, 0:L], VTOK[gbase + blk][0:L, h * 64:(h + 1) * 64], True, False, [BSM, BVT[gbase + blk]], [Bpb[ob]])
                    mm(pb[ob][0:L, h * 64:(h + 1) * 64], QTt[:, h, c0:c0 + L], Sbf[:, h, :], False, True, [BQT, BSbf], [Bpb[ob]])
                mb = 6
                for h in range(4):
                    mm(pb[mb][0:64, 256 + h * 64:256 + (h + 1) * 64], KTOK[0:L, h, :], VTOK[gbase + blk][0:L, h * 64:(h + 1) * 64], True, True, [BKTOK, BVT[gbase + blk]], [Bpb[mb]])
                V("tensor_tensor", [Bpb[mb], BEQ], [BT1], out=T1[:], in0=pb[mb][0:64, 256:512].rearrange("p (h v) -> p h v", h=4),
                  in1=EQ4[:, :, blk, L - 1:L].to_broadcast([64, 4, 64]), op=ALU.mult)
                V("tensor_tensor", [BSst[l], BEBL], [BSst[l]], out=Sst[:, l, :, :], in0=Sst[:, l, :, :], in1=EBL[:, :, blk:blk + 1].to_broadcast([64, 4, 64]), op=ALU.mult)
                V("tensor_tensor", [BSst[l], BT1], [BSst[l]], out=Sst[:, l, :, :], in0=Sst[:, l, :, :], in1=T1[:], op=ALU.add)
                if state_out is not None and (not chain or blk == nblk - 1):
                    so = state_out[blk] if not chain else state_out
                    V("tensor_copy", [BSst[l]], [Bhst], out=hst[:].rearrange("p (h v) -> p h v", h=4), in_=Sst[:, l, :, :])
                    Dm("sp", "hso", so.rearrange("h c v -> c h v"), hst[:].rearrange("p (h v) -> p h v", h=4), [Bhst], [outbuf("o")])
                def post_blk(ob=ob, gi=gbase + blk, c0=c0):
                    S.tag = "hgrn.out"
                    A(osq[0:L, :], pb[ob][0:L, 0:256], AF.Square, [Bpb[ob]], [Bosq])
                    V("tensor_reduce", [Bosq], [Boss], out=oss[0:L, 0:4], in_=osq[0:L, :].rearrange("p (h v) -> p h v", h=4), axis=AX.X, op=ALU.add)
                    V("tensor_scalar", [Boss], [Boss], out=oss[0:L, 0:4], in0=oss[0:L, 0:4], scalar1=1.0 / 64, scalar2=RMS_EPS, op0=ALU.mult, op1=ALU.add)
                    A(oss[0:L, 0:4], oss[0:L, 0:4], AF.Ln, [Boss], [Boss])
                    A(oss[0:L, 4:8], oss[0:L, 0:4], AF.Exp, [Boss], [Boss], scale=-0.5)
                    V("tensor_tensor", [Bpb[ob], Boss, Bosq], [Bosq], out=osq[0:L, :].rearrange("p (h v) -> p h v", h=4), in0=pb[ob][0:L, 0:256].rearrange("p (h v) -> p h v", h=4),
                      in1=oss[0:L, 4:8].unsqueeze(2).to_broadcast([L, 4, 64]), op=ALU.mult)
                    V("tensor_tensor", [Bosq, BG2[gi]], [Boc], out=octok[0:L, :], in0=osq[0:L, :], in1=G2[gi][0:L, :], op=ALU.mult)
                    for c in range(2):
                        tr(pbb[7][:, 512 + c * 64:512 + c * 64 + L], octok[0:L, c * 128:(c + 1) * 128], identb[0:L, 0:L], [Boc, Bid], [Bpb[7]])
                    A(hT[:, 6:8, mixc0 + c0:mixc0 + c0 + L], pbb[7][:, 512:640].rearrange("p (c n) -> p c n", c=2)[:, :, 0:L], AF.Copy, [Bpb[7]], BhT[6:8])
                if hg["pending"] is not None:
                    hg["pending"]()
                hg["pending"] = post_blk

        def mixer(l, T, tiles, nseq, first_tile, last_tile, is_prompt):
            S.tag = "qkv"
            gslot = load_gb(l, 1)
            Dm("sp", "eb", eb34[:], Wd["rb34"][l].rearrange("h r n -> r h n"), [], [Beb])
            A(eb34[:], eb34[:], AF.Exp, [Beb], [Beb])
            V("memset", [Beb], [Beb], ap=eb34[64:128, :, 128:192], constant=0.0)
            wq, Bq_ = wnext("k", "win", l, 0, 512)
            for c in range(4):
                bank = rr("pg", 2)
                for kc in range(8):
                    mm(pb[bank][:, 0:T], wq[:, kc, c * 128:(c + 1) * 128], xT[:, kc, 0:T], kc == 0, kc == 7, [Bq_] + BxT, [Bpb[bank]])
                A(qT[:, c, 0:T], pb[bank][:, 0:T], AF.Copy, [Bpb[bank]], [BqT])
            wk, Bk_ = wnext("k", "win", l, 512, 512)
            for c in range(4):
                bank = rr("pg", 2)
                for kc in range(8):
                    mm(pb[bank][:, 0:T], wk[:, kc, c * 128:(c + 1) * 128], xT[:, kc, 0:T], kc == 0, kc == 7, [Bk_] + BxT, [Bpb[bank]])
                V("tensor_copy", [Bpb[bank]], [Bkcur], out=kcur[:, c, 0:T], in_=pb[bank][:, 0:T])
            kout = (last_tile or not is_prompt)
            if kout:
                for (tt, ntok) in tiles:
                    bank = 2 + rr("pu", 2)
                    for kc in range(8):
                        mm(pb[bank][0:ntok, :], xT[:, kc, tt * 128:tt * 128 + ntok], wk[:, kc, :], kc == 0, kc == 7, [Bk_] + BxT, [Bpb[bank]])
                    ks = rr("kvst", 2)
                    A(kvst[ks][0:ntok, :], pb[bank][0:ntok, :], AF.Copy, [Bpb[bank]], [Bkvst[ks]])
                    dst = (pk[l][tt * 128:tt * 128 + ntok, :] if is_prompt else sk[l][tt * 128:tt * 128 + ntok, :])
                    Dm("sp", f"kvo{ks}", dst, kvst[ks][0:ntok, :], [Bkvst[ks]], [outbuf("o")])
            wvv, Bv_ = wnext("k", "win", l, 1024, 512)
            vtiles = [(tt, tt * 128, ntok) for (tt, ntok) in tiles] if is_prompt else [(si, si * 32, 32) for si in range(nseq)]
            for (vi, tk0, ntok) in vtiles:
                bank = 2 + rr("pu", 2)
                for kc in range(8):
                    mm(pb[bank][0:ntok, :], xT[:, kc, tk0:tk0 + ntok], wvv[:, kc, :], kc == 0, kc == 7, [Bv_] + BxT, [Bpb[bank]])
                V("tensor_copy", [Bpb[bank]], [Bvcur[vi]], out=vcur[0:ntok, vi, :, 0:64], in_=pb[bank][0:ntok, :].rearrange("p (h d) -> p h d", h=8))
                if kout:
                    ks = rr("kvst", 2)
                    A(kvst[ks][0:ntok, :], pb[bank][0:ntok, :], AF.Copy, [Bpb[bank], Bvcur[vi]], [Bkvst[ks]])
                    dst = (pv[l][tk0:tk0 + ntok, :] if is_prompt else sv[l][tk0:tk0 + ntok, :])
                    Dm("sp", f"kvo{ks}", dst, kvst[ks][0:ntok, :], [Bkvst[ks]], [outbuf("o")])
            if is_prompt:
                for p in range(4):
                    kts = []
                    for i in range(5):
                        g = p - 4 + i
                        kind = "c" if i < 3 else ("3" if i == 3 else "4")
                        if g < 0:
                            if first_tile:
                                continue
                            ci = 4 + g
                            kts.append((kind, (lambda c, ci=ci: kc_l[l][:, c, ci * 128:(ci + 1) * 128]), Bkc[l][ci], vc_l[l][:, ci, :, :], Bvc[l][ci], 128))
                        else:
                            kts.append((kind, (lambda c, g=g: kcur[:, c, g * 128:(g + 1) * 128]), Bkcur, vcur[:, g, :, :], Bvcur[g], 128))
                    attention(l, p * 128, 128, kts, p * 128, mask_first=(len(kts) == 5), mask_last=True)
            else:
                for si in range(nseq):
                    for ci in range(4):
                        Dm("pool", "ckst", ckst[:], ck[l, si, ci * 128:(ci + 1) * 128, :], [], [Bckst])
                        for c in range(4):
                            tr(pbb[6][:, c * 128:(c + 1) * 128], ckst[:, c * 128:(c + 1) * 128], identb[:], [Bckst, Bid], [Bpb[6]])
                        A(kc_l[l][:, :, ci * 128:(ci + 1) * 128], pbb[6][:, 0:512].rearrange("p (c n) -> p c n", c=4), AF.Copy, [Bpb[6]], [Bkc[l][ci]])
                        Dm("pool", f"vc{ci}", vc_l[l][:, ci, :, 0:64], cv[l, si, ci * 128:(ci + 1) * 128, :].rearrange("p (h d) -> p h d", h=8), [], [Bvc[l][ci]])
                    kts = []
                    for i in range(4):
                        kind = "c" if i < 3 else "3"
                        kts.append((kind, (lambda c, i=i: kc_l[l][:, c, i * 128:(i + 1) * 128]), Bkc[l][i], vc_l[l][:, i, :, :], Bvc[l][i], 128))
                    kts.append(("4", (lambda c, si=si: kcur[:, c, si * 32:si * 32 + 32]), Bkcur, vcur[0:32, si, :, :], Bvcur[si], 32))
                    attention(l, si * 32, 32, kts, si * 32, mask_first=False, mask_last=False)
            S.tag = "carry"
            if is_prompt and not last_tile:
                A(kc_l[l][:], kcur[:], AF.Copy, [Bkcur], Bkc[l])
                V("tensor_copy", Bvcur, Bvc[l], out=vc_l[l][:, :, :, 0:64], in_=vcur[:, :, :, 0:64])
            w3 = wnext("k", "win", l, 1536, 512)
            w4 = wnext("k", "win", l, 2048, 512)
            w5 = wnext("k", "win", l, 2560, 256)
            if is_prompt:
                if first_tile:
                    V("memset", [], [Bucar[l]], ap=ucar[:, l, :, :], constant=0.0)
                    V("memset", [], [BSst[l]], ap=Sst[:, l, :, :], constant=0.0)
                for g in range(4):
                    lastg = last_tile and g == 3
                    p2 = pool_group(l, g * 128, 128, first_tile and g == 0, None, ppool[l] if lastg else None, g * 128, w3)
                    hgrn_group(l, g * 128, 128, 64, w3, w4, w5, None, phg[l] if lastg else None, g * 128, True, after_proj=p2)
            else:
                for si in range(nseq):
                    pool_group(l, si * 32, 32, False, spool[l, si], spool_o[l, si], si * 32, w3)()
                hgrn_group(l, 0, 32 * nseq, 32, w3, w4, w5, [shg[l, si] for si in range(nseq)], [shg_o[l, si] for si in range(nseq)], 0, False)
            hgrn_drain()
            S.tag = "wout"
            wo = [wnext("k", "wout", l, 0, 512), wnext("k", "wout", l, 512, 512)]
            lp = LNPipe(l, 1, gslot)
            for (tt, ntok) in tiles:
                S.tag = "wout"
                pys = []
                for nh in range(2):
                    py = 4 + rr("py", 2)
                    pys.append(py)
                    wv_, Bw_ = wo[nh]
                    for kc in range(8):
                        mm(pb[py][0:ntok, :], hT[:, kc, tt * 128:tt * 128 + ntok], wv_[:, kc, :], kc == 0, kc == 7, [BhT[kc], Bw_], [Bpb[py]])
                lp.before_acc()
                for nh in range(2):
                    xs_ = x[0:ntok, tt, nh * 512:(nh + 1) * 512]
                    V("scalar_tensor_tensor", [Bx[tt], Bpb[pys[nh]]], [Bx[tt]], out=xs_, in0=xs_, scalar=ALPHA, in1=pb[pys[nh]][0:ntok, :], op0=ALU.mult, op1=ALU.add)
                lp.step(tt, ntok)
            lp.finish()

        def ple(l, T, tiles, prow0, is_prompt):
            S.tag = "ple"
            gslot = load_gb(l, 3)
            psrc = pp if is_prompt else ps_
            for (tt, ntok) in tiles:
                Dm("sp", "ptokp", ptokp[0:ntok, 0, :], psrc[l, prow0 + tt * 128:prow0 + tt * 128 + ntok, :], [], [Bptokp])
                for c in range(2):
                    tr(pb[7][:, c * 128:c * 128 + ntok], ptokp[0:ntok, 0, c * 128:(c + 1) * 128], ident[0:ntok, 0:ntok], [Bptokp, Bid], [Bpb[7]])
                V("tensor_copy", [Bpb[7]], [BpT], out=pTt[:, :, tt * 128:tt * 128 + ntok], in_=pb[7][:, 0:256].rearrange("p (c n) -> p c n", c=2)[:, :, 0:ntok])
            wg = [wnext("k", "pleg", l, 0, 512), wnext("k", "pleg", l, 512, 512)]
            wp_, Bwp_ = wnext("j", "plep", l, 0, 2)
            lp = LNPipe(l, 3, gslot)
            for (tt, ntok) in tiles:
                S.tag = "ple"
                banks = []
                for nh in range(2):
                    pg = rr("pg", 2); pu = 2 + rr("pu", 2)
                    banks.append((pg, pu))
                    wv_, Bw_ = wg[nh]
                    for kc in range(8):
                        mm(pb[pg][0:ntok, :], xT[:, kc, tt * 128:tt * 128 + ntok], wv_[:, kc, :], kc == 0, kc == 7, [BxT[tt], Bw_], [Bpb[pg]])
                    for c in range(2):
                        mm(pb[pu][0:ntok, :], pTt[:, c, tt * 128:tt * 128 + ntok], wp_[:, c, nh * 512:(nh + 1) * 512], c == 0, c == 1, [BpT, Bwp_], [Bpb[pu]])
                lp.before_acc()
                for nh in range(2):
                    pg, pu = banks[nh]
                    ei = rr("etile", 2)
                    A(etile[ei][0:ntok, :], pb[pg][0:ntok, :], AF.Sigmoid, [Bpb[pg]], [Bet2[ei]])
                    V("tensor_tensor", [Bet2[ei], Bpb[pu]], [Bet2[ei]], out=etile[ei][0:ntok, :], in0=etile[ei][0:ntok, :], in1=pb[pu][0:ntok, :], op=ALU.mult)
                    xs_ = x[0:ntok, tt, nh * 512:(nh + 1) * 512]
                    V("scalar_tensor_tensor", [Bx[tt], Bet2[ei]], [Bx[tt]], out=xs_, in0=xs_, scalar=ALPHA, in1=etile[ei][0:ntok, :], op0=ALU.mult, op1=ALU.add)
                lp.step(tt, ntok)
            lp.finish()

        def run_tile(xsrc, ydst, row0, T, tiles, is_prompt, first_tile, last_tile, nseq):
            for (tt, ntok) in tiles:
                Dm("sp", f"xin{tt}", x[0:ntok, tt, :], xsrc[row0 + tt * 128:row0 + tt * 128 + ntok, :], [], [Bx[tt]])
                transposes(tt, ntok)
            for l in range(depth):
                ffn(l, "f1", 0, T, tiles)
                mixer(l, T, tiles, nseq, first_tile, last_tile, is_prompt)
                ffn(l, "f2", 2, T, tiles)
                ple(l, T, tiles, row0, is_prompt)
            for (tt, ntok) in tiles:
                Dm("sp", f"xout{tt}", ydst[row0 + tt * 128:row0 + tt * 128 + ntok, :], x[0:ntok, tt, :], [Bx[tt]], [outbuf("o")])

        for ti in range(self.n_ptiles):
            run_tile(xp, yp, ti * 512, 512, [(t, 128) for t in range(4)], True, ti == 0, ti == self.n_ptiles - 1, 0)
        if self.n_samp:
            run_tile(xs, ys, 0, NS, [(0, NS)], False, False, False, nsm)
        assert consumed[0] == len(plan), (consumed[0], len(plan))
        self.nsem = S.emit(nc, outbufs)
        self.st.close()
        return nc


def _rel_tables(attn_rel_bias):
    depth = attn_rel_bias.shape[0]
    r = np.arange(128)[:, None]
    j = np.arange(128)[None, :]
    d3 = np.clip(128 - r + j, -63, 128) + 63
    d4 = np.clip(j - r, -63, 128) + 63
    rb34 = np.concatenate([attn_rel_bias[:, :, d3], attn_rel_bias[:, :, d4]], axis=-1)
    rbc = attn_rel_bias[:, :, 191]
    return np.ascontiguousarray(rb34, dtype=np.float32), np.ascontiguousarray(rbc, dtype=np.float32)


_CACHE = {}


def run(inputs, depth, n_ptiles, n_samp_per_core, n_cores, prompt_of_core, samp_of_core):
    key = (depth, n_ptiles, n_samp_per_core)
    if key not in _CACHE:
        _CACHE[key] = Prog(depth, n_ptiles, n_samp_per_core).build()
    nc = _CACHE[key]
    f = lambda a: np.ascontiguousarray(np.asarray(a), dtype=np.float32)
    rb34, rbc = _rel_tables(f(inputs["attn_rel_bias"]))
    common = {
        "f1gu": f(inputs["ffn1_w_gu"]), "f1d": f(inputs["ffn1_w_down"]), "win": f(inputs["w_in"]), "wout": f(inputs["w_out"]),
        "f2gu": f(inputs["ffn2_w_gu"]), "f2d": f(inputs["ffn2_w_down"]), "pleg": f(inputs["ple_w_gate"]), "plep": f(inputs["ple_w_proj"]),
        "lng": f(inputs["ln_g"]), "lnb": f(inputs["ln_b"]), "poolw": f(inputs["pool_w"]), "pscale": f(inputs["pool_scale"]),
        "lbraw": f(inputs["hgrn_lower_bounds"]), "normw": f(inputs["hgrn_norm_w"]), "rb34": rb34, "rbc": rbc,
    }
    xp, pp_ = f(inputs["x_prompt"]), f(inputs["p_prompt"])
    xs, ps_ = f(inputs["x_sample"]), f(inputs["p_sample"])
    ck, cv = f(inputs["cache_attn_k"]), f(inputs["cache_attn_v"])
    sp_, sh = f(inputs["state_pool"]), f(inputs["state_hgrn"])
    in_maps = []
    for c in range(n_cores):
        b = prompt_of_core[c]
        sb = samp_of_core[c]
        m = dict(common)
        m["xp"] = xp[b]; m["pp"] = np.ascontiguousarray(pp_[:, b])
        m["xs"] = np.ascontiguousarray(xs[sb].reshape(-1, D)); m["ps"] = np.ascontiguousarray(ps_[:, sb].reshape(depth, -1, 256))
        m["ck"] = np.ascontiguousarray(ck[:, sb].reshape(depth, len(sb), 512, 512)); m["cv"] = np.ascontiguousarray(cv[:, sb].reshape(depth, len(sb), 512, 512))
        m["spool"] = np.ascontiguousarray(sp_[:, sb]); m["shg"] = np.ascontiguousarray(sh[:, sb])
        in_maps.append(m)
    res = run_bass_kernel_spmd(nc, in_maps, core_ids=list(range(n_cores)))
    return res.results


def kernel(**inputs):
    depth = 4
    n_cores = 8
    prompt_of_core = [c % 4 for c in range(n_cores)]
    samp_of_core = [list(range(4 * c, 4 * c + 4)) for c in range(n_cores)]
    r = run(inputs, depth, 16, 4, n_cores, prompt_of_core, samp_of_core)
    B, SEQ = 4, 8192
    y_prompt = np.stack([r[b]["yp"] for b in range(B)]).astype(np.float32)
    y_sample = np.concatenate([r[c]["ys"].reshape(4, 32, D) for c in range(n_cores)]).astype(np.float32)
    pk = np.stack([r[b]["pk"].reshape(depth, 512, 512) for b in range(B)], axis=1).reshape(depth, B, 512, 8, 64).astype(np.float32)
    pv = np.stack([r[b]["pv"].reshape(depth, 512, 512) for b in range(B)], axis=1).reshape(depth, B, 512, 8, 64).astype(np.float32)
    ppool = np.stack([r[b]["ppool"].reshape(depth, 15, 256) for b in range(B)], axis=1).astype(np.float32)
    phg = np.stack([r[b]["phg"].reshape(depth, 4, 64, 64) for b in range(B)], axis=1).astype(np.float32)
    sk = np.concatenate([r[c]["sk"].reshape(depth, 4, 32, 8, 64) for c in range(n_cores)], axis=1).astype(np.float32)
    sv = np.concatenate([r[c]["sv"].reshape(depth, 4, 32, 8, 64) for c in range(n_cores)], axis=1).astype(np.float32)
    spool = np.concatenate([r[c]["spool_o"].reshape(depth, 4, 15, 256) for c in range(n_cores)], axis=1).astype(np.float32)
    shg = np.concatenate([r[c]["shg_o"].reshape(depth, 4, 4, 64, 64) for c in range(n_cores)], axis=1).astype(np.float32)
    return (y_prompt, y_sample, pk, pv, ppool, phg, sk, sv, spool, shg)
```

```python
import contextlib
import numpy as np
import concourse.bass as bass
import concourse.mybir as mybir
from concourse.bass_utils import run_bass_kernel_spmd

F32 = mybir.dt.float32
BF16 = mybir.dt.bfloat16
AF = mybir.ActivationFunctionType
ALU = mybir.AluOpType
AX = mybir.AxisListType

D = 1024
HID = 2816
INW = 2816
ALPHA = float(8 ** 0.25)
LN_EPS = 1e-5
RMS_EPS = 1e-6
NSLOT = 6
STREAMS = ("pe", "act", "dve", "pool", "sp")


class Buf:
    __slots__ = ("name", "lw", "rd", "excl")

    def __init__(self, name, excl=False):
        self.name = name
        self.lw = None
        self.rd = []
        self.excl = excl


class Op:
    __slots__ = ("stream", "fn", "waits", "sig", "comp", "cidx", "tag")

    def __init__(self, stream, fn, comp, cidx):
        self.stream, self.fn, self.comp, self.cidx = stream, fn, comp, cidx
        self.tag = None
        self.waits = []
        self.sig = False


class Sched:
    def __init__(self):
        self.ops = {s: [] for s in STREAMS}
        self.comp_ops = {}
        self.seen = {s: {} for s in STREAMS}
        self.nops = 0
        self.tag = "init"
        self.annotate = False

    def _add(self, stream, fn, reads, writes, dkey=None):
        comp = stream if dkey is None else "d:" + dkey
        lst = self.comp_ops.setdefault(comp, [])
        op = Op(stream, fn, comp, len(lst))
        op.tag = self.tag
        if dkey is not None:
            op.sig = True
        lst.append(op)
        me = (comp, op.cidx)
        deps = {}

        def need(d):
            if d is None:
                return
            e, i = d
            if e == "pe" and comp == "pe":
                return
            if deps.get(e, -1) < i:
                deps[e] = i

        for b in reads:
            need(b.lw)
            if b.excl:
                for r in b.rd:
                    if r[0] != comp:
                        need(r)
        for b in writes:
            need(b.lw)
            for r in b.rd:
                need(r)
        if dkey is not None and op.cidx > 0:
            need((comp, op.cidx - 1))
        seen = self.seen[stream]
        for e, i in deps.items():
            if seen.get(e, -1) >= i:
                continue
            seen[e] = i
            op.waits.append((e, i))
            self.comp_ops[e][i].sig = True
        for b in reads:
            if len(b.rd) > 64:
                last = {}
                for r in b.rd:
                    if last.get(r[0], -1) < r[1]:
                        last[r[0]] = r[1]
                b.rd = list(last.items())
            b.rd.append(me)
        for b in writes:
            b.lw = me
            b.rd = []
        self.ops[stream].append(op)
        self.nops += 1
        return op

    def mm(self, out, lhsT, rhs, start, stop, reads, writes):
        return self._add("pe", ("matmul", dict(out=out, lhsT=lhsT, rhs=rhs, start=start, stop=stop)), reads, writes)

    def tr(self, out, in_, identity, reads, writes):
        return self._add("pe", ("transpose", dict(out=out, in_=in_, identity=identity)), reads, writes)

    def A(self, out, in_, func, reads, writes, **kw):
        return self._add("act", ("activation", dict(out=out, in_=in_, func=func, **kw)), reads, writes)

    def V(self, name, reads, writes, **kw):
        return self._add("dve", (name, kw), reads, writes)

    def G(self, name, reads, writes, **kw):
        return self._add("pool", (name, kw), reads, writes)

    def D(self, stream, dkey, out, in_, reads, writes, slow=False):
        kw = dict(out=out, in_=in_)
        if slow:
            kw["allow_slow_non_contiguous"] = True
        return self._add(stream, ("dma_start", kw), reads, writes, dkey=dkey)

    def emit(self, nc, final_bufs):
        self._add("sp", None, final_bufs, [])
        rank = {}
        for comp, lst in self.comp_ops.items():
            c = 0
            for op in lst:
                if op.sig:
                    c += 1
                    rank[(comp, op.cidx)] = c
        with contextlib.ExitStack() as st:
            sems = {}
            for comp in self.comp_ops:
                if any(op.sig for op in self.comp_ops[comp]):
                    sems[comp] = st.enter_context(nc.semaphore("s_" + comp.replace(":", "_")))
            block = st.enter_context(nc.Block())

            def run(stream):
                def body(eng):
                    for op in self.ops[stream]:
                        for (e, i) in op.waits:
                            eng.wait_ge(sems[e], rank[(e, i)] * (16 if e.startswith("d:") else 1))
                        if op.fn is None:
                            continue
                        ins = getattr(eng, op.fn[0])(**op.fn[1])
                        if self.annotate:
                            ins.annotate(op.tag)
                        if op.sig:
                            ins.then_inc(sems[op.comp], 16 if op.comp.startswith("d:") else 1)
                return body

            block.tensor(run("pe"))
            block.scalar(run("act"))
            block.vector(run("dve"))
            block.gpsimd(run("pool"))
            block.sync(run("sp"))
        return len(sems)


class Prog:
    def __init__(self, depth, n_ptiles, n_samp):
        self.depth, self.n_ptiles, self.n_samp = depth, n_ptiles, n_samp
        self.nc = bass.Bass("TRN2", target_bir_lowering=False)
        self.S = Sched()
        self.st = contextlib.ExitStack()
        self.rr = {}

    def dram(self, name, shape, kind):
        return self.nc.dram_tensor(name, list(shape), F32, kind=kind).ap()

    def sb(self, name, shape, dt=F32):
        return self.st.enter_context(self.nc.sbuf_tensor("sb_" + name, list(shape), dt))

    def build(self):
        nc, S, depth = self.nc, self.S, self.depth
        NP = self.n_ptiles * 512
        NS = self.n_samp * 32
        nsm = self.n_samp
        I, O = "ExternalInput", "ExternalOutput"
        dr = self.dram
        xp = dr("xp", [max(NP, 1), D], I); pp = dr("pp", [depth, max(NP, 1), 256], I)
        xs = dr("xs", [NS, D], I); ps_ = dr("ps", [depth, NS, 256], I)
        ck = dr("ck", [depth, nsm, 512, 512], I); cv = dr("cv", [depth, nsm, 512, 512], I)
        spool = dr("spool", [depth, nsm, 15, 256], I); shg = dr("shg", [depth, nsm, 4, 64, 64], I)
        Wd = {}
        for nm, shp in [("f1gu", [depth, D, 2 * HID]), ("f1d", [depth, HID, D]), ("win", [depth, D, INW]),
                        ("wout", [depth, D, D]), ("f2gu", [depth, D, 2 * HID]), ("f2d", [depth, HID, D]),
                        ("pleg", [depth, D, D]), ("plep", [depth, 256, D]), ("lng", [depth, 4, D]),
                        ("lnb", [depth, 4, D]), ("poolw", [depth, 4, 64, 64]), ("pscale", [depth, 256]),
                        ("lbraw", [depth, 256]), ("normw", [depth, 64]), ("rb34", [depth, 8, 128, 256]),
                        ("rbc", [depth, 8])]:
            Wd[nm] = dr(nm, shp, I)
        yp = dr("yp", [max(NP, 1), D], O); ys = dr("ys", [NS, D], O)
        pk2 = dr("pk", [depth * 512, 512], O); pv2 = dr("pv", [depth * 512, 512], O)
        ppool2 = dr("ppool", [depth * 15, 256], O); phg2 = dr("phg", [depth * 256, 64], O)
        sk2 = dr("sk", [depth * NS, 512], O); sv2 = dr("sv", [depth * NS, 512], O)
        spool2 = dr("spool_o", [depth * nsm * 15, 256], O); shg2 = dr("shg_o", [depth * nsm * 256, 64], O)
        pk = [pk2[l * 512:(l + 1) * 512, :] for l in range(depth)]; pv = [pv2[l * 512:(l + 1) * 512, :] for l in range(depth)]
        sk = [sk2[l * NS:(l + 1) * NS, :] for l in range(depth)]; sv = [sv2[l * NS:(l + 1) * NS, :] for l in range(depth)]
        ppool = [ppool2[l * 15:(l + 1) * 15, :] for l in range(depth)]
        phg = [phg2[l * 256:(l + 1) * 256, :].rearrange("(h c) v -> h c v", h=4) for l in range(depth)]
        spool_o = {(l, si): spool2[(l * nsm + si) * 15:(l * nsm + si + 1) * 15, :] for l in range(depth) for si in range(nsm)}
        shg_o = {(l, si): shg2[(l * nsm + si) * 256:(l * nsm + si + 1) * 256, :].rearrange("(h c) v -> h c v", h=4) for l in range(depth) for si in range(nsm)}
        outbufs = []

        def outbuf(name):
            b = Buf(name)
            outbufs.append(b)
            return b

        sb = self.sb
        x = sb("x", [128, 4, D]); Bx = [Buf(f"x{t}") for t in range(4)]
        xT = sb("xT", [128, 8, 512], BF16); BxT = [Buf(f"xT{t}") for t in range(4)]
        hT = sb("hT", [128, 12, 512], BF16); BhT = [Buf(f"hT{j}") for j in range(12)]
        ring = [sb(f"ring{i}", [128, 4096], BF16) for i in range(NSLOT)]
        Bring = [Buf(f"ring{i}") for i in range(NSLOT)]
        gb = [sb(f"gb{i}", [128, 2, D]) for i in range(2)]; Bgb = [Buf(f"gb{i}") for i in range(2)]
        ident = sb("ident", [128, 128]); identb = sb("identb", [128, 128], BF16); Bid = Buf("ident")
        tri = sb("tri", [64, 64]); Btri = Buf("tri")
        resetm = sb("resetm", [64, 2, 128]); Bresetm = Buf("resetm")
        ptmp = sb("ptmp", [128, 16]); Bptmp = Buf("ptmp")
        mhalf = sb("mhalf", [128, 1]); Bmh = Buf("mhalf")
        sgb = [sb(f"sg{i}", [128, 512]) for i in range(2)]; Bsg = [Buf(f"sg{i}") for i in range(2)]
        stt = [sb(f"stt{i}", [128, 24]) for i in range(4)]; Bstt = [Buf(f"stt{i}") for i in range(4)]
        kc_l = [sb(f"kcar{l}", [128, 4, 512], BF16) for l in range(depth)]
        vc_l = [sb(f"vcar{l}", [128, 4, 8, 65], BF16) for l in range(depth)]
        Bkc = [[Buf(f"kc{l}_{t}") for t in range(4)] for l in range(depth)]
        Bvc = [[Buf(f"vc{l}_{t}") for t in range(4)] for l in range(depth)]
        kcur = sb("kcur", [128, 4, 512], BF16); Bkcur = Buf("kcur")
        vcur = sb("vcur", [128, 4, 8, 65], BF16); Bvcur = [Buf(f"vcur{t}") for t in range(4)]
        qT = sb("qT", [128, 4, 512], BF16); BqT = Buf("qT")
        eb34 = sb("eb34", [128, 8, 256]); Beb = Buf("eb34")
        cbt = sb("cbt", [128, depth, 8]); Bcb = Buf("cbt")
        PT = [sb(f"PT{i}", [128, 640], BF16) for i in range(3)]; BPT = [Buf(f"PT{i}") for i in range(3)]
        etmp = [sb(f"etmp{i}", [128, 256]) for i in range(2)]; Bet = [Buf(f"etmp{i}") for i in range(2)]
        oa = sb("oa", [128, 512], BF16); Boa = Buf("oa")
        rden = sb("rden", [128, 8]); Brd = Buf("rden")
        kvst = sgb; Bkvst = Bsg
        uT = sb("uT", [128, 2, 144]); BuT = Buf("uT")
        sA = sb("sA", [128, 2, 144]); BsA = Buf("sA")
        sBb = sb("sBb", [128, 2, 144]); BsB = Buf("sBb")
        dT = sb("dT", [128, 2, 128], BF16); BdT = Buf("dT")
        ucar = sb("ucar", [128, depth, 2, 16]); Bucar = [Buf(f"ucar{l}") for l in range(depth)]
        wpool = sb("wpool", [128, depth, 2, 128], BF16); Bwp = Buf("wpool")
        pscale = sb("pscale", [128, depth, 2]); Bps = Buf("pscale")
        rc16 = sb("rc16", [128, 2, 16]); Brc = Buf("rc16")
        pst = sb("pst", [16, 256]); Bpst = Buf("pst")
        ptok = sb("ptok", [16, 256]); Bptok = Buf("ptok")
        lbt = sb("lbt", [64, depth, 4]); omlt = sb("omlt", [64, depth, 4]); Blb = Buf("lb")
        lbtmp = sb("lbtmp", [64, depth, 4])
        lbsum = sb("lbsum", [64, 4])
        normwb = sb("normwb", [64, depth, 64]); Bnw = Buf("normw")
        Sst = sb("Sst", [64, depth, 4, 64]); BSst = [Buf(f"Sst{l}") for l in range(depth)]
        QS = sb("QS", [64, 4, 128]); BQS = Buf("QS")
        Fb = sb("Fb", [64, 4, 128]); BF_ = Buf("F")
        KIN = sb("KIN", [64, 4, 128]); BKIN = Buf("KIN")
        Bc = sb("Bc", [64, 4, 128]); BBc = Buf("Bc")
        EQ = sb("EQ", [64, 4, 128]); BEQ = Buf("EQ")
        EK = sb("EK", [64, 4, 128]); BEK = Buf("EK")
        QTt = sb("QTt", [64, 4, 128], BF16); BQT = Buf("QTt")
        KTt = sb("KTt", [64, 4, 128], BF16); BKT = Buf("KTt")
        EBM = sb("EBM", [64, 4, 8]); BEBM = Buf("EBM")
        EBL = sb("EBL", [64, 4, 8]); BEBL = Buf("EBL")
        VTOK = [sb(f"VTOK{i}", [64, 256], BF16) for i in range(4)]; BVT = [Buf(f"VTOK{i}") for i in range(4)]
        G2 = [sb(f"G2{i}", [64, 256]) for i in range(4)]; BG2 = [Buf(f"G2{i}") for i in range(4)]
        KTOK = sb("KTOK", [64, 4, 64], BF16); BKTOK = Buf("KTOK")
        SM = sb("SM", [64, 4, 64], BF16); BSM = Buf("SM")
        Sbf = sb("Sbf", [64, 4, 64], BF16); BSbf = Buf("Sbf")
        T1 = sb("T1", [64, 4, 64]); BT1 = Buf("T1")
        osq = sb("osq", [64, 256]); Bosq = Buf("osq")
        oss = sb("oss", [64, 8]); Boss = Buf("oss")
        octok = sb("octok", [64, 256], BF16); Boc = Buf("octok")
        hst = sb("hst", [64, 256]); Bhst = Buf("hst")
        ptokp = sb("ptokp", [128, 1, 256]); Bptokp = Buf("ptokp")
        pTt = sb("pTt", [128, 2, 512], BF16); BpT = Buf("pTt")
        etile = sgb; Bet2 = Bsg
        ckst = oa; Bckst = Boa
        pb = [self.st.enter_context(nc.psum_tensor(f"pb{i}", [128, 512], F32)) for i in range(8)]
        Bpb = [Buf(f"pb{i}", excl=True) for i in range(8)]
        pbb = [p.bitcast(BF16) for p in pb]

        def rr(key, n):
            v = self.rr.get(key, 0)
            self.rr[key] = v + 1
            return v % n

        plan = []
        consumed = [0]
        issued = [0]

        def layer_plan(l):
            p = []
            for f in ("f1", "f2"):
                q = []
                for g in (0, 1):
                    for b in ((0, 1, 2) if g == 0 else (3, 4, 5)):
                        nc_ = 512 if b < 5 else 256
                        q.append(("k", f + "gu", l, b * 512, nc_))
                        q.append(("k", f + "gu", l, HID + b * 512, nc_))
                    for b in ((0, 1, 2) if g == 0 else (3, 4, 5)):
                        q.append(("j", f + "d", l, b * 512, 4 if b < 5 else 2))
                if f == "f1":
                    p += q
                    for b in range(6):
                        p.append(("k", "win", l, b * 512, 512 if b < 5 else 256))
                    p.append(("k", "wout", l, 0, 512)); p.append(("k", "wout", l, 512, 512))
                else:
                    p += q
                    p.append(("k", "pleg", l, 0, 512)); p.append(("k", "pleg", l, 512, 512))
                    p.append(("j", "plep", l, 0, 2))
            return p

        ntiles_total = self.n_ptiles + (1 if self.n_samp else 0)
        for _ in range(ntiles_total):
            for l in range(depth):
                plan.extend(layer_plan(l))

        def issue_upto(n):
            while issued[0] < min(n, len(plan)):
                i = issued[0]
                kind, nm, l, a0, n_ = plan[i]
                s = i % NSLOT
                if kind == "k":
                    src = Wd[nm][l][:, a0:a0 + n_].rearrange("(k p) n -> p k n", p=128)
                    dst = ring[s][:, 0:8 * n_].rearrange("p (k n) -> p k n", k=8)
                else:
                    src = Wd[nm][l][a0:a0 + n_ * 128, :].rearrange("(j p) n -> p j n", p=128)
                    dst = ring[s][:, 0:n_ * 1024].rearrange("p (j n) -> p j n", j=n_)
                S.D("pool", f"ring{s}", dst, src, [], [Bring[s]])
                issued[0] += 1

        def wnext(kind, nm, l, a0, n_):
            i = consumed[0]
            assert plan[i] == (kind, nm, l, a0, n_), (plan[i], (kind, nm, l, a0, n_))
            issue_upto(i + NSLOT - 2)
            consumed[0] += 1
            s = i % NSLOT
            if kind == "k":
                return ring[s][:, 0:8 * n_].rearrange("p (k n) -> p k n", k=8), Bring[s]
            return ring[s][:, 0:n_ * 1024].rearrange("p (j n) -> p j n", j=n_), Bring[s]

        mm, tr, A, V, G, Dm = S.mm, S.tr, S.A, S.V, S.G, S.D
        G("memset", [], [Bid], ap=ident[:], constant=1.0)
        G("affine_select", [Bid], [Bid], out=ident[:], in_=ident[:], pattern=[[-1, 128]], compare_op=ALU.is_equal, fill=0.0, base=0, channel_multiplier=1)
        V("tensor_copy", [Bid], [Bid], out=identb[:], in_=ident[:])
        G("memset", [], [Btri], ap=tri[:], constant=1.0)
        G("affine_select", [Btri], [Btri], out=tri[:], in_=tri[:], pattern=[[1, 64]], compare_op=ALU.is_ge, fill=0.0, base=0, channel_multiplier=-1)
        G("memset", [], [Bresetm], ap=resetm[:], constant=1.0)
        for blk in range(2):
            G("memset", [Bresetm], [Bresetm], ap=resetm[:, 0, blk * 64:blk * 64 + 1], constant=0.0)
        for blk in range(4):
            G("memset", [Bresetm], [Bresetm], ap=resetm[:, 1, blk * 32:blk * 32 + 1], constant=0.0)
        for blk in range(8):
            pass
        G("memset", [], [Bmh], ap=mhalf[:], constant=-0.5)
        G("memset", [], [Bptok], ap=ptok[:], constant=0.0)
        for l in range(depth):
            G("memset", [], Bvc[l], ap=vc_l[l][:], constant=1.0)
            G("memset", [], Bkc[l], ap=kc_l[l][:], constant=0.0)
        G("memset", [], Bvcur, ap=vcur[:], constant=1.0)
        G("memset", [], [Brc], ap=rc16[:], constant=1.0)
        for c in range(2):
            for hf in range(2):
                w = (2, 4, 8, 16)[c * 2 + hf]
                for t in range(1, 16):
                    val = 1.0 / min(t + 1, w)
                    if t < w:
                        hi = 16 if t == w - 1 else t + 1
                        G("memset", [Brc], [Brc], ap=rc16[hf * 64:hf * 64 + 64, c, t:hi], constant=val)
        G("memset", [], [Bwp], ap=wpool[:], constant=0.0)
        for l in range(depth):
            for g in range(4):
                c, hf = g // 2, g % 2
                Dm("pool", "cstw", wpool[hf * 64:hf * 64 + 64, l, c, hf * 64:hf * 64 + 64], Wd["poolw"][l, g], [Bwp], [Bwp])
        with nc.allow_non_contiguous_dma(reason="tiny constant loads"):
            for l in range(depth):
                Dm("sp", "cst", pscale[:, l, :], Wd["pscale"][l].rearrange("(c p) -> p c", p=128), [], [Bps], slow=True)
                Dm("sp", "cst", lbtmp[:, l, :], Wd["lbraw"][l].rearrange("(h c) -> c h", c=64), [], [Blb], slow=True)
                Dm("sp", "cst", cbt[:, l, :], Wd["rbc"][l].partition_broadcast(128), [], [Bcb])
                Dm("sp", "cst", normwb[:, l, :], Wd["normw"][l].partition_broadcast(64), [], [Bnw])
        A(lbtmp[:], lbtmp[:], AF.Exp, [Blb], [Blb])
        V("tensor_copy", [Blb], [Blb], out=lbsum[:], in_=lbtmp[:, 0, :])
        for l in range(1, depth):
            V("tensor_tensor", [Blb], [Blb], out=lbsum[:], in0=lbsum[:], in1=lbtmp[:, l, :], op=ALU.add)
        V("reciprocal", [Blb], [Blb], out=lbsum[:], in_=lbsum[:])
        V("memset", [Blb], [Blb], ap=lbt[:, 0, :], constant=0.0)
        for l in range(1, depth):
            V("tensor_tensor", [Blb], [Blb], out=lbtmp[:, l, :], in0=lbtmp[:, l, :], in1=lbsum[:], op=ALU.mult)
            V("tensor_tensor", [Blb], [Blb], out=lbt[:, l, :], in0=lbt[:, l - 1, :], in1=lbtmp[:, l, :], op=ALU.add)
        V("tensor_scalar", [Blb], [Blb], out=omlt[:], in0=lbt[:], scalar1=-1.0, scalar2=1.0, op0=ALU.mult, op1=ALU.add)

        def load_gb(l, i):
            s = rr("gb", 2)
            Dm("sp", f"gb{s}", gb[s][:, 0, :], Wd["lng"][l, i].partition_broadcast(128), [], [Bgb[s]])
            Dm("sp", f"gb{s}", gb[s][:, 1, :], Wd["lnb"][l, i].partition_broadcast(128), [], [Bgb[s]])
            return s

        def transposes(tt, ntok):
            S.tag = "tr"
            for half in range(2):
                b = 6 + half
                for k4 in range(4):
                    kc = half * 4 + k4
                    tr(pb[b][:, k4 * 128:k4 * 128 + ntok], x[0:ntok, tt, kc * 128:(kc + 1) * 128], ident[0:ntok, 0:ntok], [Bx[tt], Bid], [Bpb[b]])
                src = pb[b][:].rearrange("p (k n) -> p k n", k=4)[:, :, 0:ntok]
                dst = xT[:, half * 4:half * 4 + 4, tt * 128:tt * 128 + ntok]
                A(dst, src, AF.Copy, [Bpb[b]], [BxT[tt]])

        def post(l, i, tt, ntok, gslot):
            post_a(l, i, tt, ntok, gslot)
            post_b(l, i, tt, ntok, gslot)

        def post_a(l, i, tt, ntok, gslot):
            s = tt
            S.tag = "ln"
            xt = x[0:ntok, tt, :]
            st_ = stt[s]
            V("bn_stats", [Bx[tt]], [Bstt[s]], out=st_[0:ntok, 0:6], in_=x[0:ntok, tt, 0:512])
            V("bn_stats", [Bx[tt]], [Bstt[s]], out=st_[0:ntok, 6:12], in_=x[0:ntok, tt, 512:1024])
            V("bn_aggr", [Bstt[s]], [Bstt[s]], out=st_[0:ntok, 12:14], in_=st_[0:ntok, 0:12])
            V("tensor_scalar", [Bstt[s]], [Bstt[s]], out=st_[0:ntok, 14:15], in0=st_[0:ntok, 13:14], scalar1=LN_EPS, scalar2=None, op0=ALU.add)
            A(st_[0:ntok, 17:18], st_[0:ntok, 14:15], AF.Sqrt, [Bstt[s]], [Bstt[s]])

        def post_b(l, i, tt, ntok, gslot):
            s = tt
            S.tag = "ln"
            xt = x[0:ntok, tt, :]
            st_ = stt[s]
            V("reciprocal", [Bstt[s]], [Bstt[s]], out=st_[0:ntok, 15:16], in_=st_[0:ntok, 17:18])
            V("scalar_tensor_tensor", [Bstt[s]], [Bstt[s]], out=st_[0:ntok, 16:17], in0=st_[0:ntok, 12:13], scalar=-1.0, in1=st_[0:ntok, 15:16], op0=ALU.mult, op1=ALU.mult)
            A(xt, xt, AF.Identity, [Bx[tt], Bstt[s]], [Bx[tt]], bias=st_[0:ntok, 16:17], scale=st_[0:ntok, 15:16])
            V("tensor_tensor", [Bx[tt], Bgb[gslot]], [Bx[tt]], out=xt, in0=xt, in1=gb[gslot][0:ntok, 0, :], op=ALU.mult)
            V("tensor_tensor", [Bx[tt], Bgb[gslot]], [Bx[tt]], out=xt, in0=xt, in1=gb[gslot][0:ntok, 1, :], op=ALU.add)

        class LNPipe:
            def __init__(self, l, i, gslot):
                self.l, self.i, self.g = l, i, gslot
                self.prev = None
                self.dl = Delayed()
            def before_acc(self):
                self.dl.flush()
            def step(self, tt, ntok):
                post_a(self.l, self.i, tt, ntok, self.g)
                if self.prev is not None:
                    post_b(self.l, self.i, *self.prev, self.g)
                    self.dl.pending = self.prev
                self.prev = (tt, ntok)
            def finish(self):
                if self.prev is not None:
                    post_b(self.l, self.i, *self.prev, self.g)
                    self.dl.pending = self.prev
                    self.prev = None
                self.dl.flush(final=True)

        class Delayed:
            DEPTH = 2
            def __init__(self):
                self.q = []
            def flush(self, final=False):
                while self.q and (final or len(self.q) >= self.DEPTH):
                    transposes(*self.q.pop(0))
            @property
            def pending(self):
                return None
            @pending.setter
            def pending(self, v):
                self.q.append(v)

        def ffn(l, f, lni, T, tiles):
            gslot = load_gb(l, lni)
            for g in (0, 1):
                blks = (0, 1, 2) if g == 0 else (3, 4, 5)
                jj = 0
                for b in blks:
                    ncols = 512 if b < 5 else 256
                    S.tag = "ffn.up"
                    wg, Bg = wnext("k", f + "gu", l, b * 512, ncols)
                    wu, Bu = wnext("k", f + "gu", l, HID + b * 512, ncols)
                    for cc in range(ncols // 128):
                        pg = rr("pg", 2); pu = 2 + rr("pu", 2)
                        for kc in range(8):
                            mm(pb[pg][:, 0:T], wg[:, kc, cc * 128:(cc + 1) * 128], xT[:, kc, 0:T], kc == 0, kc == 7, [Bg] + BxT, [Bpb[pg]])
                        for kc in range(8):
                            mm(pb[pu][:, 0:T], wu[:, kc, cc * 128:(cc + 1) * 128], xT[:, kc, 0:T], kc == 0, kc == 7, [Bu] + BxT, [Bpb[pu]])
                        sgi = rr("sg", 2)
                        A(sgb[sgi][:, 0:T], pb[pg][:, 0:T], AF.Silu, [Bpb[pg]], [Bsg[sgi]])
                        V("scalar_tensor_tensor", [Bpb[pu], Bsg[sgi]], [BhT[jj]], out=hT[:, jj, 0:T], in0=pb[pu][:, 0:T], scalar=0.5, in1=sgb[sgi][:, 0:T], op0=ALU.mult, op1=ALU.mult)
                        jj += 1
                nj = jj
                wds = []
                S.tag = "ffn.down"
                for b in blks:
                    wds.append(wnext("j", f + "d", l, b * 512, 4 if b < 5 else 2))
                lp = LNPipe(l, lni, gslot)
                for (tt, ntok) in tiles:
                    S.tag = "ffn.down"
                    pys = []
                    for nh in range(2):
                        py = 4 + rr("py", 2)
                        pys.append(py)
                        for j in range(nj):
                            wd, Bw = wds[j // 4]
                            mm(pb[py][0:ntok, :], hT[:, j, tt * 128:tt * 128 + ntok], wd[:, j % 4, nh * 512:(nh + 1) * 512], j == 0, j == nj - 1, [BhT[j], Bw], [Bpb[py]])
                    lp.before_acc()
                    for nh in range(2):
                        py = pys[nh]
                        xs_ = x[0:ntok, tt, nh * 512:(nh + 1) * 512]
                        if g == 0:
                            V("scalar_tensor_tensor", [Bx[tt], Bpb[py]], [Bx[tt]], out=xs_, in0=xs_, scalar=ALPHA, in1=pb[py][0:ntok, :], op0=ALU.mult, op1=ALU.add)
                        else:
                            V("tensor_tensor", [Bx[tt], Bpb[py]], [Bx[tt]], out=xs_, in0=xs_, in1=pb[py][0:ntok, :], op=ALU.add)
                    if g == 1:
                        lp.step(tt, ntok)
                lp.finish()

        def attention(l, q0, nq, ktiles, mix_c0, mask_first, mask_last):
            nt = len(ktiles)
            S.tag = "attn"
            if nq == 128 and len(ktiles) >= 2 and all(kt[5] == 128 for kt in ktiles) and ktiles[-2][0] == "3" and ktiles[-1][0] == "4":
                return attention_fast(l, q0, ktiles, mix_c0, mask_first)
            attention_drain()
            offs = [i * 128 for i in range(nt)]
            for h in range(8):
                c, hb = h // 2, (h % 2) * 64
                sbank = 2 * rr("sbank", 2)
                pt = rr("PT", 3)
                for i, (kind, kf, Bk, vap, Bv, nk) in enumerate(ktiles):
                    bank = sbank + (offs[i] // 512)
                    col = offs[i] % 512
                    kap = kf(c)
                    mm(pb[bank][0:nk, col:col + nq], kap[hb:hb + 64, 0:nk], qT[hb:hb + 64, c, q0:q0 + nq], True, True, [Bk, BqT], [Bpb[bank]])
                for i, (kind, kf, Bk, vap, Bv, nk) in enumerate(ktiles):
                    bank = sbank + (offs[i] // 512)
                    col = offs[i] % 512
                    o_ = offs[i]
                    if kind == "c":
                        A(PT[pt][0:nk, o_:o_ + nq], pb[bank][0:nk, col:col + nq], AF.Exp, [Bpb[bank], Bcb], [BPT[pt]], bias=cbt[0:nk, l, h:h + 1], scale=0.125)
                        if i == 0 and mask_first:
                            V("memset", [BPT[pt]], [BPT[pt]], ap=PT[pt][0:64, o_ + 64:o_ + 128], constant=0.0)
                    else:
                        et = rr("etmp", 2)
                        eo = 0 if kind == "3" else 128
                        A(etmp[et][0:nk, 0:nq], pb[bank][0:nk, col:col + nq], AF.Exp, [Bpb[bank]], [Bet[et]], scale=0.125)
                        V("tensor_tensor", [Bet[et], Beb], [BPT[pt]], out=PT[pt][0:nk, o_:o_ + nq], in0=etmp[et][0:nk, 0:nq], in1=eb34[0:nk, h, eo:eo + nq], op=ALU.mult)
                        if kind == "4" and mask_last:
                            V("memset", [BPT[pt]], [BPT[pt]], ap=PT[pt][64:128, o_:o_ + 64], constant=0.0)
                ob = 4 + h // 4
                for i, (kind, kf, Bk, vap, Bv, nk) in enumerate(ktiles):
                    o_ = offs[i]
                    mm(pb[ob][0:nq, (h % 4) * 65:(h % 4) * 65 + 65], PT[pt][0:nk, o_:o_ + nq], vap[0:nk, h, :], i == 0, i == nt - 1, [BPT[pt], Bv], [Bpb[ob]])
            for hh in range(2):
                ob = 4 + hh
                pv3 = pb[ob][0:nq, 0:260].rearrange("p (h d) -> p h d", h=4)
                V("reciprocal", [Bpb[ob]], [Brd], out=rden[0:nq, hh * 4:hh * 4 + 4], in_=pv3[:, :, 64])
                V("tensor_tensor", [Bpb[ob], Brd], [Boa], out=oa[0:nq, hh * 256:(hh + 1) * 256].rearrange("p (h d) -> p h d", h=4), in0=pv3[:, :, 0:64],
                  in1=rden[0:nq, hh * 4:hh * 4 + 4].unsqueeze(2).to_broadcast([nq, 4, 64]), op=ALU.mult)
            for c in range(4):
                tr(pbb[6][:, c * 128:c * 128 + nq], oa[0:nq, c * 128:(c + 1) * 128], identb[0:nq, 0:nq], [Boa, Bid], [Bpb[6]])
            A(hT[:, 0:4, mix_c0:mix_c0 + nq], pbb[6][:, 0:512].rearrange("p (c n) -> p c n", c=4)[:, :, 0:nq], AF.Copy, [Bpb[6]], BhT[0:4])

        attn_tail = {"f": None}

        def attention_drain():
            if attn_tail["f"] is not None:
                attn_tail["f"]()
                attn_tail["f"] = None

        def attention_fast(l, q0, ktiles, mix_c0, mask_first):
            nq = 128
            ncst = len(ktiles) - 2
            cts, t3, t4 = ktiles[:ncst], ktiles[-2], ktiles[-1]
            order = list(cts) + [t3, t4]
            pcol = [i * 128 for i in range(ncst)] + [384, 512]
            st = {}

            def issue_S(h):
                c, hb = h // 2, (h % 2) * 64
                sbank = 2 * rr("sbank", 2)
                for i, (kind, kf, Bk, vap, Bv, nk) in enumerate(cts):
                    mm(pb[sbank][:, i * 128:(i + 1) * 128], kf(c)[hb:hb + 64, :], qT[hb:hb + 64, c, q0:q0 + nq], True, True, [Bk, BqT], [Bpb[sbank]])
                for i, (kind, kf, Bk, vap, Bv, nk) in enumerate((t3, t4)):
                    mm(pb[sbank + 1][:, i * 128:(i + 1) * 128], kf(c)[hb:hb + 64, :], qT[hb:hb + 64, c, q0:q0 + nq], True, True, [Bk, BqT], [Bpb[sbank + 1]])
                st[h] = sbank

            def issue_E(h):
                sbank = st[h]
                pt = rr("PT", 3)
                if ncst:
                    A(PT[pt][:, 0:ncst * 128], pb[sbank][:, 0:ncst * 128], AF.Exp, [Bpb[sbank], Bcb], [BPT[pt]], bias=cbt[:, l, h:h + 1], scale=0.125)
                    if mask_first:
                        V("memset", [BPT[pt]], [BPT[pt]], ap=PT[pt][0:64, 64:128], constant=0.0)
                et = rr("etmp", 2)
                A(etmp[et][:, 0:256], pb[sbank + 1][:, 0:256], AF.Exp, [Bpb[sbank + 1]], [Bet[et]], scale=0.125)
                V("tensor_tensor", [Bet[et], Beb], [BPT[pt]], out=PT[pt][:, 384:640], in0=etmp[et][:, 0:256], in1=eb34[:, h, 0:256], op=ALU.mult)
                st[h] = pt

            def issue_PV(h):
                pt = st[h]
                ob = 4 + h // 4
                for i, (kind, kf, Bk, vap, Bv, nk) in enumerate(order):
                    mm(pb[ob][:, (h % 4) * 65:(h % 4) * 65 + 65], PT[pt][:, pcol[i]:pcol[i] + nq], vap[:, h, :], i == 0, i == len(order) - 1, [BPT[pt], Bv], [Bpb[ob]])

            issue_S(0)
            for h in range(8):
                if h + 1 < 8:
                    issue_S(h + 1)
                if h == 0:
                    attention_drain()
                issue_E(h)
                issue_PV(h)
            for hh in range(2):
                ob = 4 + hh
                pv3 = pb[ob][:, 0:260].rearrange("p (h d) -> p h d", h=4)
                V("reciprocal", [Bpb[ob]], [Brd], out=rden[:, hh * 4:hh * 4 + 4], in_=pv3[:, :, 64])
                V("tensor_tensor", [Bpb[ob], Brd], [Boa], out=oa[:, hh * 256:(hh + 1) * 256].rearrange("p (h d) -> p h d", h=4), in0=pv3[:, :, 0:64],
                  in1=rden[:, hh * 4:hh * 4 + 4].unsqueeze(2).to_broadcast([128, 4, 64]), op=ALU.mult)
            def tail(mix_c0=mix_c0):
                S.tag = "attn"
                for c in range(4):
                    tr(pbb[6][:, c * 128:(c + 1) * 128], oa[:, c * 128:(c + 1) * 128], identb[:], [Boa, Bid], [Bpb[6]])
                A(hT[:, 0:4, mix_c0:mix_c0 + nq], pbb[6][:, 0:512].rearrange("p (c n) -> p c n", c=4), AF.Copy, [Bpb[6]], BhT[0:4])
            attn_tail["f"] = tail

        def pool_group(l, t0, GT, first_of_seq, hist_dram, out_dram, mixc0, w3):
            S.tag = "pool"
            if hist_dram is not None:
                Dm("sp", "ptok", ptok[0:15, :], hist_dram, [], [Bptok])
                for c in range(2):
                    tr(pb[7][:, c * 16:c * 16 + 16], ptok[0:16, c * 128:(c + 1) * 128], ident[0:16, 0:16], [Bptok, Bid], [Bpb[7]])
                V("tensor_copy", [Bpb[7]], [BuT], out=uT[:, :, 1:16], in_=pb[7][:, 0:32].rearrange("p (c n) -> p c n", c=2)[:, :, 0:15])
            else:
                V("tensor_copy", [Bucar[l]], [BuT], out=uT[:, :, 0:16], in_=ucar[:, l, :, :])
            wv, Bw = w3
            for c in range(2):
                bank = rr("pg", 2)
                for kc in range(8):
                    mm(pb[bank][:, 0:GT], wv[:, kc, c * 128:(c + 1) * 128], xT[:, kc, t0:t0 + GT], kc == 0, kc == 7, [Bw] + BxT, [Bpb[bank]])
                A(uT[:, c, 16:16 + GT], pb[bank][:, 0:GT], AF.Copy, [Bpb[bank]], [BuT])
            n = 16 + GT
            V("tensor_tensor", [BuT], [BsA], out=sA[:, :, 2:n], in0=uT[:, :, 2:n], in1=uT[:, :, 1:n - 1], op=ALU.add)
            V("tensor_tensor", [BsA], [BsB], out=sBb[:, :, 4:n], in0=sA[:, :, 4:n], in1=sA[:, :, 2:n - 2], op=ALU.add)

            def dcalc(src, Bsrc, c, hf, w):
                lo, hi = hf * 64, hf * 64 + 64
                V("scalar_tensor_tensor", [Bsrc, BuT], [BdT], out=dT[lo:hi, c, 0:GT], in0=src[lo:hi, c, 16:n], scalar=1.0 / w, in1=uT[lo:hi, c, 16:n], op0=ALU.mult, op1=ALU.subtract)
                if first_of_seq:
                    V("tensor_tensor", [Bsrc, Brc], [Bptmp], out=ptmp[lo:hi, 0:16], in0=src[lo:hi, c, 16:32], in1=rc16[lo:hi, c, :], op=ALU.mult)
                    V("tensor_tensor", [Bptmp, BuT], [BdT], out=dT[lo:hi, c, 0:16], in0=ptmp[lo:hi, 0:16], in1=uT[lo:hi, c, 16:32], op=ALU.subtract)

            dcalc(sA, BsA, 0, 0, 2)
            dcalc(sBb, BsB, 0, 1, 4)
            V("tensor_tensor", [BsB], [BsA], out=sA[:, :, 8:n], in0=sBb[:, :, 8:n], in1=sBb[:, :, 4:n - 4], op=ALU.add)
            dcalc(sA, BsA, 1, 0, 8)
            V("tensor_tensor", [BsA], [BsB], out=sBb[:, :, 16:n], in0=sA[:, :, 16:n], in1=sA[:, :, 8:n - 8], op=ALU.add)
            dcalc(sBb, BsB, 1, 1, 16)
            return lambda: pool_part2(l, GT, hist_dram, out_dram, mixc0, n)

        def pool_part2(l, GT, hist_dram, out_dram, mixc0, n):
            S.tag = "pool"
            for c in range(2):
                bank = rr("pg", 2)
                mm(pb[bank][:, 0:GT], wpool[:, l, c, :], dT[:, c, 0:GT], True, True, [Bwp, BdT], [Bpb[bank]])
                A(hT[:, 4 + c, mixc0:mixc0 + GT], pb[bank][:, 0:GT], AF.Copy, [Bpb[bank], Bps], [BhT[4 + c]], scale=pscale[:, l, c:c + 1])
            if out_dram is not None:
                for c in range(2):
                    tr(pb[7][0:16, 128 + c * 128:256 + c * 128], uT[:, c, n - 16:n], ident[:], [BuT, Bid], [Bpb[7]])
                V("tensor_copy", [Bpb[7]], [Bpst], out=pst[0:16, :], in_=pb[7][0:16, 128:384])
                Dm("sp", "pst", out_dram, pst[1:16, :], [Bpst], [outbuf("o")])
            if hist_dram is None:
                V("tensor_copy", [BuT], [Bucar[l]], out=ucar[:, l, :, :], in_=uT[:, :, GT:GT + 16])

        hg = {"pending": None, "gcount": 0}

        def hgrn_drain():
            if hg["pending"] is not None:
                hg["pending"]()
                hg["pending"] = None

        def hgrn_group(l, t0, GT, L, w3, w4, w5, state_in, state_out, mixc0, chain, after_proj=None):
            nblk = GT // L
            gbase = 0 if nblk == 4 else 2 * (hg["gcount"] % 2)
            hg["gcount"] += 1
            S.tag = "hgrn.proj"
            (wv3, Bw3), (wv4, Bw4), (wv5, Bw5) = w3, w4, w5
            rmask = resetm[:, 0, 0:GT] if L == 64 else resetm[:, 1, 0:GT]
            for h in range(4):
                bank = rr("pg", 2)
                for kc in range(8):
                    mm(pb[bank][0:64, 0:GT], wv3[:, kc, 256 + h * 64:256 + h * 64 + 64], xT[:, kc, t0:t0 + GT], kc == 0, kc == 7, [Bw3] + BxT, [Bpb[bank]])
                A(QS[:, h, 0:GT], pb[bank][0:64, 0:GT], AF.Silu, [Bpb[bank]], [BQS])
            for blk in range(nblk):
                tb = t0 + blk * L
                bank = 2 + rr("pu", 2)
                for kc in range(8):
                    mm(pb[bank][0:L, 0:256], xT[:, kc, tb:tb + L], wv4[:, kc, 256:512], kc == 0, kc == 7, [Bw4] + BxT, [Bpb[bank]])
                for kc in range(8):
                    mm(pb[bank][0:L, 256:512], xT[:, kc, tb:tb + L], wv5[:, kc, 0:256], kc == 0, kc == 7, [Bw5] + BxT, [Bpb[bank]])
                V("tensor_copy", [Bpb[bank]], [BVT[gbase + blk]], out=VTOK[gbase + blk][0:L, :], in_=pb[bank][0:L, 0:256])
                A(G2[gbase + blk][0:L, :], pb[bank][0:L, 256:512], AF.Silu, [Bpb[bank]], [BG2[gbase + blk]])
                V("tensor_tensor", [BG2[gbase + blk], Bnw], [BG2[gbase + blk]], out=G2[gbase + blk][0:L, :].rearrange("p (h d) -> p h d", h=4), in0=G2[gbase + blk][0:L, :].rearrange("p (h d) -> p h d", h=4), in1=normwb[0:L, l, :].unsqueeze(1).to_broadcast([L, 4, 64]), op=ALU.mult)
            for h in range(4):
                bank = rr("pg", 2)
                for kc in range(8):
                    mm(pb[bank][0:64, 0:GT], wv4[:, kc, h * 64:h * 64 + 64], xT[:, kc, t0:t0 + GT], kc == 0, kc == 7, [Bw4] + BxT, [Bpb[bank]])
                A(Fb[:, h, 0:GT], pb[bank][0:64, 0:GT], AF.Sigmoid, [Bpb[bank]], [BF_])
            if after_proj is not None:
                after_proj()
            S.tag = "hgrn.gate"
            V("tensor_tensor", [BF_, Blb], [BF_], out=Fb[:, :, 0:GT], in0=Fb[:, :, 0:GT], in1=omlt[:, l, :].unsqueeze(2).to_broadcast([64, 4, GT]), op=ALU.mult)
            V("tensor_tensor", [BF_, Blb], [BF_], out=Fb[:, :, 0:GT], in0=Fb[:, :, 0:GT], in1=lbt[:, l, :].unsqueeze(2).to_broadcast([64, 4, GT]), op=ALU.add)
            V("tensor_scalar", [BF_], [BKIN], out=KIN[:, :, 0:GT], in0=Fb[:, :, 0:GT], scalar1=-1.0, scalar2=1.0, op0=ALU.mult, op1=ALU.add)
            A(Fb[:, :, 0:GT], Fb[:, :, 0:GT], AF.Ln, [BF_], [BF_])
            for h in range(4):
                V("tensor_tensor_scan", [BF_, Bresetm], [BBc], out=Bc[:, h, 0:GT], data0=rmask, data1=Fb[:, h, 0:GT], initial=0.0, op0=ALU.mult, op1=ALU.add)
            mid = L // 2 - 1
            Bc4 = Bc[:, :, 0:GT].rearrange("p h (b t) -> p h b t", t=L)
            V("tensor_tensor", [BBc, BF_], [BF_], out=Fb[:, :, 0:GT].rearrange("p h (b t) -> p (h b) t", t=L), in0=Bc[:, :, 0:GT].rearrange("p h (b t) -> p (h b) t", t=L),
              in1=Bc[:, :, 0:GT].rearrange("p h (b t) -> p (h b) t", t=L)[:, :, mid:mid + 1].to_broadcast([64, 4 * nblk, L]), op=ALU.subtract)
            A(EQ[:, :, 0:GT], Fb[:, :, 0:GT], AF.Exp, [BF_], [BEQ])
            A(EK[:, :, 0:GT], Fb[:, :, 0:GT], AF.Exp, [BF_], [BEK], scale=-1.0)
            A(EBM[:, :, 0:nblk], Bc4[:, :, :, mid], AF.Exp, [BBc], [BEBM])
            A(EBL[:, :, 0:nblk], Bc4[:, :, :, L - 1], AF.Exp, [BBc], [BEBL])
            V("tensor_tensor", [BQS, BEQ], [BQT], out=QTt[:, :, 0:GT], in0=QS[:, :, 0:GT], in1=EQ[:, :, 0:GT], op=ALU.mult)
            V("tensor_tensor", [BKIN, BEK], [BKT], out=KTt[:, :, 0:GT], in0=KIN[:, :, 0:GT], in1=EK[:, :, 0:GT], op=ALU.mult)
            EQ4 = EQ[:, :, 0:GT].rearrange("p h (b t) -> p h b t", t=L)
            S.tag = "hgrn.blk"
            for blk in range(nblk):
                c0 = blk * L
                if not chain:
                    Dm("sp", "hst", Sst[:, l, :, :], state_in[blk].rearrange("h c v -> c h v"), [], [BSst[l]])
                for h in range(4):
                    tr(pbb[7][0:L, h * 64:(h + 1) * 64], KTt[:, h, c0:c0 + L], identb[0:64, 0:64], [BKT, Bid], [Bpb[7]])
                A(KTOK[0:L, :, :], pbb[7][0:L, 0:256].rearrange("p (h c) -> p h c", h=4), AF.Copy, [Bpb[7]], [BKTOK])
                for h in range(4):
                    mm(pb[6][0:L, h * 64:h * 64 + L], KTt[:, h, c0:c0 + L], QTt[:, h, c0:c0 + L], True, True, [BKT, BQT], [Bpb[6]])
                V("tensor_tensor", [Bpb[6], Btri], [BSM], out=SM[0:L, :, 0:L], in0=pb[6][0:L, 0:256].rearrange("p (h t) -> p h t", h=4)[:, :, 0:L],
                  in1=tri[0:L, 0:L].unsqueeze(1).to_broadcast([L, 4, L]), op=ALU.mult)
                V("tensor_tensor", [BSst[l], BEBM], [BSbf], out=Sbf[:], in0=Sst[:, l, :, :], in1=EBM[:, :, blk:blk + 1].to_broadcast([64, 4, 64]), op=ALU.mult)
                ob = 4 + rr("hgo", 2)
                for h in range(4):
                    mm(pb[ob][0:L, h * 64:(h + 1) * 64], SM[0:L, h, 0:L], VTOK[gbase + blk][0:L, h * 64:(h + 1) * 64], True, False, [BSM, BVT[gbase + blk]], [Bpb[ob]])
                    mm(pb[ob][0:L, h * 64:(h + 1) * 64], QTt[:, h, c0:c0 + L], Sbf[:, h, :], False, True, [BQT, BSbf], [Bpb[ob]])
                mb = 6
                for h in range(4):
                    mm(pb[mb][0:64, 256 + h * 64:256 + (h + 1) * 64], KTOK[0:L, h, :], VTOK[gbase + blk][0:L, h * 64:(h + 1) * 64], True, True, [BKTOK, BVT[gbase + blk]], [Bpb[mb]])
                V("tensor_tensor", [Bpb[mb], BEQ], [BT1], out=T1[:], in0=pb[mb][0:64, 256:512].rearrange("p (h v) -> p h v", h=4),
                  in1=EQ4[:, :, blk, L - 1:L].to_broadcast([64, 4, 64]), op=ALU.mult)
                V("tensor_tensor", [BSst[l], BEBL], [BSst[l]], out=Sst[:, l, :, :], in0=Sst[:, l, :, :], in1=EBL[:, :, blk:blk + 1].to_broadcast([64, 4, 64]), op=ALU.mult)
                V("tensor_tensor", [BSst[l], BT1], [BSst[l]], out=Sst[:, l, :, :], in0=Sst[:, l, :, :], in1=T1[:], op=ALU.add)
                if state_out is not None and (not chain or blk == nblk - 1):
                    so = state_out[blk] if not chain else state_out
                    V("tensor_copy", [BSst[l]], [Bhst], out=hst[:].rearrange("p (h v) -> p h v", h=4), in_=Sst[:, l, :, :])
                    Dm("sp", "hso", so.rearrange("h c v -> c h v"), hst[:].rearrange("p (h v) -> p h v", h=4), [Bhst], [outbuf("o")])
                def post_blk(ob=ob, gi=gbase + blk, c0=c0):
                    S.tag = "hgrn.out"
                    A(osq[0:L, :], pb[ob][0:L, 0:256], AF.Square, [Bpb[ob]], [Bosq])
                    V("tensor_reduce", [Bosq], [Boss], out=oss[0:L, 0:4], in_=osq[0:L, :].rearrange("p (h v) -> p h v", h=4), axis=AX.X, op=ALU.add)
                    V("tensor_scalar", [Boss], [Boss], out=oss[0:L, 0:4], in0=oss[0:L, 0:4], scalar1=1.0 / 64, scalar2=RMS_EPS, op0=ALU.mult, op1=ALU.add)
                    A(oss[0:L, 0:4], oss[0:L, 0:4], AF.Ln, [Boss], [Boss])
                    A(oss[0:L, 4:8], oss[0:L, 0:4], AF.Exp, [Boss], [Boss], scale=-0.5)
                    V("tensor_tensor", [Bpb[ob], Boss, Bosq], [Bosq], out=osq[0:L, :].rearrange("p (h v) -> p h v", h=4), in0=pb[ob][0:L, 0:256].rearrange("p (h v) -> p h v", h=4),
                      in1=oss[0:L, 4:8].unsqueeze(2).to_broadcast([L, 4, 64]), op=ALU.mult)
                    V("tensor_tensor", [Bosq, BG2[gi]], [Boc], out=octok[0:L, :], in0=osq[0:L, :], in1=G2[gi][0:L, :], op=ALU.mult)
                    for c in range(2):
                        tr(pbb[7][:, 512 + c * 64:512 + c * 64 + L], octok[0:L, c * 128:(c + 1) * 128], identb[0:L, 0:L], [Boc, Bid], [Bpb[7]])
                    A(hT[:, 6:8, mixc0 + c0:mixc0 + c0 + L], pbb[7][:, 512:640].rearrange("p (c n) -> p c n", c=2)[:, :, 0:L], AF.Copy, [Bpb[7]], BhT[6:8])
                if hg["pending"] is not None:
                    hg["pending"]()
                hg["pending"] = post_blk

        def mixer(l, T, tiles, nseq, first_tile, last_tile, is_prompt):
            S.tag = "qkv"
            gslot = load_gb(l, 1)
            Dm("sp", "eb", eb34[:], Wd["rb34"][l].rearrange("h r n -> r h n"), [], [Beb])
            A(eb34[:], eb34[:], AF.Exp, [Beb], [Beb])
            V("memset", [Beb], [Beb], ap=eb34[64:128, :, 128:192], constant=0.0)
            wq, Bq_ = wnext("k", "win", l, 0, 512)
            for c in range(4):
                bank = rr("pg", 2)
                for kc in range(8):
                    mm(pb[bank][:, 0:T], wq[:, kc, c * 128:(c + 1) * 128], xT[:, kc, 0:T], kc == 0, kc == 7, [Bq_] + BxT, [Bpb[bank]])
                A(qT[:, c, 0:T], pb[bank][:, 0:T], AF.Copy, [Bpb[bank]], [BqT])
            wk, Bk_ = wnext("k", "win", l, 512, 512)
            for c in range(4):
                bank = rr("pg", 2)
                for kc in range(8):
                    mm(pb[bank][:, 0:T], wk[:, kc, c * 128:(c + 1) * 128], xT[:, kc, 0:T], kc == 0, kc == 7, [Bk_] + BxT, [Bpb[bank]])
                V("tensor_copy", [Bpb[bank]], [Bkcur], out=kcur[:, c, 0:T], in_=pb[bank][:, 0:T])
            kout = (last_tile or not is_prompt)
            if kout:
                for (tt, ntok) in tiles:
                    bank = 2 + rr("pu", 2)
                    for kc in range(8):
                        mm(pb[bank][0:ntok, :], xT[:, kc, tt * 128:tt * 128 + ntok], wk[:, kc, :], kc == 0, kc == 7, [Bk_] + BxT, [Bpb[bank]])
                    ks = rr("kvst", 2)
                    A(kvst[ks][0:ntok, :], pb[bank][0:ntok, :], AF.Copy, [Bpb[bank]], [Bkvst[ks]])
                    dst = (pk[l][tt * 128:tt * 128 + ntok, :] if is_prompt else sk[l][tt * 128:tt * 128 + ntok, :])
                    Dm("sp", f"kvo{ks}", dst, kvst[ks][0:ntok, :], [Bkvst[ks]], [outbuf("o")])
            wvv, Bv_ = wnext("k", "win", l, 1024, 512)
            vtiles = [(tt, tt * 128, ntok) for (tt, ntok) in tiles] if is_prompt else [(si, si * 32, 32) for si in range(nseq)]
            for (vi, tk0, ntok) in vtiles:
                bank = 2 + rr("pu", 2)
                for kc in range(8):
                    mm(pb[bank][0:ntok, :], xT[:, kc, tk0:tk0 + ntok], wvv[:, kc, :], kc == 0, kc == 7, [Bv_] + BxT, [Bpb[bank]])
                V("tensor_copy", [Bpb[bank]], [Bvcur[vi]], out=vcur[0:ntok, vi, :, 0:64], in_=pb[bank][0:ntok, :].rearrange("p (h d) -> p h d", h=8))
                if kout:
                    ks = rr("kvst", 2)
                    A(kvst[ks][0:ntok, :], pb[bank][0:ntok, :], AF.Copy, [Bpb[bank], Bvcur[vi]], [Bkvst[ks]])
                    dst = (pv[l][tk0:tk0 + ntok, :] if is_prompt else sv[l][tk0:tk0 + ntok, :])
                    Dm("sp", f"kvo{ks}", dst, kvst[ks][0:ntok, :], [Bkvst[ks]], [outbuf("o")])
            if is_prompt:
                for p in range(4):
                    kts = []
                    for i in range(5):
                        g = p - 4 + i
                        kind = "c" if i < 3 else ("3" if i == 3 else "4")
                        if g < 0:
                            if first_tile:
                                continue
                            ci = 4 + g
                            kts.append((kind, (lambda c, ci=ci: kc_l[l][:, c, ci * 128:(ci + 1) * 128]), Bkc[l][ci], vc_l[l][:, ci, :, :], Bvc[l][ci], 128))
                        else:
                            kts.append((kind, (lambda c, g=g: kcur[:, c, g * 128:(g + 1) * 128]), Bkcur, vcur[:, g, :, :], Bvcur[g], 128))
                    attention(l, p * 128, 128, kts, p * 128, mask_first=(len(kts) == 5), mask_last=True)
            else:
                for si in range(nseq):
                    for ci in range(4):
                        Dm("pool", "ckst", ckst[:], ck[l, si, ci * 128:(ci + 1) * 128, :], [], [Bckst])
                        for c in range(4):
                            tr(pbb[6][:, c * 128:(c + 1) * 128], ckst[:, c * 128:(c + 1) * 128], identb[:], [Bckst, Bid], [Bpb[6]])
                        A(kc_l[l][:, :, ci * 128:(ci + 1) * 128], pbb[6][:, 0:512].rearrange("p (c n) -> p c n", c=4), AF.Copy, [Bpb[6]], [Bkc[l][ci]])
                        Dm("pool", f"vc{ci}", vc_l[l][:, ci, :, 0:64], cv[l, si, ci * 128:(ci + 1) * 128, :].rearrange("p (h d) -> p h d", h=8), [], [Bvc[l][ci]])
                    kts = []
                    for i in range(4):
                        kind = "c" if i < 3 else "3"
                        kts.append((kind, (lambda c, i=i: kc_l[l][:, c, i * 128:(i + 1) * 128]), Bkc[l][i], vc_l[l][:, i, :, :], Bvc[l][i], 128))
                    kts.append(("4", (lambda c, si=si: kcur[:, c, si * 32:si * 32 + 32]), Bkcur, vcur[0:32, si, :, :], Bvcur[si], 32))
                    attention(l, si * 32, 32, kts, si * 32, mask_first=False, mask_last=False)
            attention_drain()
            S.tag = "carry"
            if is_prompt and not last_tile:
                A(kc_l[l][:], kcur[:], AF.Copy, [Bkcur], Bkc[l])
                V("tensor_copy", Bvcur, Bvc[l], out=vc_l[l][:, :, :, 0:64], in_=vcur[:, :, :, 0:64])
            w3 = wnext("k", "win", l, 1536, 512)
            w4 = wnext("k", "win", l, 2048, 512)
            w5 = wnext("k", "win", l, 2560, 256)
            if is_prompt:
                if first_tile:
                    V("memset", [], [Bucar[l]], ap=ucar[:, l, :, :], constant=0.0)
                    V("memset", [], [BSst[l]], ap=Sst[:, l, :, :], constant=0.0)
                for g in range(4):
                    lastg = last_tile and g == 3
                    p2 = pool_group(l, g * 128, 128, first_tile and g == 0, None, ppool[l] if lastg else None, g * 128, w3)
                    hgrn_group(l, g * 128, 128, 64, w3, w4, w5, None, phg[l] if lastg else None, g * 128, True, after_proj=p2)
            else:
                for si in range(nseq):
                    pool_group(l, si * 32, 32, False, spool[l, si], spool_o[l, si], si * 32, w3)()
                hgrn_group(l, 0, 32 * nseq, 32, w3, w4, w5, [shg[l, si] for si in range(nseq)], [shg_o[l, si] for si in range(nseq)], 0, False)
            hgrn_drain()
            S.tag = "wout"
            wo = [wnext("k", "wout", l, 0, 512), wnext("k", "wout", l, 512, 512)]
            lp = LNPipe(l, 1, gslot)
            for (tt, ntok) in tiles:
                S.tag = "wout"
                pys = []
                for nh in range(2):
                    py = 4 + rr("py", 2)
                    pys.append(py)
                    wv_, Bw_ = wo[nh]
                    for kc in range(8):
                        mm(pb[py][0:ntok, :], hT[:, kc, tt * 128:tt * 128 + ntok], wv_[:, kc, :], kc == 0, kc == 7, [BhT[kc], Bw_], [Bpb[py]])
                lp.before_acc()
                for nh in range(2):
                    xs_ = x[0:ntok, tt, nh * 512:(nh + 1) * 512]
                    V("scalar_tensor_tensor", [Bx[tt], Bpb[pys[nh]]], [Bx[tt]], out=xs_, in0=xs_, scalar=ALPHA, in1=pb[pys[nh]][0:ntok, :], op0=ALU.mult, op1=ALU.add)
                lp.step(tt, ntok)
            lp.finish()

        def ple(l, T, tiles, prow0, is_prompt):
            S.tag = "ple"
            gslot = load_gb(l, 3)
            psrc = pp if is_prompt else ps_
            for (tt, ntok) in tiles:
                Dm("sp", "ptokp", ptokp[0:ntok, 0, :], psrc[l, prow0 + tt * 128:prow0 + tt * 128 + ntok, :], [], [Bptokp])
                for c in range(2):
                    tr(pb[7][:, c * 128:c * 128 + ntok], ptokp[0:ntok, 0, c * 128:(c + 1) * 128], ident[0:ntok, 0:ntok], [Bptokp, Bid], [Bpb[7]])
                V("tensor_copy", [Bpb[7]], [BpT], out=pTt[:, :, tt * 128:tt * 128 + ntok], in_=pb[7][:, 0:256].rearrange("p (c n) -> p c n", c=2)[:, :, 0:ntok])
            wg = [wnext("k", "pleg", l, 0, 512), wnext("k", "pleg", l, 512, 512)]
            wp_, Bwp_ = wnext("j", "plep", l, 0, 2)
            lp = LNPipe(l, 3, gslot)
            for (tt, ntok) in tiles:
                S.tag = "ple"
                banks = []
                for nh in range(2):
                    pg = rr("pg", 2); pu = 2 + rr("pu", 2)
                    banks.append((pg, pu))
                    wv_, Bw_ = wg[nh]
                    for kc in range(8):
                        mm(pb[pg][0:ntok, :], xT[:, kc, tt * 128:tt * 128 + ntok], wv_[:, kc, :], kc == 0, kc == 7, [BxT[tt], Bw_], [Bpb[pg]])
                    for c in range(2):
                        mm(pb[pu][0:ntok, :], pTt[:, c, tt * 128:tt * 128 + ntok], wp_[:, c, nh * 512:(nh + 1) * 512], c == 0, c == 1, [BpT, Bwp_], [Bpb[pu]])
                lp.before_acc()
                for nh in range(2):
                    pg, pu = banks[nh]
                    ei = rr("etile", 2)
                    A(etile[ei][0:ntok, :], pb[pg][0:ntok, :], AF.Sigmoid, [Bpb[pg]], [Bet2[ei]])
                    V("tensor_tensor", [Bet2[ei], Bpb[pu]], [Bet2[ei]], out=etile[ei][0:ntok, :], in0=etile[ei][0:ntok, :], in1=pb[pu][0:ntok, :], op=ALU.mult)
                    xs_ = x[0:ntok, tt, nh * 512:(nh + 1) * 512]
                    V("scalar_tensor_tensor", [Bx[tt], Bet2[ei]], [Bx[tt]], out=xs_, in0=xs_, scalar=ALPHA, in1=etile[ei][0:ntok, :], op0=ALU.mult, op1=ALU.add)
                lp.step(tt, ntok)
            lp.finish()

        def run_tile(xsrc, ydst, row0, T, tiles, is_prompt, first_tile, last_tile, nseq):
            for (tt, ntok) in tiles:
                Dm("sp", f"xin{tt}", x[0:ntok, tt, :], xsrc[row0 + tt * 128:row0 + tt * 128 + ntok, :], [], [Bx[tt]])
                transposes(tt, ntok)
            for l in range(depth):
                ffn(l, "f1", 0, T, tiles)
                mixer(l, T, tiles, nseq, first_tile, last_tile, is_prompt)
                ffn(l, "f2", 2, T, tiles)
                ple(l, T, tiles, row0, is_prompt)
            for (tt, ntok) in tiles:
                Dm("sp", f"xout{tt}", ydst[row0 + tt * 128:row0 + tt * 128 + ntok, :], x[0:ntok, tt, :], [Bx[tt]], [outbuf("o")])

        for ti in range(self.n_ptiles):
            run_tile(xp, yp, ti * 512, 512, [(t, 128) for t in range(4)], True, ti == 0, ti == self.n_ptiles - 1, 0)
        if self.n_samp:
            run_tile(xs, ys, 0, NS, [(0, NS)], False, False, False, nsm)
        assert consumed[0] == len(plan), (consumed[0], len(plan))
        self.nsem = S.emit(nc, outbufs)
        self.st.close()
        return nc


def _rel_tables(attn_rel_bias):
    depth = attn_rel_bias.shape[0]
    r = np.arange(128)[:, None]
    j = np.arange(128)[None, :]
    d3 = np.clip(128 - r + j, -63, 128) + 63
    d4 = np.clip(j - r, -63, 128) + 63
    rb34 = np.concatenate([attn_rel_bias[:, :, d3], attn_rel_bias[:, :, d4]], axis=-1)
    rbc = attn_rel_bias[:, :, 191]
    return np.ascontiguousarray(rb34, dtype=np.float32), np.ascontiguousarray(rbc, dtype=np.float32)


_CACHE = {}


def run(inputs, depth, n_ptiles, n_samp_per_core, n_cores, prompt_of_core, samp_of_core):
    key = (depth, n_ptiles, n_samp_per_core)
    if key not in _CACHE:
        _CACHE[key] = Prog(depth, n_ptiles, n_samp_per_core).build()
    nc = _CACHE[key]
    f = lambda a: np.ascontiguousarray(np.asarray(a), dtype=np.float32)
    rb34, rbc = _rel_tables(f(inputs["attn_rel_bias"]))
    common = {
        "f1gu": f(inputs["ffn1_w_gu"]), "f1d": f(inputs["ffn1_w_down"]), "win": f(inputs["w_in"]), "wout": f(inputs["w_out"]),
        "f2gu": f(inputs["ffn2_w_gu"]), "f2d": f(inputs["ffn2_w_down"]), "pleg": f(inputs["ple_w_gate"]), "plep": f(inputs["ple_w_proj"]),
        "lng": f(inputs["ln_g"]), "lnb": f(inputs["ln_b"]), "poolw": f(inputs["pool_w"]), "pscale": f(inputs["pool_scale"]),
        "lbraw": f(inputs["hgrn_lower_bounds"]), "normw": f(inputs["hgrn_norm_w"]), "rb34": rb34, "rbc": rbc,
    }
    xp, pp_ = f(inputs["x_prompt"]), f(inputs["p_prompt"])
    xs, ps_ = f(inputs["x_sample"]), f(inputs["p_sample"])
    ck, cv = f(inputs["cache_attn_k"]), f(inputs["cache_attn_v"])
    sp_, sh = f(inputs["state_pool"]), f(inputs["state_hgrn"])
    in_maps = []
    for c in range(n_cores):
        b = prompt_of_core[c]
        sb = samp_of_core[c]
        m = dict(common)
        m["xp"] = xp[b]; m["pp"] = np.ascontiguousarray(pp_[:, b])
        m["xs"] = np.ascontiguousarray(xs[sb].reshape(-1, D)); m["ps"] = np.ascontiguousarray(ps_[:, sb].reshape(depth, -1, 256))
        m["ck"] = np.ascontiguousarray(ck[:, sb].reshape(depth, len(sb), 512, 512)); m["cv"] = np.ascontiguousarray(cv[:, sb].reshape(depth, len(sb), 512, 512))
        m["spool"] = np.ascontiguousarray(sp_[:, sb]); m["shg"] = np.ascontiguousarray(sh[:, sb])
        in_maps.append(m)
    res = run_bass_kernel_spmd(nc, in_maps, core_ids=list(range(n_cores)))
    return res.results


def kernel(**inputs):
    depth = 4
    n_cores = 8
    prompt_of_core = [c % 4 for c in range(n_cores)]
    samp_of_core = [list(range(4 * c, 4 * c + 4)) for c in range(n_cores)]
    r = run(inputs, depth, 16, 4, n_cores, prompt_of_core, samp_of_core)
    B, SEQ = 4, 8192
    y_prompt = np.stack([r[b]["yp"] for b in range(B)]).astype(np.float32)
    y_sample = np.concatenate([r[c]["ys"].reshape(4, 32, D) for c in range(n_cores)]).astype(np.float32)
    pk = np.stack([r[b]["pk"].reshape(depth, 512, 512) for b in range(B)], axis=1).reshape(depth, B, 512, 8, 64).astype(np.float32)
    pv = np.stack([r[b]["pv"].reshape(depth, 512, 512) for b in range(B)], axis=1).reshape(depth, B, 512, 8, 64).astype(np.float32)
    ppool = np.stack([r[b]["ppool"].reshape(depth, 15, 256) for b in range(B)], axis=1).astype(np.float32)
    phg = np.stack([r[b]["phg"].reshape(depth, 4, 64, 64) for b in range(B)], axis=1).astype(np.float32)
    sk = np.concatenate([r[c]["sk"].reshape(depth, 4, 32, 8, 64) for c in range(n_cores)], axis=1).astype(np.float32)
    sv = np.concatenate([r[c]["sv"].reshape(depth, 4, 32, 8, 64) for c in range(n_cores)], axis=1).astype(np.float32)
    spool = np.concatenate([r[c]["spool_o"].reshape(depth, 4, 15, 256) for c in range(n_cores)], axis=1).astype(np.float32)
    shg = np.concatenate([r[c]["shg_o"].reshape(depth, 4, 4, 64, 64) for c in range(n_cores)], axis=1).astype(np.float32)
    return (y_prompt, y_sample, pk, pv, ppool, phg, sk, sv, spool, shg)
```
